# Optimizing a Trainium2 kernel written in Bass

```python
import math
import jax, jax.numpy as jnp
from jax import lax
import numpy as np

D_MODEL = 2048
BATCH = 4
SEQ = 4096
DEPTH = 4

GRID_W = 64
CTX_LEN = 256
EPS = 1e-6
ROPE_THETA = 10000.0
N_MOD = 9
D_FF = 5632

SSD_HEADS = 16
SSD_HEADDIM = 64
SSD_INNER = SSD_HEADS * SSD_HEADDIM
SSD_GROUPS = 2
SSD_STATE = 128
SSD_CONV = 5
SSD_CHUNK = 128
SSD_CONV_CH = SSD_INNER + 2 * SSD_GROUPS * SSD_STATE
DT_MIN = 0.001
DT_MAX = 0.1

MLA_HEADS = 8
MLA_Q_RANK = 384
MLA_KV_RANK = 256
MLA_NOPE = 64
MLA_ROPE = 32
MLA_V = 64
MLA_BLOCK = 128

SWA_HEADS = 8
SWA_KV_HEADS = 2
SWA_HEADDIM = 64
SWA_WINDOW = 128
SWA_BLOCK = 128

D_MIX = SSD_INNER + MLA_HEADS * MLA_V + SWA_HEADS * SWA_HEADDIM
IN_SIZES = (SSD_INNER, SSD_CONV_CH, 2 * SSD_HEADS, MLA_Q_RANK, MLA_KV_RANK, MLA_ROPE,
            SWA_HEADS * SWA_HEADDIM, SWA_KV_HEADS * SWA_HEADDIM, SWA_KV_HEADS * SWA_HEADDIM)
D_IN = (SSD_INNER + SSD_CONV_CH + 2 * SSD_HEADS + MLA_Q_RANK + MLA_KV_RANK + MLA_ROPE
        + SWA_HEADS * SWA_HEADDIM + 2 * SWA_KV_HEADS * SWA_HEADDIM)

kernel_name = "hybrid_parallel_group_diffusion_trunk"


def rms_norm(x, g):
    xf = x.astype(jnp.float32)
    y = xf * lax.rsqrt(jnp.mean(xf * xf, axis=-1, keepdims=True) + EPS)
    return (y * g.astype(jnp.float32)).astype(x.dtype)


def modulate(u, shift, scale):
    return u * (1 + scale) + shift


def swiglu(u, wg, wu, wd):
    return (jax.nn.silu(u @ wg) * (u @ wu)) @ wd


def split_cols(p):
    offs = np.cumsum(IN_SIZES)[:-1].tolist()
    return jnp.split(p, offs, axis=-1)


def axial_rope_tables(pos_row, pos_col, dim):
    quarter = dim // 4
    inv_freq = ROPE_THETA ** (-jnp.arange(quarter, dtype=jnp.float32) / quarter)
    ang_r = pos_row.astype(jnp.float32)[:, None] * inv_freq[None, :]
    ang_c = pos_col.astype(jnp.float32)[:, None] * inv_freq[None, :]
    ang = jnp.concatenate([ang_r, ang_r, ang_c, ang_c], axis=-1)
    return jnp.cos(ang), jnp.sin(ang)


def _rotate_half(v):
    a, b = jnp.split(v, 2, axis=-1)
    return jnp.concatenate([-b, a], axis=-1)


def apply_axial_rope(x, cos, sin):
    xr, xc = jnp.split(x, 2, axis=-1)
    rot = jnp.concatenate([_rotate_half(xr), _rotate_half(xc)], axis=-1)
    return x * cos[None, :, None, :].astype(x.dtype) + rot * sin[None, :, None, :].astype(x.dtype)


def dwconv_centred(x, w, b):
    y = lax.conv_general_dilated(
        x, w[:, None, :].astype(x.dtype), window_strides=(1,),
        padding=[(SSD_CONV // 2, SSD_CONV // 2)],
        dimension_numbers=("NWC", "WIO", "NWC"), feature_group_count=x.shape[-1])
    return y + b.astype(x.dtype)


def ssd_chunk_terms(xs, dt, A, Bm):
    b, l, h, pdim = xs.shape
    g, n = Bm.shape[2], Bm.shape[3]
    r, nc = h // g, l // SSD_CHUNK
    xdt = (xs * dt[..., None]).reshape(b, nc, SSD_CHUNK, g, r, pdim)
    a_cum = jnp.cumsum((dt * A).reshape(b, nc, SSD_CHUNK, g, r), axis=2)
    Bc = Bm.reshape(b, nc, SSD_CHUNK, g, n)
    return xdt, a_cum, Bc


def ssd_chunk_states(xdt, a_cum, Bc, h0):
    decay_to_end = jnp.exp(a_cum[:, :, -1:] - a_cum).astype(xdt.dtype)
    s_local = jnp.einsum('bcsgn,bcsgr,bcsgrp->bcgrpn', Bc, decay_to_end, xdt).astype(jnp.float32)
    chunk_decay = jnp.exp(a_cum[:, :, -1])

    def step(h, inp):
        s, dec = inp
        return dec[..., None, None] * h + s, h

    h_final, h_in = lax.scan(step, h0, (jnp.moveaxis(s_local, 1, 0), jnp.moveaxis(chunk_decay, 1, 0)))
    return jnp.moveaxis(h_in, 0, 1), h_final


def ssd_chunk_outputs(xdt, a_cum, Bc, Cm, h_in):
    b, nc, q, g, r, pdim = xdt.shape
    Cc = Cm.reshape(b, nc, q, g, -1)
    seg = a_cum[:, :, :, None] - a_cum[:, :, None, :]
    lower_tri = jnp.tril(jnp.ones((q, q), bool))[None, None, :, :, None, None]
    decay = jnp.exp(jnp.where(lower_tri, seg, -jnp.inf)).astype(xdt.dtype)
    cb = jnp.einsum('bclgn,bcsgn->bclsg', Cc, Bc)
    y_diag = jnp.einsum('bclsg,bclsgr,bcsgrp->bclgrp', cb, decay, xdt)
    y_off = jnp.einsum('bclgn,bclgr,bcgrpn->bclgrp', Cc, jnp.exp(a_cum).astype(xdt.dtype),
                       h_in.astype(xdt.dtype))
    return (y_diag + y_off).reshape(b, nc * q, g * r, pdim)


def ssd_scan(xs, dt, A, Bm, Cm, h0):
    xdt, a_cum, Bc = ssd_chunk_terms(xs, dt, A, Bm)
    h_in, _ = ssd_chunk_states(xdt, a_cum, Bc, h0)
    return ssd_chunk_outputs(xdt, a_cum, Bc, Cm, h_in)


def _orient(t, d):
    return t if d == 0 else jnp.flip(t, axis=1)


def _ssd_inputs(xbc, dt_raw, p):
    b, l = xbc.shape[:2]
    xbc = jax.nn.silu(dwconv_centred(xbc, p['conv_w'], p['conv_b']))
    xs, Bm, Cm = jnp.split(xbc, [SSD_INNER, SSD_INNER + SSD_GROUPS * SSD_STATE], axis=-1)
    xs = xs.reshape(b, l, SSD_HEADS, SSD_HEADDIM)
    Bm = Bm.reshape(b, l, SSD_GROUPS, SSD_STATE)
    Cm = Cm.reshape(b, l, SSD_GROUPS, SSD_STATE)
    dt = jax.nn.softplus(dt_raw.reshape(b, l, 2, SSD_HEADS).astype(jnp.float32)
                         + p['dt_bias'].astype(jnp.float32))
    return xs, dt, Bm, Cm


def _ssd_finish(y_dirs, xs, z, p):
    b, l = xs.shape[:2]
    y = y_dirs[0] + y_dirs[1] + p['d_skip'][:, None].astype(xs.dtype) * xs
    y = y.reshape(b, l, SSD_INNER) * jax.nn.silu(z)
    y = rms_norm(y.reshape(b, l, SSD_GROUPS, SSD_INNER // SSD_GROUPS),
                 p['ssd_norm'].reshape(SSD_GROUPS, SSD_INNER // SSD_GROUPS))
    return y.reshape(b, l, SSD_INNER)


def ssd_group(zx, xbcx, dtx, zc, xbcc, dtc, p, ctx_out):
    A = -jnp.exp(p['a_log'].astype(jnp.float32))
    xs_x, dt_x, B_x, C_x = _ssd_inputs(xbcx, dtx, p)
    xs_c, dt_c, B_c, C_c = _ssd_inputs(xbcc, dtc, p)
    b = xs_c.shape[0]
    h0 = jnp.zeros((b, SSD_GROUPS, SSD_HEADS // SSD_GROUPS, SSD_HEADDIM, SSD_STATE), jnp.float32)
    y_x, y_c = [], []
    for d in range(2):
        terms = ssd_chunk_terms(_orient(xs_c, d), _orient(dt_c[:, :, d], d), A[d], _orient(B_c, d))
        h_in_c, h_ctx = ssd_chunk_states(*terms, h0)
        if ctx_out:
            y_c.append(_orient(ssd_chunk_outputs(*terms, _orient(C_c, d), h_in_c), d))
        y = ssd_scan(_orient(xs_x, d), _orient(dt_x[:, :, d], d), A[d], _orient(B_x, d),
                     _orient(C_x, d), h_ctx)
        y_x.append(_orient(y, d))
    out_x = _ssd_finish(y_x, xs_x, zx, p)
    out_c = _ssd_finish(y_c, xs_c, zc, p) if ctx_out else None
    return out_x, out_c


def mla_qkv(cq, ckv, kr, p, rope):
    b, l = cq.shape[:2]
    q = (rms_norm(cq, p['q_norm']) @ p['w_uq']).reshape(b, l, MLA_HEADS, MLA_NOPE + MLA_ROPE)
    kv = (rms_norm(ckv, p['kv_norm']) @ p['w_ukv']).reshape(b, l, MLA_HEADS, MLA_NOPE + MLA_V)
    q_nope, q_rope = jnp.split(q, [MLA_NOPE], axis=-1)
    k_nope, v = jnp.split(kv, [MLA_NOPE], axis=-1)
    k_rope = kr[:, :, None, :]
    if rope is not None:
        q_rope = apply_axial_rope(q_rope, *rope)
        k_rope = apply_axial_rope(k_rope, *rope)
    q = jnp.concatenate([q_nope, q_rope], axis=-1)
    k = jnp.concatenate([k_nope, jnp.broadcast_to(k_rope, (b, l, MLA_HEADS, MLA_ROPE))], axis=-1)
    return q, k, v


def dense_attention_blocks(q, k, v):
    b, L, h, d = q.shape
    n = L // MLA_BLOCK
    scale = d ** -0.5
    qb = jnp.moveaxis(q.reshape(b, n, MLA_BLOCK, h, d), 1, 0)

    def one_block(qblk):
        s = jnp.einsum('bqhd,bkhd->bhqk', qblk, k).astype(jnp.float32) * scale
        pr = jax.nn.softmax(s, axis=-1).astype(v.dtype)
        return jnp.einsum('bhqk,bkhd->bqhd', pr, v)

    o = lax.map(one_block, qb)
    return jnp.moveaxis(o, 0, 1).reshape(b, L, h * v.shape[-1])


def swa_latent(q, k, v, kc, vc, sink):
    b, L, H, d = q.shape
    G = SWA_KV_HEADS
    r, W, n, Cn = H // G, SWA_BLOCK, L // SWA_BLOCK, kc.shape[1]
    scale = d ** -0.5
    qb = q.reshape(b, n, W, G, r, d)
    pad = ((0, 0), (W, W), (0, 0), (0, 0))
    kp = jnp.pad(k, pad).reshape(b, n + 2, W, G, d)
    vp = jnp.pad(v, pad).reshape(b, n + 2, W, G, d)
    kwin = jnp.concatenate([kp[:, :-2], kp[:, 1:-1], kp[:, 2:]], axis=2)
    vwin = jnp.concatenate([vp[:, :-2], vp[:, 1:-1], vp[:, 2:]], axis=2)
    qi = jnp.arange(W)
    kj = jnp.arange(3 * W)
    rel = kj[None, :] - W - qi[:, None]
    key_pos = jnp.arange(n)[:, None] * W - W + kj[None, :]
    mask = (jnp.abs(rel) <= SWA_WINDOW)[None] & ((key_pos >= 0) & (key_pos < L))[:, None, :]
    s_loc = jnp.einsum('bnqgrd,bnkgd->bngrqk', qb, kwin).astype(jnp.float32) * scale
    s_loc = jnp.where(mask[None, :, None, None], s_loc, -jnp.inf)
    s_ctx = jnp.einsum('bnqgrd,bcgd->bngrqc', qb, kc).astype(jnp.float32) * scale
    s_sink = jnp.broadcast_to(sink.reshape(G, r)[None, None, :, :, None, None].astype(jnp.float32),
                              s_ctx.shape[:-1] + (1,))
    pr = jax.nn.softmax(jnp.concatenate([s_ctx, s_loc, s_sink], axis=-1), axis=-1).astype(v.dtype)
    p_ctx, p_loc = pr[..., :Cn], pr[..., Cn:Cn + 3 * W]
    o = (jnp.einsum('bngrqc,bcgd->bnqgrd', p_ctx, vc)
         + jnp.einsum('bngrqk,bnkgd->bnqgrd', p_loc, vwin))
    return o.reshape(b, L, H * d)


def sink_attention_ctx(q, k, v, sink):
    b, Cn, H, d = q.shape
    G = k.shape[2]
    r = H // G
    qg = q.reshape(b, Cn, G, r, d)
    s = jnp.einsum('bqgrd,bkgd->bgrqk', qg, k).astype(jnp.float32) * (d ** -0.5)
    s_sink = jnp.broadcast_to(sink.reshape(G, r)[None, :, :, None, None].astype(jnp.float32),
                              s.shape[:-1] + (1,))
    pr = jax.nn.softmax(jnp.concatenate([s, s_sink], axis=-1), axis=-1)[..., :-1].astype(v.dtype)
    return jnp.einsum('bgrqk,bkgd->bqgrd', pr, v).reshape(b, Cn, H * d)


def head_group_mixing(ux, uc, p, rope_mla, rope_swa, ctx_out):
    b, L = ux.shape[:2]
    Cn = uc.shape[1]
    zx, xbcx, dtx, cqx, ckvx, krx, qsx, ksx, vsx = split_cols(ux @ p['w_in'])
    zc, xbcc, dtc, cqc, ckvc, krc, qsc, ksc, vsc = split_cols(uc @ p['w_in'])
    ssd_x, ssd_c = ssd_group(zx, xbcx, dtx, zc, xbcc, dtc, p, ctx_out)
    q_mx, k_mx, v_mx = mla_qkv(cqx, ckvx, krx, p, rope_mla)
    q_mc, k_mc, v_mc = mla_qkv(cqc, ckvc, krc, p, None)
    k_all = jnp.concatenate([k_mx, k_mc], axis=1)
    v_all = jnp.concatenate([v_mx, v_mc], axis=1)
    mla_x = dense_attention_blocks(q_mx, k_all, v_all)
    q_sx = apply_axial_rope(qsx.reshape(b, L, SWA_HEADS, SWA_HEADDIM), *rope_swa)
    k_sx = apply_axial_rope(ksx.reshape(b, L, SWA_KV_HEADS, SWA_HEADDIM), *rope_swa)
    v_sx = vsx.reshape(b, L, SWA_KV_HEADS, SWA_HEADDIM)
    k_sc = ksc.reshape(b, Cn, SWA_KV_HEADS, SWA_HEADDIM)
    v_sc = vsc.reshape(b, Cn, SWA_KV_HEADS, SWA_HEADDIM)
    swa_x = swa_latent(q_sx, k_sx, v_sx, k_sc, v_sc, p['sink'])
    ox = jnp.concatenate([ssd_x, rms_norm(mla_x, p['mla_out_norm']),
                          rms_norm(swa_x, p['swa_out_norm'])], axis=-1)
    if not ctx_out:
        return ox, None
    mla_c = dense_attention_blocks(q_mc, k_mc, v_mc)
    swa_c = sink_attention_ctx(qsc.reshape(b, Cn, SWA_HEADS, SWA_HEADDIM), k_sc, v_sc, p['sink'])
    oc = jnp.concatenate([ssd_c, rms_norm(mla_c, p['mla_out_norm']),
                          rms_norm(swa_c, p['swa_out_norm'])], axis=-1)
    return ox, oc


def setup_inputs(seed: int = 0) -> dict:
    key = jax.random.key(seed)
    ks = jax.random.split(key, 32)
    f32 = jnp.float32

    def nrm(k, shape, scale):
        return jax.random.normal(k, shape, f32) * scale

    def gain(k, shape):
        return 1.0 + 0.05 * jax.random.normal(k, shape, f32)

    u = jax.random.uniform(ks[14], (DEPTH, 2, SSD_HEADS), f32)
    dt0 = jnp.exp(u * (math.log(DT_MAX) - math.log(DT_MIN)) + math.log(DT_MIN))
    return {
        "x": nrm(ks[0], (BATCH, SEQ, D_MODEL), 1.0),
        "c": nrm(ks[1], (BATCH, D_MODEL), 1.0),
        "ctx": nrm(ks[2], (BATCH, CTX_LEN, D_MODEL), 1.0),
        "c_ctx": nrm(ks[3], (D_MODEL,), 1.0),
        "w_mod": nrm(ks[4], (DEPTH, D_MODEL, N_MOD * D_MODEL), 0.5 * D_MODEL ** -0.5),
        "b_mod": nrm(ks[5], (DEPTH, N_MOD * D_MODEL), 0.02),
        "norm_g": gain(ks[6], (DEPTH, 3, D_MODEL)),
        "ffn_w_gate": nrm(ks[7], (DEPTH, 2, D_MODEL, D_FF), D_MODEL ** -0.5),
        "ffn_w_up": nrm(ks[8], (DEPTH, 2, D_MODEL, D_FF), D_MODEL ** -0.5),
        "ffn_w_down": nrm(ks[9], (DEPTH, 2, D_FF, D_MODEL), D_FF ** -0.5),
        "w_in": nrm(ks[10], (DEPTH, D_MODEL, D_IN), D_MODEL ** -0.5),
        "w_out": nrm(ks[11], (DEPTH, D_MIX, D_MODEL), D_MIX ** -0.5),
        "ssd_conv_w": nrm(ks[12], (DEPTH, SSD_CONV, SSD_CONV_CH), SSD_CONV ** -0.5),
        "ssd_conv_b": nrm(ks[13], (DEPTH, SSD_CONV_CH), 0.02),
        "ssd_dt_bias": dt0 + jnp.log(-jnp.expm1(-dt0)),
        "ssd_a_log": jnp.log(jax.random.uniform(ks[15], (DEPTH, 2, SSD_HEADS), f32, 1.0, 16.0)),
        "ssd_d": 1.0 + 0.1 * jax.random.normal(ks[16], (DEPTH, SSD_HEADS), f32),
        "ssd_norm": gain(ks[17], (DEPTH, SSD_INNER)),
        "mla_q_norm": gain(ks[18], (DEPTH, MLA_Q_RANK)),
        "mla_w_uq": nrm(ks[19], (DEPTH, MLA_Q_RANK, MLA_HEADS * (MLA_NOPE + MLA_ROPE)), MLA_Q_RANK ** -0.5),
        "mla_kv_norm": gain(ks[20], (DEPTH, MLA_KV_RANK)),
        "mla_w_ukv": nrm(ks[21], (DEPTH, MLA_KV_RANK, MLA_HEADS * (MLA_NOPE + MLA_V)), MLA_KV_RANK ** -0.5),
        "mla_out_norm": gain(ks[22], (DEPTH, MLA_HEADS * MLA_V)),
        "swa_sink": nrm(ks[23], (DEPTH, SWA_HEADS), 0.5),
        "swa_out_norm": gain(ks[24], (DEPTH, SWA_HEADS * SWA_HEADDIM)),
        "final_norm": gain(ks[25], (D_MODEL,)),
    }


def reference(x, c, ctx, c_ctx, w_mod, b_mod, norm_g, ffn_w_gate, ffn_w_up, ffn_w_down,
              w_in, w_out, ssd_conv_w, ssd_conv_b, ssd_dt_bias, ssd_a_log, ssd_d, ssd_norm,
              mla_q_norm, mla_w_uq, mla_kv_norm, mla_w_ukv, mla_out_norm, swa_sink,
              swa_out_norm, final_norm):
    L = x.shape[1]
    ROWS = L // GRID_W
    pos_row = jnp.repeat(jnp.arange(ROWS, dtype=jnp.int32), GRID_W)
    pos_col = jnp.tile(jnp.arange(GRID_W, dtype=jnp.int32), ROWS)
    rope_mla = axial_rope_tables(pos_row, pos_col, MLA_ROPE)
    rope_swa = axial_rope_tables(pos_row, pos_col, SWA_HEADDIM)

    hx, hc = x, ctx
    for l in range(DEPTH):
        last = l == DEPTH - 1
        mod_x = jnp.split((jax.nn.silu(c) @ w_mod[l] + b_mod[l])[:, None, :], N_MOD, axis=-1)
        mod_c = jnp.split((jax.nn.silu(c_ctx) @ w_mod[l] + b_mod[l])[None, None, :], N_MOD, axis=-1)
        hx = hx + 0.5 * mod_x[2] * swiglu(modulate(rms_norm(hx, norm_g[l, 0]), mod_x[0], mod_x[1]),
                                          ffn_w_gate[l, 0], ffn_w_up[l, 0], ffn_w_down[l, 0])
        hc = hc + 0.5 * mod_c[2] * swiglu(modulate(rms_norm(hc, norm_g[l, 0]), mod_c[0], mod_c[1]),
                                          ffn_w_gate[l, 0], ffn_w_up[l, 0], ffn_w_down[l, 0])
        p = {"w_in": w_in[l], "conv_w": ssd_conv_w[l], "conv_b": ssd_conv_b[l],
             "dt_bias": ssd_dt_bias[l], "a_log": ssd_a_log[l], "d_skip": ssd_d[l],
             "ssd_norm": ssd_norm[l], "q_norm": mla_q_norm[l], "w_uq": mla_w_uq[l],
             "kv_norm": mla_kv_norm[l], "w_ukv": mla_w_ukv[l], "mla_out_norm": mla_out_norm[l],
             "sink": swa_sink[l], "swa_out_norm": swa_out_norm[l]}
        ux = modulate(rms_norm(hx, norm_g[l, 1]), mod_x[3], mod_x[4])
        uc = modulate(rms_norm(hc, norm_g[l, 1]), mod_c[3], mod_c[4])
        ox, oc = head_group_mixing(ux, uc, p, rope_mla, rope_swa, not last)
        hx = hx + mod_x[5] * (ox @ w_out[l])
        hx = hx + 0.5 * mod_x[8] * swiglu(modulate(rms_norm(hx, norm_g[l, 2]), mod_x[6], mod_x[7]),
                                          ffn_w_gate[l, 1], ffn_w_up[l, 1], ffn_w_down[l, 1])
        if not last:
            hc = hc + mod_c[5] * (oc @ w_out[l])
            hc = hc + 0.5 * mod_c[8] * swiglu(modulate(rms_norm(hc, norm_g[l, 2]), mod_c[6], mod_c[7]),
                                              ffn_w_gate[l, 1], ffn_w_up[l, 1], ffn_w_down[l, 1])
    return rms_norm(hx, final_norm)
```

```python
import numpy as np
import ml_dtypes
from contextlib import ExitStack
import concourse.bass as bass
import concourse.mybir as mybir
from concourse.bass_utils import run_bass_kernel_spmd

F32 = mybir.dt.float32
BF16 = mybir.dt.bfloat16
ALU = mybir.AluOpType
AF = mybir.ActivationFunctionType
AX = mybir.AxisListType

EPS = 1e-6
ROPE_THETA = 10000.0
GRID_W = 64


class Tok:
    __slots__ = ("w", "r", "name")

    def __init__(self, name=""):
        self.w = None
        self.r = []
        self.name = name


class _Op:
    __slots__ = ("eng", "fn", "deps", "dma", "signal", "sem", "val", "pos", "idx", "prev")


ENGS = ("pe", "act", "dve", "pool", "sp")
SEM_LIMIT = 30000
N_DMA_SEMS = 24


class Prog:
    def __init__(self, nc):
        self.nc = nc
        self.ops = []
        self.streams = {e: [] for e in ENGS}
        self.dma_since_barrier = []

    def op(self, eng, fn, reads=(), writes=(), dma=False, nobar=False):
        idx = len(self.ops)
        deps = set()
        for t in reads:
            if t.w is not None:
                deps.add(t.w)
        for t in writes:
            if t.w is not None:
                deps.add(t.w)
            deps.update(t.r)
        for t in reads:
            t.r.append(idx)
        for t in writes:
            t.w = idx
            t.r = []
        deps.discard(idx)
        o = _Op()
        o.eng, o.fn, o.deps, o.dma, o.signal = eng, fn, deps, dma, False
        o.sem = o.val = None
        o.idx = idx
        o.pos = len(self.streams[eng])
        self.streams[eng].append(o)
        self.ops.append(o)
        if dma and not nobar:
            self.dma_since_barrier.append(idx)
        return idx

    def dma(self, eng, out, in_, reads=(), writes=(), nobar=False):
        return self.op(eng, lambda e: e.dma_start(out=out, in_=in_), reads, writes, dma=True, nobar=nobar)

    def barrier(self):
        deps = set(self.dma_since_barrier)
        for e in ENGS:
            for o in reversed(self.streams[e]):
                if o.fn is not None:
                    deps.add(o.idx)
                    break
        self.dma_since_barrier = []
        for e in ENGS:
            idx = self.op(e, None)
            self.ops[idx].deps = set(d for d in deps)

    def _needs_wait(self, o, d):
        if d.dma:
            return True
        if d.eng == o.eng:
            if o.dma:
                return True
            if o.eng == "pe":
                return False
            return d.pos >= o.pos - 3
        return True

    def emit(self, es):
        nc = self.nc
        ops = self.ops
        for o in ops:
            for di in o.deps:
                d = ops[di]
                if self._needs_wait(o, d):
                    d.signal = True
        for o in ops:
            if o.dma:
                o.signal = True
        sems = {}
        n_sem = [0]

        def new_sem(tag):
            n_sem[0] += 1
            return es.enter_context(nc.semaphore(f"s_{tag}_{n_sem[0]}"))

        for e in ENGS:
            cur = None
            cnt = 0
            dsem = []
            dcnt = []
            k = 0
            for o in self.streams[e]:
                if not o.signal:
                    continue
                if o.dma:
                    if len(dsem) < N_DMA_SEMS:
                        dsem.append(new_sem(e + "d"))
                        dcnt.append(0)
                    j = k % N_DMA_SEMS
                    k += 1
                    if dcnt[j] + 16 > SEM_LIMIT:
                        dsem[j] = new_sem(e + "d")
                        dcnt[j] = 0
                    o.prev = dcnt[j]
                    dcnt[j] += 16
                    o.sem, o.val = dsem[j], dcnt[j]
                else:
                    if cur is None or cnt + 1 > SEM_LIMIT:
                        cur = new_sem(e)
                        cnt = 0
                    cnt += 1
                    o.sem, o.val = cur, cnt
        self.n_sems = n_sem[0]

        engobj = {"pe": "tensor", "act": "scalar", "dve": "vector", "pool": "gpsimd", "sp": "sync"}

        def run_stream(ename):
            def body(eng):
                seen = {}
                for o in self.streams[ename]:
                    waits = {}
                    for di in o.deps:
                        d = ops[di]
                        if not self._needs_wait(o, d):
                            continue
                        key = id(d.sem)
                        if seen.get(key, 0) >= d.val:
                            continue
                        if key not in waits or waits[key][1] < d.val:
                            waits[key] = (d.sem, d.val)
                    if o.dma and o.prev > 0 and seen.get(id(o.sem), 0) < o.prev:
                        waits[id(o.sem)] = (o.sem, o.prev)
                    for key, (s, v) in waits.items():
                        eng.wait_ge(s, v)
                        seen[key] = v
                    if o.fn is None:
                        continue
                    ins = o.fn(eng)
                    if o.signal:
                        ins.then_inc(o.sem, 16 if o.dma else 1)
            return body

        with nc.Block() as block:
            for ename in ENGS:
                if not self.streams[ename]:
                    continue
                getattr(block, engobj[ename])(run_stream(ename))


class Cfg:
    pass


def make_cfg(D, FF, SEQ, CTX, DEPTH):
    c = Cfg()
    c.D, c.FF, c.SEQ, c.CTX, c.L = D, FF, SEQ, CTX, DEPTH
    c.DC = D // 128
    c.FC = FF // 128
    c.NH = 2 if c.FC % 2 == 0 else 1
    c.FCH = c.FC // c.NH
    c.NTOK = CTX + SEQ
    c.TT = 512
    tiles = []
    t0 = 0
    while t0 < CTX:
        T = min(c.TT, CTX - t0)
        tiles.append((t0, T, True))
        t0 += T
    while t0 < c.NTOK:
        T = min(c.TT, c.NTOK - t0)
        tiles.append((t0, T, False))
        t0 += T
    c.tiles = tiles
    return c


NCH_WIN = 41
D_MIX = 2048
KM = 16
NHM = 8
NHS = 8
SC_MLA = 96 ** -0.5
SC_SWA = 64 ** -0.5


def _prod(xs):
    r = 1
    for x in xs:
        r *= x
    return r


class Arena:
    def __init__(self, base, nbytes):
        self.base, self.n, self.off = base, nbytes, 0

    def reset(self):
        self.off = 0

    def alloc(self, shape, dt):
        cols = _prod(shape[1:])
        esz = 4 if dt == F32 else 2
        nb = cols * esz
        o = self.off
        self.off += (nb + 63) // 64 * 64
        assert self.off <= self.n, f"arena overflow {self.off} > {self.n}"
        v = self.base[:, o // 2:o // 2 + nb // 2]
        if dt == F32:
            v = v.bitcast(F32)
        if len(shape) == 3:
            v = v.rearrange("p (a b) -> p a b", b=shape[2])
        elif len(shape) == 4:
            v = v.rearrange("p (a b c) -> p a b c", b=shape[2], c=shape[3])
        if shape[0] < 128:
            v = v[0:shape[0]]
        return v


def build_program(c):
    nc = bass.Bass("TRN2", target_bir_lowering=False)
    P = Prog(nc)
    D, FF, DC, FC, NH, FCH, L, NTOK, SEQ, CTX = c.D, c.FF, c.DC, c.FC, c.NH, c.FCH, c.L, c.NTOK, c.SEQ, c.CTX
    NB = NTOK // 128
    NBC = CTX // 128
    TT = c.TT
    WK = max(DC, KM)

    def dram_in(name, shape, dt=F32):
        return nc.dram_tensor(name, list(shape), dt, kind="ExternalInput").ap()

    def dram_scr(name, shape, dt):
        return nc.dram_tensor(name, list(shape), dt, kind="Internal").ap()

    xT_in = dram_in("xT", [D, NTOK])
    cT_in = dram_in("cT", [128, DC, 2])
    wmod_in = dram_in("wmod_t", [L, 9 * DC, 128, DC, 128])
    bmod_in = dram_in("bmod_t", [L, 128, 9 * DC])
    ng_in = dram_in("ng_t", [L, 128, 3, DC])
    wgu_in = dram_in("wgu_t", [L, 2, FC, 2, 128, DC, 128])
    wd_in = dram_in("wd_t", [L, 2, NH, DC, 128, FCH, 128])
    fin_in = dram_in("fin_t", [128, DC])
    win_in = dram_in("win_t", [L, NCH_WIN, 128, DC, 128])
    wout_in = dram_in("wout_t", [L, DC, 128, KM, 128])
    wuq_in = dram_in("wuq_t", [L, 2, NHM, 128, 3, 96])
    wukvk_in = dram_in("wukvk_t", [L, 4, 128, 2, 128])
    wukvv_in = dram_in("wukvv_t", [L, 128, 2, 512])
    qn_in = dram_in("qn_t", [L, 128, 3])
    kvn_in = dram_in("kvn_t", [L, 128, 2])
    convw_in = dram_in("convw_t", [L, 128, 12, 5])
    convb_in = dram_in("convb_t", [L, 128, 12])
    dtb_in = dram_in("dtb_t", [L, 64, 1])
    alog_in = dram_in("alog_t", [L, 64, 1])
    dsk_in = dram_in("dsk_t", [L, 128, 16])
    sn_in = dram_in("sn_t", [L, 128, 1024])
    mon_in = dram_in("mon_t", [L, 64, 8])
    son_in = dram_in("son_t", [L, 64, 8])
    sink_in = dram_in("sink_t", [L, 64, 8])
    ropeM_in = dram_in("ropeM_t", [2, 32, SEQ])
    ropeS_in = dram_in("ropeS_t", [2, 128, SEQ])
    cst_in = dram_in("cst_t", [128, 6, 128])
    out_ap = nc.dram_tensor("outT", [D, SEQ], F32, kind="ExternalOutput").ap()

    hxT = dram_scr("hxT", [D, NTOK], F32)
    wgu_bf = [dram_scr(f"wgu_bf{l}", [2, FC, 2, 128, DC, 128], BF16) for l in range(L)]
    wd_bf = [dram_scr(f"wd_bf{l}", [2, NH, DC, 128, FCH, 128], BF16) for l in range(L)]
    win_bf = [dram_scr(f"win_bf{l}", [NCH_WIN, 128, DC, 128], BF16) for l in range(L)]
    wout_bf = [dram_scr(f"wout_bf{l}", [DC, 128, KM, 128], BF16) for l in range(L)]
    XBC = dram_scr("XBC", [1536, NTOK], BF16)
    DTR = dram_scr("DTR", [32, NTOK], F32)
    QM = dram_scr("QM", [NHM, 96, NTOK], BF16)
    KN = dram_scr("KN", [512, NTOK], BF16)
    KR = dram_scr("KR", [32, NTOK], BF16)
    VM = dram_scr("VM", [NB, 128, NHM, 65], BF16)
    QS = dram_scr("QS", [4, 128, NTOK], BF16)
    KS = dram_scr("KS", [2, 128, NTOK], BF16)
    VS = dram_scr("VS", [NB, 128, 2, 65], BF16)
    SZ = dram_scr("SZ", [NB, 128, 1024], BF16)
    OX = dram_scr("OX", [D_MIX, NTOK], BF16)
    OM = dram_scr("OM", [NHM, 65, NTOK], F32)
    OS = dram_scr("OS", [NHS, 65, NTOK], F32)
    XS = dram_scr("XS", [NB, 128, 1024], BF16)
    BTM = dram_scr("BTM", [NB, 128, 2, 128], BF16)
    BCT = dram_scr("BCT", [NB, 128, 4, 128], BF16)
    DTA = dram_scr("DTA", [NB, 128, 64], F32)
    YF = dram_scr("YF", [NB, 128, 1024], F32)

    es = ExitStack()

    def sb(name, shape, dt):
        return nc.alloc_sbuf_tensor("sb_" + name, list(shape), dt).ap()

    cst = sb("cst", [128, 6, 128], F32)
    IDENT_F, T_F, T_B, U_F, U_B, ONES_F = [cst[:, i, :] for i in range(6)]
    cstb = sb("cstb", [128, 6, 128], BF16)
    IDENT_B, TB_F, TB_B, _ub0, _ub1, ones_bf = [cstb[:, i, :] for i in range(6)]
    t_cst = Tok()
    P.dma("sp", cst[:], cst_in[:], writes=[t_cst])
    P.op("dve", lambda e: e.tensor_copy(cstb[:], cst[:]), reads=[t_cst], writes=[t_cst])
    t_ones = t_cst
    eps_ap = sb("eps_ap", [128, 1], F32)
    one_ap = sb("one_ap", [128, 1], F32)
    t_eps = Tok()
    P.op("pool", lambda e: e.memset(eps_ap[:], EPS), writes=[t_eps])
    P.op("pool", lambda e: e.memset(one_ap[:], 1.0), writes=[t_eps])

    banks = [nc.alloc_psum_tensor(f"bank{i}", [128, 512], F32).ap() for i in range(8)]
    banks_b = [b.bitcast(BF16) for b in banks]
    t_bank = [Tok(f"bank{i}") for i in range(8)]

    def mm_group(out, pairs, reads, writes):
        def f(e):
            ins = None
            n = len(pairs)
            for i, (a, b) in enumerate(pairs):
                ins = e.matmul(out, a, b, start=(i == 0), stop=(i == n - 1))
            return ins
        return P.op("pe", f, reads, writes)

    t_hx_dram = [Tok(f"hxd{i}") for i in range(len(c.tiles))]
    P.dma("sp", hxT[:, :], xT_in[:, :], writes=t_hx_dram)

    cT = sb("cTs", [128, DC, 2], F32)
    csil = sb("csil", [128, DC, 2], BF16)
    t_c = Tok()
    t_csil = Tok()
    P.dma("sp", cT[:], cT_in[:], writes=[t_c])
    P.op("act", lambda e: e.activation(csil[:], cT[:], AF.Silu), reads=[t_c], writes=[t_csil])
    NMC = 9 * DC
    MOD = [sb(f"mod{l}", [128, NMC, 2], F32) for l in range(L)]
    bmod = [sb(f"bmod{l}", [128, NMC], F32) for l in range(L)]
    ngt = [sb(f"ng{l}", [128, 3, DC], F32) for l in range(L)]
    AV = [sb(f"av{l}", [128, 3, DC, 2], F32) for l in range(L)]
    GV = [sb(f"gv{l}", [128, 3, DC, 2], F32) for l in range(L)]
    t_mod = [Tok() for _ in range(L)]
    fin = sb("fin", [128, DC], F32)
    t_fin = Tok()
    P.dma("sp", fin[:], fin_in[:], writes=[t_fin])

    NWM = 3
    wm_buf = [sb(f"wm{i}", [128, DC, 128], BF16) for i in range(NWM)]
    t_wm = [Tok() for _ in range(NWM)]
    ARENA_BYTES = nc.sbuf_bytes_remaining // 128 - 2048 if nc.sbuf_bytes_remaining > 4 * 1024 * 1024 else nc.sbuf_bytes_remaining - 2048
    ARENA_BYTES = (ARENA_BYTES // 64) * 64
    arena_t = sb("arena", [128, ARENA_BYTES // 2], BF16)
    AR = Arena(arena_t, ARENA_BYTES)
    print("arena bytes/partition:", ARENA_BYTES)

    for l in range(L):
        t_bm = Tok()
        P.dma("sp", bmod[l][:], bmod_in[l], writes=[t_bm])
        P.dma("sp", ngt[l][:], ng_in[l], writes=[t_bm])
        bank = banks[l % 2]
        tb = t_bank[l % 2]
        for cc in range(NMC):
            i = (l * NMC + cc) % NWM
            P.dma("pool", wm_buf[i][:], wmod_in[l, cc], writes=[t_wm[i]], nobar=True)
            mm_group(bank[:, cc * 2:cc * 2 + 2], [(wm_buf[i][:, kc, :], csil[:, kc, :]) for kc in range(DC)],
                     [t_wm[i], t_csil], [tb])
        P.op("dve", lambda e, l=l, bank=bank: e.tensor_tensor(
            MOD[l][:], bank[:, 0:2 * NMC].rearrange("p (c t) -> p c t", t=2),
            bmod[l][:].unsqueeze(2).to_broadcast([128, NMC, 2]), ALU.add),
            reads=[tb, t_bm], writes=[t_mod[l]])
        for i3 in range(3):
            sc = MOD[l][:, (3 * i3 + 1) * DC:(3 * i3 + 2) * DC, :]
            gt = MOD[l][:, (3 * i3 + 2) * DC:(3 * i3 + 3) * DC, :]
            P.op("dve", lambda e, l=l, i3=i3, sc=sc: e.scalar_tensor_tensor(
                AV[l][:, i3], sc, 1.0, ngt[l][:, i3].unsqueeze(2).to_broadcast([128, DC, 2]),
                ALU.add, ALU.mult), reads=[t_mod[l], t_bm], writes=[t_mod[l]])
            P.op("dve", lambda e, l=l, i3=i3, gt=gt: e.tensor_scalar(
                GV[l][:, i3], gt, 0.5 if i3 != 1 else 1.0, None, ALU.mult),
                reads=[t_mod[l]], writes=[t_mod[l]])

    t_wgu = [[Tok() for _ in range(2)] for _ in range(L)]
    t_wd = [[Tok() for _ in range(2)] for _ in range(L)]
    t_win = [Tok() for _ in range(L)]
    t_wout = [Tok() for _ in range(L)]
    for l in range(L):
        for ch0 in range(0, NCH_WIN, 8):
            ch1 = min(NCH_WIN, ch0 + 8)
            P.dma("pool", win_bf[l][ch0:ch1], win_in[l, ch0:ch1], writes=[t_win[l]], nobar=True)
        for i in range(2):
            if i == 1:
                for dc0 in range(0, DC, 4):
                    P.dma("pool", wout_bf[l][dc0:min(DC, dc0 + 4)], wout_in[l, dc0:min(DC, dc0 + 4)], writes=[t_wout[l]], nobar=True)
            for fc in range(FC):
                P.dma("pool", wgu_bf[l][i, fc], wgu_in[l, i, fc], writes=[t_wgu[l][i]], nobar=True)
            for hf in range(NH):
                for dc0 in range(0, DC, 4):
                    P.dma("pool", wd_bf[l][i, hf, dc0:min(DC, dc0 + 4)], wd_in[l, i, hf, dc0:min(DC, dc0 + 4)], writes=[t_wd[l][i]], nobar=True)

    hx_view = hxT.rearrange("(c p) t -> p c t", p=128)
    out_view = out_ap.rearrange("(c p) t -> p c t", p=128)
    out_stores = []
    cnt = {}

    def nxt(key, n):
        v = cnt.get(key, 0)
        cnt[key] = v + 1
        return v % n

    def dense_alloc():
        AR.reset()
        B = Cfg()
        B.hx = AR.alloc([128, DC, TT], F32)
        B.u = AR.alloc([128, WK, TT], BF16)
        B.hbuf = AR.alloc([128, max(FCH, DC, 3), TT], BF16)
        B.rstd = AR.alloc([128, TT], F32)
        B.tmp = [AR.alloc([128, TT], F32) for _ in range(2)]
        B.sg = [AR.alloc([128, TT], F32) for _ in range(2)]
        B.NW = 6
        B.w = [AR.alloc([128, WK, 128], BF16) for _ in range(B.NW)]
        B.wd = [AR.alloc([128, FCH, 128], BF16) for _ in range(2)]
        B.t_hx = [Tok() for _ in range(DC)]
        B.t_u = [Tok() for _ in range(WK)]
        B.t_h = [Tok() for _ in range(max(FCH, DC, 3))]
        B.t_rstd = Tok()
        B.t_tmp = [Tok(), Tok()]
        B.t_sg = [Tok(), Tok()]
        B.t_w = [Tok() for _ in range(B.NW)]
        B.t_wd = [Tok(), Tok()]
        return B

    def fm_rstd(B, T, srcs, nfeat, t_src, rstd_out, t_out, nrows=128):
        n = len(srcs)
        for i, s_ap in enumerate(srcs):
            P.op("act", lambda e, i=i, s_ap=s_ap: e.activation(B.hbuf[0:nrows, i, :T], s_ap, AF.Square),
                 reads=t_src, writes=[B.t_h[i]])
        bk, tb = banks[6], t_bank[6]
        mm_group(bk[:, :T], [(ones_bf[0:nrows, :], B.hbuf[0:nrows, i, :T]) for i in range(n)],
                 [B.t_h[i] for i in range(n)] + [t_ones], [tb])
        P.op("act", lambda e: e.activation(rstd_out[:, :T], bk[:, :T], AF.Sqrt, bias=eps_ap[:, 0:1],
                                           scale=1.0 / nfeat), reads=[tb, t_eps], writes=[t_out])
        P.op("dve", lambda e: e.reciprocal(rstd_out[:, :T], rstd_out[:, :T]), reads=[t_out], writes=[t_out])

    def modnorm(B, T, l, avec, bvec, col):
        fm_rstd(B, T, [B.hx[:, dc, :T] for dc in range(DC)], D, B.t_hx, B.rstd, B.t_rstd)
        for dc in range(DC):
            k = nxt("tmp", 2)
            P.op("dve", lambda e, dc=dc, k=k: e.scalar_tensor_tensor(
                B.tmp[k][:, :T], B.hx[:, dc, :T], avec[:, dc, col:col + 1], B.rstd[:, :T], ALU.mult, ALU.mult),
                reads=[B.t_hx[dc], B.t_rstd, t_mod[l]], writes=[B.t_tmp[k]])
            P.op("act", lambda e, dc=dc, k=k: e.activation(
                B.u[:, dc, :T], B.tmp[k][:, :T], AF.Identity, bias=bvec[:, dc, col:col + 1], scale=1.0),
                reads=[B.t_tmp[k], t_mod[l]], writes=[B.t_u[dc]])

    def load_w(B, src, t_src, K=None):
        s = nxt("w", B.NW)
        K = DC if K is None else K
        P.dma("sp", B.w[s][:, :K, :], src, reads=[t_src], writes=[B.t_w[s]])
        return s

    def ffn(B, T, l, i, gvec, col):
        tu_all = B.t_u[:DC]
        for hf in range(NH):
            for f in range(FCH):
                fc = hf * FCH + f
                sgi = load_w(B, wgu_bf[l][i, fc, 0], t_wgu[l][i])
                sui = load_w(B, wgu_bf[l][i, fc, 1], t_wgu[l][i])
                bg, bu_ = banks[(2 * f) % 4], banks[(2 * f + 1) % 4]
                tg, tu = t_bank[(2 * f) % 4], t_bank[(2 * f + 1) % 4]
                mm_group(bg[:, :T], [(B.w[sgi][:, kc, :], B.u[:, kc, :T]) for kc in range(DC)],
                         [B.t_w[sgi]] + tu_all, [tg])
                mm_group(bu_[:, :T], [(B.w[sui][:, kc, :], B.u[:, kc, :T]) for kc in range(DC)],
                         [B.t_w[sui]] + tu_all, [tu])
                k = nxt("sg", 2)
                P.op("act", lambda e, k=k, bg=bg: e.activation(B.sg[k][:, :T], bg[:, :T], AF.Silu),
                     reads=[tg], writes=[B.t_sg[k]])
                P.op("dve", lambda e, k=k, bu_=bu_, f=f: e.tensor_tensor(
                    B.hbuf[:, f, :T], B.sg[k][:, :T], bu_[:, :T], ALU.mult),
                    reads=[B.t_sg[k], tu], writes=[B.t_h[f]])
            for dc in range(DC):
                s = nxt("wd", 2)
                P.dma("sp", B.wd[s][:], wd_bf[l][i, hf, dc], reads=[t_wd[l][i]], writes=[B.t_wd[s]])
                bo, to = banks[4 + dc % 2], t_bank[4 + dc % 2]
                mm_group(bo[:, :T], [(B.wd[s][:, f, :], B.hbuf[:, f, :T]) for f in range(FCH)],
                         [B.t_wd[s]] + B.t_h[:FCH], [to])
                P.op("dve", lambda e, dc=dc, bo=bo: e.scalar_tensor_tensor(
                    B.hx[:, dc, :T], bo[:, :T], gvec[:, dc, col:col + 1], B.hx[:, dc, :T], ALU.mult, ALU.add),
                    reads=[to, t_mod[l]], writes=[B.t_hx[dc]])

    NT = len(c.tiles)
    t_scr = [Tok(f"scr{i}") for i in range(NT)]
    t_ox = [Tok(f"ox{i}") for i in range(NT)]

    def inproj(B, X, T, l, ti, t0, is_ctx):
        tu_all = B.t_u[:DC]
        nblk = T // 128
        ts = t_scr[ti]
        ring = [0, 1, 2, 3, 4, 5]

        def fm_chunk(ch, M=128):
            s = load_w(B, win_bf[l][ch], t_win[l])
            bi = ring[nxt("ipb", 6)]
            mm_group(banks[bi][0:M, :T], [(B.w[s][:, kc, 0:M], B.u[:, kc, :T]) for kc in range(DC)],
                     [B.t_w[s]] + tu_all, [t_bank[bi]])
            return bi

        def stage_bf():
            k = nxt("stb", 3)
            return X.stb[k], X.t_stb[k]

        for ch in range(12):
            bi = fm_chunk(ch)
            st, tst = stage_bf()
            eng = "act" if ch % 2 == 0 else "dve"
            if eng == "act":
                P.op("act", lambda e, bi=bi, st=st: e.activation(st[:, :T], banks[bi][:, :T], AF.Copy),
                     reads=[t_bank[bi]], writes=[tst])
            else:
                P.op("dve", lambda e, bi=bi, st=st: e.tensor_copy(st[:, :T], banks[bi][:, :T]),
                     reads=[t_bank[bi]], writes=[tst])
            P.dma("sp", XBC[ch * 128:(ch + 1) * 128, t0:t0 + T], st[:, :T], reads=[tst], writes=[ts])
        bi = fm_chunk(12, 32)
        P.op("act", lambda e, bi=bi: e.activation(X.stf[0:32, :T], banks[bi][0:32, :T], AF.Copy),
             reads=[t_bank[bi]], writes=[X.t_stf])
        P.dma("sp", DTR[:, t0:t0 + T], X.stf[0:32, :T], reads=[X.t_stf], writes=[ts])
        if not is_ctx:
            p0 = t0 - CTX
            P.dma("sp", X.ropeM[64:96, 0, :T], ropeM_in[0, :, p0:p0 + T], writes=[X.t_rope])
            P.dma("sp", X.ropeM[64:96, 1, :T], ropeM_in[1, :, p0:p0 + T], writes=[X.t_rope])
            P.dma("sp", X.ropeS[:, 0, :T], ropeS_in[0, :, p0:p0 + T], writes=[X.t_rope])
            P.dma("sp", X.ropeS[:, 1, :T], ropeS_in[1, :, p0:p0 + T], writes=[X.t_rope])

        def rope_evac(ba, bb, r0, r1, table, st):
            k = nxt("tmp", 2)
            k2 = nxt("sg", 2)
            P.op("dve", lambda e: e.tensor_tensor(B.tmp[k][r0:r1, :T], banks[ba][r0:r1, :T], table[r0:r1, 0, :T], ALU.mult),
                 reads=[t_bank[ba], X.t_rope], writes=[B.t_tmp[k]])
            P.op("dve", lambda e: e.tensor_tensor(B.sg[k2][r0:r1, :T], banks[bb][r0:r1, :T], table[r0:r1, 1, :T], ALU.mult),
                 reads=[t_bank[bb], X.t_rope], writes=[B.t_sg[k2]])
            P.op("pool", lambda e: e.tensor_tensor(st[r0:r1, :T], B.tmp[k][r0:r1, :T], B.sg[k2][r0:r1, :T], ALU.add),
                 reads=[B.t_tmp[k], B.t_sg[k2]], writes=[])

        def latent_norm(chs, nfeat, buf, t_buf, gn, outn, t_outn):
            for j, ch in enumerate(chs):
                bi = fm_chunk(ch)
                P.op("act", lambda e, bi=bi, j=j: e.activation(buf[:, j, :T], banks[bi][:, :T], AF.Copy),
                     reads=[t_bank[bi]], writes=[t_buf])
            fm_rstd(B, T, [buf[:, j, :T] for j in range(len(chs))], nfeat, [t_buf], X.rstd2, X.t_rstd2)
            for j in range(len(chs)):
                P.op("dve", lambda e, j=j: e.scalar_tensor_tensor(
                    outn[:, j, :T], buf[:, j, :T], gn[:, j:j + 1], X.rstd2[:, :T], ALU.mult, ALU.mult),
                    reads=[t_buf, X.t_rstd2, X.t_small], writes=[t_outn])

        latent_norm([13, 14, 15], 384, X.cq, X.t_cq, X.qn, X.cqn, X.t_cqn)
        for h in range(NHM):
            ba = ring[nxt("ipb", 6)]
            mm_group(banks[ba][0:96, :T], [(X.wuq[:, 0, h, kc, :], X.cqn[:, kc, :T]) for kc in range(3)],
                     [X.t_cqn, X.t_small], [t_bank[ba]])
            st, tst = stage_bf()
            if is_ctx:
                P.op("act", lambda e, ba=ba, st=st: e.activation(st[0:96, :T], banks[ba][0:96, :T], AF.Copy),
                     reads=[t_bank[ba]], writes=[tst])
            else:
                bb = ring[nxt("ipb", 6)]
                mm_group(banks[bb][0:96, :T], [(X.wuq[:, 1, h, kc, :], X.cqn[:, kc, :T]) for kc in range(3)],
                         [X.t_cqn, X.t_small], [t_bank[bb]])
                P.op("act", lambda e, ba=ba, st=st: e.activation(st[0:64, :T], banks[ba][0:64, :T], AF.Copy),
                     reads=[t_bank[ba]], writes=[tst])
                k = nxt("tmp", 2)
                k2 = nxt("sg", 2)
                P.op("dve", lambda e, ba=ba, k=k: e.tensor_tensor(B.tmp[k][64:96, :T], banks[ba][64:96, :T],
                                                                  X.ropeM[64:96, 0, :T], ALU.mult),
                     reads=[t_bank[ba], X.t_rope], writes=[B.t_tmp[k]])
                P.op("dve", lambda e, bb=bb, k2=k2: e.tensor_tensor(B.sg[k2][64:96, :T], banks[bb][64:96, :T],
                                                                    X.ropeM[64:96, 1, :T], ALU.mult),
                     reads=[t_bank[bb], X.t_rope], writes=[B.t_sg[k2]])
                P.op("pool", lambda e, k=k, k2=k2, st=st: e.tensor_tensor(st[64:96, :T], B.tmp[k][64:96, :T],
                                                                           B.sg[k2][64:96, :T], ALU.add),
                     reads=[B.t_tmp[k], B.t_sg[k2]], writes=[tst])
            P.dma("sp", QM[h, :, t0:t0 + T], st[0:96, :T], reads=[tst], writes=[ts])
        latent_norm([16, 17], 256, X.ckv, X.t_ckv, X.kvn, X.ckvn, X.t_ckvn)
        for j in range(4):
            ba = ring[nxt("ipb", 6)]
            mm_group(banks[ba][:, :T], [(X.wukvk[:, j, kc, :], X.ckvn[:, kc, :T]) for kc in range(2)],
                     [X.t_ckvn, X.t_small], [t_bank[ba]])
            st, tst = stage_bf()
            P.op("act", lambda e, ba=ba, st=st: e.activation(st[:, :T], banks[ba][:, :T], AF.Copy),
                 reads=[t_bank[ba]], writes=[tst])
            P.dma("sp", KN[j * 128:(j + 1) * 128, t0:t0 + T], st[:, :T], reads=[tst], writes=[ts])
        for bl in range(nblk):
            ba = ring[nxt("ipb", 6)]
            mm_group(banks[ba][:, :512], [(X.ckvn[:, kc, bl * 128:(bl + 1) * 128], X.wukvv[:, kc, :]) for kc in range(2)],
                     [X.t_ckvn, X.t_small], [t_bank[ba]])
            k = nxt("vst", 2)
            P.op("act", lambda e, ba=ba, k=k: e.activation(
                X.vst[k][:, :, 0:64], banks[ba][:, :512].rearrange("p (h d) -> p h d", d=64), AF.Copy),
                reads=[t_bank[ba]], writes=[X.t_vst[k]])
            P.dma("sp", VM[(t0 // 128) + bl], X.vst[k][:], reads=[X.t_vst[k]], writes=[ts])
        ba = fm_chunk(18, 96)
        st, tst = stage_bf()
        if is_ctx:
            P.op("act", lambda e, ba=ba, st=st: e.activation(st[64:96, :T], banks[ba][64:96, :T], AF.Copy),
                 reads=[t_bank[ba]], writes=[tst])
        else:
            bb = fm_chunk(19, 96)
            k = nxt("tmp", 2)
            k2 = nxt("sg", 2)
            P.op("dve", lambda e, ba=ba, k=k: e.tensor_tensor(B.tmp[k][64:96, :T], banks[ba][64:96, :T],
                                                              X.ropeM[64:96, 0, :T], ALU.mult),
                 reads=[t_bank[ba], X.t_rope], writes=[B.t_tmp[k]])
            P.op("dve", lambda e, bb=bb, k2=k2: e.tensor_tensor(B.sg[k2][64:96, :T], banks[bb][64:96, :T],
                                                                X.ropeM[64:96, 1, :T], ALU.mult),
                 reads=[t_bank[bb], X.t_rope], writes=[B.t_sg[k2]])
            P.op("pool", lambda e, k=k, k2=k2, st=st: e.tensor_tensor(st[64:96, :T], B.tmp[k][64:96, :T],
                                                                       B.sg[k2][64:96, :T], ALU.add),
                 reads=[B.t_tmp[k], B.t_sg[k2]], writes=[tst])
        P.dma("sp", KR[:, t0:t0 + T], st[64:96, :T], reads=[tst], writes=[ts])
        for j in range(6):
            ch_a = 20 + j if j < 4 else 28 + (j - 4)
            ch_b = 24 + j if j < 4 else 30 + (j - 4)
            dst = QS[j] if j < 4 else KS[j - 4]
            ba = fm_chunk(ch_a)
            st, tst = stage_bf()
            if is_ctx:
                P.op("act", lambda e, ba=ba, st=st: e.activation(st[:, :T], banks[ba][:, :T], AF.Copy),
                     reads=[t_bank[ba]], writes=[tst])
            else:
                bb = fm_chunk(ch_b)
                k = nxt("tmp", 2)
                k2 = nxt("sg", 2)
                P.op("dve", lambda e, ba=ba, k=k: e.tensor_tensor(B.tmp[k][:, :T], banks[ba][:, :T],
                                                                  X.ropeS[:, 0, :T], ALU.mult),
                     reads=[t_bank[ba], X.t_rope], writes=[B.t_tmp[k]])
                P.op("dve", lambda e, bb=bb, k2=k2: e.tensor_tensor(B.sg[k2][:, :T], banks[bb][:, :T],
                                                                    X.ropeS[:, 1, :T], ALU.mult),
                     reads=[t_bank[bb], X.t_rope], writes=[B.t_sg[k2]])
                P.op("pool", lambda e, k=k, k2=k2, st=st: e.tensor_tensor(st[:, :T], B.tmp[k][:, :T],
                                                                           B.sg[k2][:, :T], ALU.add),
                     reads=[B.t_tmp[k], B.t_sg[k2]], writes=[tst])
            P.dma("sp", dst[:, t0:t0 + T], st[:, :T], reads=[tst], writes=[ts])
        s_vs = load_w(B, win_bf[l][40], t_win[l])
        for bl in range(nblk):
            ba = ring[nxt("ipb", 6)]
            mm_group(banks[ba][:, :128], [(B.u[:, kc, bl * 128:(bl + 1) * 128], B.w[s_vs][:, kc, :]) for kc in range(DC)],
                     [B.t_w[s_vs]] + tu_all, [t_bank[ba]])
            k = nxt("vsst", 2)
            P.op("act", lambda e, ba=ba, k=k: e.activation(
                X.vsst[k][:, :, 0:64], banks[ba][:, :128].rearrange("p (g d) -> p g d", d=64), AF.Copy),
                reads=[t_bank[ba]], writes=[X.t_vsst[k]])
            P.dma("sp", VS[(t0 // 128) + bl], X.vsst[k][:], reads=[X.t_vsst[k]], writes=[ts])
        for half in range(2):
            sz = [load_w(B, win_bf[l][32 + half * 4 + j], t_win[l]) for j in range(4)]
            for bl in range(nblk):
                ba = ring[nxt("ipb", 6)]
                for j in range(4):
                    mm_group(banks[ba][:, j * 128:(j + 1) * 128],
                             [(B.u[:, kc, bl * 128:(bl + 1) * 128], B.w[sz[j]][:, kc, :]) for kc in range(DC)],
                             [B.t_w[sz[j]]] + tu_all, [t_bank[ba]])
                st, tst = stage_bf()
                P.op("act", lambda e, ba=ba, st=st: e.activation(st[:, :512], banks[ba][:, :512], AF.Silu),
                     reads=[t_bank[ba]], writes=[tst])
                P.dma("sp", SZ[(t0 // 128) + bl, :, half * 512:(half + 1) * 512], st[:, :512], reads=[tst], writes=[ts])

    def phase1(l):
        B = dense_alloc()
        X = Cfg()
        X.stb = [AR.alloc([128, TT], BF16) for _ in range(3)]
        X.t_stb = [Tok() for _ in range(3)]
        X.stf = AR.alloc([128, TT], F32)
        X.t_stf = Tok()
        X.ropeM = AR.alloc([128, 2, TT], F32)
        X.ropeS = AR.alloc([128, 2, TT], F32)
        X.t_rope = Tok()
        X.cq = AR.alloc([128, 3, TT], F32)
        X.t_cq = Tok()
        X.cqn = AR.alloc([128, 3, TT], BF16)
        X.t_cqn = Tok()
        X.ckv = AR.alloc([128, 2, TT], F32)
        X.t_ckv = Tok()
        X.ckvn = AR.alloc([128, 2, TT], BF16)
        X.t_ckvn = Tok()
        X.rstd2 = AR.alloc([128, TT], F32)
        X.t_rstd2 = Tok()
        X.vst = [AR.alloc([128, NHM, 65], BF16) for _ in range(2)]
        X.t_vst = [Tok(), Tok()]
        X.vsst = [AR.alloc([128, 2, 65], BF16) for _ in range(2)]
        X.t_vsst = [Tok(), Tok()]
        X.wuq = AR.alloc([128, 2, NHM, 3, 96], BF16) if False else None
        X.wuq = AR.alloc([128, 2 * NHM * 3 * 96], BF16).rearrange("p (v h k j) -> p v h k j", v=2, h=NHM, k=3)
        X.wukvk = AR.alloc([128, 4, 2, 128], BF16)
        X.wukvv = AR.alloc([128, 2, 512], BF16)
        X.qn = AR.alloc([128, 3], F32)
        X.kvn = AR.alloc([128, 2], F32)
        X.t_small = Tok()
        for v in range(2):
            P.dma("pool", X.wuq[:, v], wuq_in[l, v].rearrange("h p k j -> p h k j"), writes=[X.t_small])
        P.dma("pool", X.wukvk[:], wukvk_in[l].rearrange("c p k j -> p c k j"), writes=[X.t_small])
        P.dma("pool", X.wukvv[:], wukvv_in[l], writes=[X.t_small])
        P.dma("sp", X.qn[:], qn_in[l], writes=[X.t_small])
        P.dma("sp", X.kvn[:], kvn_in[l], writes=[X.t_small])
        for k in range(2):
            P.op("pool", lambda e, k=k: e.memset(X.vst[k][:], 1.0), writes=[X.t_vst[k]])
            P.op("pool", lambda e, k=k: e.memset(X.vsst[k][:], 1.0), writes=[X.t_vsst[k]])
        print("phase1 arena used", AR.off)
        for ti, (t0, T, is_ctx) in enumerate(c.tiles):
            col = 1 if is_ctx else 0
            P.dma("sp", B.hx[:, :, :T], hx_view[:, :, t0:t0 + T], reads=[t_hx_dram[ti]], writes=B.t_hx)
            modnorm(B, T, l, AV[l][:, 0], MOD[l][:, 0 * DC:1 * DC, :], col)
            ffn(B, T, l, 0, GV[l][:, 0], col)
            P.dma("sp", hx_view[:, :, t0:t0 + T], B.hx[:, :, :T], reads=B.t_hx, writes=[t_hx_dram[ti]])
            modnorm(B, T, l, AV[l][:, 1], MOD[l][:, 3 * DC:4 * DC, :], col)
            inproj(B, X, T, l, ti, t0, is_ctx)
        P.barrier()

    def phase2(l):
        last = (l == L - 1)
        B = dense_alloc()
        print("phase2 arena used", AR.off)
        for ti, (t0, T, is_ctx) in enumerate(c.tiles):
            if last and is_ctx:
                continue
            col = 1 if is_ctx else 0
            P.dma("sp", B.hx[:, :, :T], hx_view[:, :, t0:t0 + T], reads=[t_hx_dram[ti]], writes=B.t_hx)
            P.dma("sp", B.u[:, :KM, :T], OX.rearrange("(c p) t -> p c t", p=128)[:, :, t0:t0 + T],
                  reads=[t_ox[ti]], writes=B.t_u[:KM])
            for dc in range(DC):
                s = load_w(B, wout_bf[l][dc], t_wout[l], K=KM)
                bo, to = banks[4 + dc % 2], t_bank[4 + dc % 2]
                mm_group(bo[:, :T], [(B.w[s][:, kc, :], B.u[:, kc, :T]) for kc in range(KM)],
                         [B.t_w[s]] + B.t_u[:KM], [to])
                P.op("dve", lambda e, dc=dc, bo=bo, T=T, col=col: e.scalar_tensor_tensor(
                    B.hx[:, dc, :T], bo[:, :T], GV[l][:, 1, dc, col:col + 1], B.hx[:, dc, :T], ALU.mult, ALU.add),
                    reads=[to, t_mod[l]], writes=[B.t_hx[dc]])
            modnorm(B, T, l, AV[l][:, 2], MOD[l][:, 6 * DC:7 * DC, :], col)
            ffn(B, T, l, 1, GV[l][:, 2], col)
            if not last:
                P.dma("sp", hx_view[:, :, t0:t0 + T], B.hx[:, :, :T], reads=B.t_hx, writes=[t_hx_dram[ti]])
            else:
                fm_rstd(B, T, [B.hx[:, dc, :T] for dc in range(DC)], D, B.t_hx, B.rstd, B.t_rstd)
                for dc in range(DC):
                    P.op("dve", lambda e, dc=dc, T=T: e.scalar_tensor_tensor(
                        B.hx[:, dc, :T], B.hx[:, dc, :T], fin[:, dc:dc + 1], B.rstd[:, :T], ALU.mult, ALU.mult),
                        reads=[B.t_hx[dc], B.t_rstd, t_fin], writes=[B.t_hx[dc]])
                out_stores.append(P.dma("sp", out_view[:, :, t0 - CTX:t0 - CTX + T], B.hx[:, :, :T],
                                        reads=B.t_hx, writes=[Tok()]))
        P.barrier()

    t_prep = [Tok(f"prep{b}") for b in range(NB)]
    t_yf = [Tok(f"yf{b}") for b in range(NB)]
    t_oxb = Tok("oxall")
    t_om = Tok("om")
    t_os = Tok("os")
    OXv = OX.rearrange("(c p) t -> p c t", p=128)

    def all_scr():
        return t_scr

    def mixer_prep(l):
        AR.reset()
        xin = [AR.alloc([128, 12, TT + 4], BF16) for _ in range(2)]
        t_xin = [Tok(), Tok()]
        acc = [AR.alloc([128, TT], F32) for _ in range(3)]
        t_acc = [Tok() for _ in range(3)]
        xc = AR.alloc([128, 12, TT], BF16)
        t_xc = [Tok() for _ in range(12)]
        xst = [AR.alloc([128, 1024], BF16) for _ in range(2)]
        t_xst = [Tok(), Tok()]
        bst = [AR.alloc([128, 2, 128], BF16) for _ in range(2)]
        t_bst = [Tok(), Tok()]
        dtin = AR.alloc([64, TT], F32)
        dte = AR.alloc([64, TT], F32)
        t_dt = Tok()
        dst = [AR.alloc([128, 64], F32) for _ in range(2)]
        t_dst = [Tok(), Tok()]
        cw = AR.alloc([128, 12, 5], F32)
        cb = AR.alloc([128, 12], F32)
        dtb = AR.alloc([64, 1], F32)
        aneg = AR.alloc([64, 1], F32)
        t_cw = Tok()
        P.dma("sp", cw[:], convw_in[l], writes=[t_cw])
        P.dma("sp", cb[:], convb_in[l], writes=[t_cw])
        P.dma("sp", dtb[:], dtb_in[l], writes=[t_cw])
        P.dma("sp", aneg[:], alog_in[l], writes=[t_cw])
        P.op("act", lambda e: e.activation(aneg[:], aneg[:], AF.Exp), reads=[t_cw], writes=[t_cw])
        P.op("dve", lambda e: e.tensor_scalar(aneg[:], aneg[:], -1.0, None, ALU.mult), reads=[t_cw], writes=[t_cw])
        print("prep arena used", AR.off)
        XBCv = XBC.rearrange("(c p) t -> p c t", p=128)
        for ti, (t0, T, is_ctx) in enumerate(c.tiles):
            seq0, seq1 = (0, CTX) if is_ctx else (CTX, NTOK)
            k = nxt("xin", 2)
            lo = max(seq0, t0 - 2)
            hi = min(seq1, t0 + T + 2)
            rd = [t_scr[ti]]
            if ti > 0:
                rd.append(t_scr[ti - 1])
            if ti + 1 < NT:
                rd.append(t_scr[ti + 1])
            if lo > t0 - 2:
                P.op("pool", lambda e, k=k: e.memset(xin[k][:, :, 0:2], 0.0), writes=[t_xin[k]])
            if hi < t0 + T + 2:
                P.op("pool", lambda e, k=k, T=T: e.memset(xin[k][:, :, T + 2:T + 4], 0.0), writes=[t_xin[k]])
            P.dma("sp", xin[k][:, :, lo - (t0 - 2):hi - (t0 - 2)], XBCv[:, :, lo:hi], reads=rd, writes=[t_xin[k]])
            for cc in range(12):
                a = nxt("acc", 3)
                P.op("act", lambda e, cc=cc, a=a, k=k, T=T: e.activation(
                    acc[a][:, :T], xin[k][:, cc, 0:T], AF.Identity, bias=cb[:, cc:cc + 1], scale=cw[:, cc, 0:1]),
                    reads=[t_xin[k], t_cw], writes=[t_acc[a]])
                for kk in range(1, 5):
                    eng = "dve"
                    P.op(eng, lambda e, cc=cc, a=a, k=k, kk=kk, T=T: e.scalar_tensor_tensor(
                        acc[a][:, :T], xin[k][:, cc, kk:kk + T], cw[:, cc, kk:kk + 1], acc[a][:, :T], ALU.mult, ALU.add),
                        reads=[t_xin[k], t_cw], writes=[t_acc[a]])
                P.op("act", lambda e, cc=cc, a=a, T=T: e.activation(xc[:, cc, :T], acc[a][:, :T], AF.Silu),
                     reads=[t_acc[a]], writes=[t_xc[cc]])
            P.dma("sp", dtin[0:32, :T], DTR[:, t0:t0 + T], reads=[t_scr[ti]], writes=[t_dt])
            P.dma("sp", dtin[32:64, :T], DTR[:, t0:t0 + T], reads=[t_scr[ti]], writes=[t_dt])
            P.op("act", lambda e, T=T: e.activation(dte[:, :T], dtin[:, :T], AF.Exp, bias=dtb[:, 0:1], scale=1.0),
                 reads=[t_dt, t_cw], writes=[t_dt])
            P.op("act", lambda e, T=T: e.activation(dtin[:, :T], dte[:, :T], AF.Ln, bias=one_ap[0:64, 0:1], scale=1.0),
                 reads=[t_dt, t_eps], writes=[t_dt])
            P.op("dve", lambda e, T=T: e.tensor_scalar(dtin[32:64, :T], dtin[32:64, :T], aneg[32:64, 0:1], None, ALU.mult),
                 reads=[t_dt, t_cw], writes=[t_dt])
            for bl in range(T // 128):
                blk = t0 // 128 + bl
                cs = slice(bl * 128, (bl + 1) * 128)
                bi = nxt("tpb", 4)
                P.op("pe", lambda e, bi=bi, cs=cs: [e.transpose(banks_b[bi][:, j * 128:(j + 1) * 128], xc[:, j, cs], IDENT_B)
                                                    for j in range(8)][-1],
                     reads=t_xc[0:8] + [t_cst], writes=[t_bank[bi]])
                ks = nxt("xst", 2)
                P.op("act", lambda e, bi=bi, ks=ks: e.activation(xst[ks][:], banks_b[bi][:, :1024], AF.Copy),
                     reads=[t_bank[bi]], writes=[t_xst[ks]])
                P.dma("sp", XS[blk], xst[ks][:], reads=[t_xst[ks]], writes=[t_prep[blk]])
                bi = nxt("tpb", 4)
                P.op("pe", lambda e, bi=bi, cs=cs: [e.transpose(banks_b[bi][:, j * 128:(j + 1) * 128], xc[:, 8 + j, cs], IDENT_B)
                                                    for j in range(2)][-1],
                     reads=t_xc[8:10] + [t_cst], writes=[t_bank[bi]])
                kb = nxt("bst", 2)
                P.op("dve", lambda e, bi=bi, kb=kb: e.tensor_copy(
                    bst[kb][:], banks_b[bi][:, :256].rearrange("p (g n) -> p g n", n=128)),
                    reads=[t_bank[bi]], writes=[t_bst[kb]])
                P.dma("sp", BTM[blk], bst[kb][:], reads=[t_bst[kb]], writes=[t_prep[blk]])
                P.dma("sp", BCT[blk], xc[:, 8:12, cs], reads=t_xc[8:12], writes=[t_prep[blk]])
                bi = 4 + nxt("tpf", 2)
                P.op("pe", lambda e, bi=bi, cs=cs: e.transpose(banks[bi][:, 0:64], dtin[:, cs], IDENT_F[0:64, 0:64]),
                     reads=[t_dt, t_cst], writes=[t_bank[bi]])
                kd = nxt("dst", 2)
                P.op("dve", lambda e, bi=bi, kd=kd: e.tensor_copy(dst[kd][:], banks[bi][:, 0:64]),
                     reads=[t_bank[bi]], writes=[t_dst[kd]])
                P.dma("sp", DTA[blk], dst[kd][:], reads=[t_dst[kd]], writes=[t_prep[blk]])
        P.barrier()

    def ssd_scan(l):
        last = (l == L - 1)
        AR.reset()
        NBUF = 2
        xs = [AR.alloc([128, 16, 64], BF16) for _ in range(NBUF)]
        btm = [AR.alloc([128, 2, 128], BF16) for _ in range(NBUF)]
        bct = [AR.alloc([128, 4, 128], BF16) for _ in range(NBUF)]
        dta = [AR.alloc([128, 64], F32) for _ in range(NBUF)]
        t_ld = [Tok() for _ in range(NBUF)]
        H = AR.alloc([128, 2, 512], F32)
        Hb = AR.alloc([128, 2, 512], BF16)
        t_H = Tok()
        t_Hb = Tok()
        acs = AR.alloc([128, 16], F32)
        wv = AR.alloc([128, 16], F32)
        dif = AR.alloc([128, 16], F32)
        fv = AR.alloc([128, 16], F32)
        cdv = AR.alloc([128, 16], F32)
        t_sm = Tok()
        xdt = AR.alloc([128, 16, 64], BF16)
        xdte = AR.alloc([128, 16, 64], BF16)
        t_xdt = Tok()
        t_xdte = Tok()
        cbm = AR.alloc([128, 2, 128], F32)
        t_cbm = Tok()
        NR = 4
        lh = [AR.alloc([128, 128], F32) for _ in range(NR)]
        t_lh = [Tok() for _ in range(NR)]
        eh = [AR.alloc([128, 128], F32) for _ in range(NR)]
        t_eh = [Tok() for _ in range(NR)]
        mh = [AR.alloc([128, 128], BF16) for _ in range(NR)]
        t_mh = [Tok() for _ in range(NR)]
        yo = AR.alloc([128, 16, 64], F32)
        t_yo = Tok()
        yst = [AR.alloc([128, 1024], F32) for _ in range(2)]
        t_yst = [Tok(), Tok()]
        yfl = AR.alloc([128, 1024], F32)
        t_yfl = Tok()
        szl = AR.alloc([128, 1024], BF16)
        t_szl = Tok()
        ysq = AR.alloc([128, 1024], F32)
        ms = AR.alloc([128, 2], F32)
        t_ms = Tok()
        ybf = AR.alloc([128, 1024], BF16)
        t_ybf = Tok()
        oxs = [AR.alloc([128, 8, 128], BF16) for _ in range(2)]
        t_oxs = [Tok(), Tok()]
        dsk = AR.alloc([128, 16], F32)
        sng = AR.alloc([128, 1024], F32)
        t_cn = Tok()
        P.dma("sp", dsk[:], dsk_in[l], writes=[t_cn])
        P.dma("sp", sng[:], sn_in[l], writes=[t_cn])
        print("ssd arena used", AR.off)
        for d in range(2):
            T_d = T_F if d == 0 else T_B
            U_d = U_F if d == 0 else U_B
            TBm = T_F if d == 0 else T_B
            ctx_order = list(range(NBC)) if d == 0 else list(range(NBC - 1, -1, -1))
            lat_order = list(range(NBC, NB)) if d == 0 else list(range(NB - 1, NBC - 1, -1))
            P.op("pool", lambda e: e.memset(H[:], 0.0), writes=[t_H])
            P.op("pool", lambda e: e.memset(Hb[:], 0.0), writes=[t_Hb])
            for blk in ctx_order + lat_order:
                is_ctx = blk < NBC
                need_y = not (last and is_ctx)
                k = nxt("sld", NBUF)
                P.dma("sp", xs[k][:], XS[blk].rearrange("p (h d) -> p h d", d=64), reads=[t_prep[blk]], writes=[t_ld[k]])
                P.dma("sp", btm[k][:], BTM[blk], reads=[t_prep[blk]], writes=[t_ld[k]])
                P.dma("sp", bct[k][:], BCT[blk], reads=[t_prep[blk]], writes=[t_ld[k]])
                P.dma("sp", dta[k][:], DTA[blk], reads=[t_prep[blk]], writes=[t_ld[k]])
                dtv = dta[k][:, d * 16:(d + 1) * 16]
                av = dta[k][:, 32 + d * 16:32 + (d + 1) * 16]
                b7, t7 = banks[7], t_bank[7]
                mm_group(b7[:, 0:16], [(T_d, av)], [t_cst, t_ld[k]], [t7])
                mm_group(b7[:, 16:32], [(ONES_F, av)], [t_cst, t_ld[k]], [t7])
                P.op("act", lambda e, b7=b7: e.activation(acs[:], b7[:, 0:16], AF.Copy), reads=[t7], writes=[t_sm])
                P.op("act", lambda e, b7=b7: e.activation(wv[:], b7[:, 0:16], AF.Exp), reads=[t7], writes=[t_sm])
                P.op("act", lambda e, b7=b7: e.activation(cdv[:], b7[:, 16:32], AF.Exp), reads=[t7], writes=[t_sm])
                P.op("dve", lambda e, b7=b7: e.tensor_tensor(dif[:], b7[:, 16:32], acs[:], ALU.subtract),
                     reads=[t7, t_sm], writes=[t_sm])
                P.op("act", lambda e: e.activation(dif[:], dif[:], AF.Exp), reads=[t_sm], writes=[t_sm])
                P.op("dve", lambda e, dtv=dtv: e.tensor_tensor(fv[:], dif[:], dtv, ALU.mult),
                     reads=[t_sm, t_ld[k]], writes=[t_sm])
                P.op("dve", lambda e, k=k, dtv=dtv: e.tensor_tensor(
                    xdt[:], xs[k][:], dtv.unsqueeze(2).to_broadcast([128, 16, 64]), ALU.mult),
                    reads=[t_ld[k]], writes=[t_xdt])
                P.op("pool", lambda e, k=k: e.tensor_tensor(
                    xdte[:], xs[k][:], fv[:].unsqueeze(2).to_broadcast([128, 16, 64]), ALU.mult),
                    reads=[t_ld[k], t_sm], writes=[t_xdte])
                for g in range(2):
                    if need_y:
                        mm_group(banks[6][:, g * 128:(g + 1) * 128], [(bct[k][:, g, :], bct[k][:, 2 + g, :])],
                                 [t_ld[k]], [t_bank[6]])
                        mm_group(banks[4 + g][:, :512], [(bct[k][:, 2 + g, :], Hb[:, g, :])], [t_ld[k], t_Hb],
                                 [t_bank[4 + g]])
                if need_y:
                    P.op("dve", lambda e, TBm=TBm: e.tensor_tensor(
                        cbm[:], banks[6][:, 0:256].rearrange("p (g n) -> p g n", n=128),
                        TBm.unsqueeze(1).to_broadcast([128, 2, 128]), ALU.mult),
                        reads=[t_bank[6], t_cst], writes=[t_cbm])
                    for g in range(2):
                        P.op("dve", lambda e, g=g: e.tensor_tensor(
                            yo[:, g * 8:(g + 1) * 8, :], banks[4 + g][:, :512].rearrange("p (h d) -> p h d", d=64),
                            wv[:, g * 8:(g + 1) * 8].unsqueeze(2).to_broadcast([128, 8, 64]), ALU.mult),
                            reads=[t_bank[4 + g], t_sm], writes=[t_yo])
                for g in range(2):
                    mm_group(banks[2 + g][:, :512], [(btm[k][:, g, :], xdte[:, g * 8:(g + 1) * 8, :].rearrange("p h d -> p (h d)"))],
                             [t_ld[k], t_xdte], [t_bank[2 + g]])
                    P.op("pool", lambda e, g=g: e.tensor_tensor(
                        H[:, g, :].rearrange("p (h d) -> p h d", d=64), H[:, g, :].rearrange("p (h d) -> p h d", d=64),
                        cdv[:, g * 8:(g + 1) * 8].unsqueeze(2).to_broadcast([128, 8, 64]), ALU.mult),
                        reads=[t_sm], writes=[t_H])
                    P.op("dve", lambda e, g=g: e.tensor_tensor(H[:, g, :], H[:, g, :], banks[2 + g][:, :512], ALU.add),
                         reads=[t_bank[2 + g]], writes=[t_H])
                    P.op("act", lambda e, g=g: e.activation(Hb[:, g, :], H[:, g, :], AF.Copy), reads=[t_H], writes=[t_Hb])
                if not need_y:
                    continue
                ky = nxt("yst", 2)
                Y, tY = yst[ky], t_yst[ky]
                for h in range(16):
                    g = h // 8
                    r = nxt("lh", NR)
                    P.op("dve" if h % 2 == 0 else "pool", lambda e, r=r, h=h, av=av, U_d=U_d: e.tensor_scalar(
                        lh[r][:], U_d, av[:, h:h + 1], None, ALU.mult), reads=[t_cst, t_ld[k]], writes=[t_lh[r]])
                    sl = nxt("dsl", 8)
                    dps = banks[sl // 4][:, (sl % 4) * 128:(sl % 4 + 1) * 128]
                    mm_group(dps, [(lh[r][:], T_d)], [t_lh[r], t_cst], [t_dsl[sl]])
                    P.op("act", lambda e, r=r, dps=dps: e.activation(eh[r][:], dps, AF.Exp),
                         reads=[t_dsl[sl]], writes=[t_eh[r]])
                    P.op("dve", lambda e, r=r, g=g: e.tensor_tensor(mh[r][:], eh[r][:], cbm[:, g, :], ALU.mult),
                         reads=[t_eh[r], t_cbm], writes=[t_mh[r]])
                    ys = nxt("ysl", 6)
                    yps = banks[7][:, 128 + ys * 64:128 + (ys + 1) * 64]
                    mm_group(yps, [(mh[r][:], xdt[:, h, :])], [t_mh[r], t_xdt], [t_ysl[ys]])
                    P.op("dve", lambda e, h=h, yps=yps, Y=Y: e.tensor_tensor(
                        Y[:, h * 64:(h + 1) * 64], yps, yo[:, h, :], ALU.add),
                        reads=[t_ysl[ys], t_yo], writes=[tY])
                if d == 0:
                    P.dma("sp", YF[blk], Y[:], reads=[tY], writes=[t_yf[blk]])
                    continue
                P.dma("sp", yfl[:], YF[blk], reads=[t_yf[blk]], writes=[t_yfl])
                P.dma("sp", szl[:], SZ[blk], reads=t_scr, writes=[t_szl])
                P.op("dve", lambda e, Y=Y: e.tensor_tensor(Y[:], Y[:], yfl[:], ALU.add), reads=[t_yfl], writes=[tY])
                P.op("pool", lambda e, k=k: e.tensor_tensor(
                    ysq[:].rearrange("p (h d) -> p h d", d=64), xs[k][:],
                    dsk[:].unsqueeze(2).to_broadcast([128, 16, 64]), ALU.mult),
                    reads=[t_ld[k], t_cn], writes=[t_ms])
                P.op("dve", lambda e, Y=Y: e.tensor_tensor(Y[:], Y[:], ysq[:], ALU.add), reads=[t_ms], writes=[tY])
                P.op("dve", lambda e, Y=Y: e.tensor_tensor(Y[:], Y[:], szl[:], ALU.mult), reads=[t_szl], writes=[tY])
                for g in range(2):
                    P.op("act", lambda e, g=g, Y=Y: e.activation(
                        ysq[:, g * 512:(g + 1) * 512], Y[:, g * 512:(g + 1) * 512], AF.Square, accum_out=ms[:, g:g + 1]),
                        reads=[tY], writes=[t_ms])
                P.op("act", lambda e: e.activation(ms[:], ms[:], AF.Sqrt, bias=eps_ap[:, 0:1], scale=1.0 / 512),
                     reads=[t_ms, t_eps], writes=[t_ms])
                P.op("dve", lambda e: e.reciprocal(ms[:], ms[:]), reads=[t_ms], writes=[t_ms])
                for g in range(2):
                    P.op("dve", lambda e, g=g, Y=Y: e.scalar_tensor_tensor(
                        ybf[:, g * 512:(g + 1) * 512], Y[:, g * 512:(g + 1) * 512], ms[:, g:g + 1],
                        sng[:, g * 512:(g + 1) * 512], ALU.mult, ALU.mult),
                        reads=[tY, t_ms, t_cn], writes=[t_ybf])
                bi = 2 + nxt("tpo", 2)
                P.op("pe", lambda e, bi=bi: [e.transpose(banks_b[bi][:, j * 128:(j + 1) * 128],
                                                        ybf[:, j * 128:(j + 1) * 128], IDENT_B) for j in range(8)][-1],
                     reads=[t_ybf, t_cst], writes=[t_bank[bi]])
                ko = nxt("oxs", 2)
                P.op("act", lambda e, bi=bi, ko=ko: e.activation(
                    oxs[ko][:], banks_b[bi][:, :1024].rearrange("p (c t) -> p c t", t=128), AF.Copy),
                    reads=[t_bank[bi]], writes=[t_oxs[ko]])
                P.dma("sp", OXv[:, 0:8, blk * 128:(blk + 1) * 128], oxs[ko][:], reads=[t_oxs[ko]], writes=[t_oxb])
        P.barrier()

    t_dsl = [Tok() for _ in range(8)]
    t_ysl = [Tok() for _ in range(6)]

    def mla_attn(l):
        last = (l == L - 1)
        AR.reset()
        Kh = [AR.alloc([96, NTOK], BF16) for _ in range(2)]
        Vh = [AR.alloc([128, NB, 65], BF16) for _ in range(2)]
        t_kv = [Tok(), Tok()]
        Qt = [AR.alloc([96, TT], BF16) for _ in range(2)]
        t_q = [Tok(), Tok()]
        NE = 4
        E = [AR.alloc([128, TT], BF16) for _ in range(NE)]
        t_E = [Tok() for _ in range(NE)]
        Ost = [AR.alloc([65, TT], F32) for _ in range(2)]
        t_ost = [Tok(), Tok()]
        print("mla arena used", AR.off)
        for h in range(NHM):
            kk = nxt("mkv", 2)
            P.dma("sp", Kh[kk][0:64, :], KN[h * 64:(h + 1) * 64, :], reads=t_scr, writes=[t_kv[kk]])
            P.dma("sp", Kh[kk][64:96, :], KR[:, :], reads=t_scr, writes=[t_kv[kk]])
            P.dma("sp", Vh[kk][:], VM[:, :, h, :].rearrange("b p d -> p b d"), reads=t_scr, writes=[t_kv[kk]])
            for ti, (t0, T, is_ctx) in enumerate(c.tiles):
                if is_ctx and last:
                    continue
                kq = nxt("mq", 2)
                P.dma("sp", Qt[kq][:, :T], QM[h, :, t0:t0 + T], reads=[t_scr[ti]], writes=[t_q[kq]])
                kblocks = list(range(NBC)) if is_ctx else list(range(NB))
                ob = 6 + nxt("mob", 2)
                for i, kb in enumerate(kblocks):
                    sb_ = nxt("msb", 4)
                    mm_group(banks[sb_][:, :T], [(Kh[kk][:, kb * 128:(kb + 1) * 128], Qt[kq][:, :T])],
                             [t_kv[kk], t_q[kq]], [t_bank[sb_]])
                    ke = nxt("me", NE)
                    P.op("act", lambda e, sb_=sb_, ke=ke, T=T: e.activation(E[ke][:, :T], banks[sb_][:, :T], AF.Exp,
                                                                            scale=SC_MLA),
                         reads=[t_bank[sb_]], writes=[t_E[ke]])

                    def pv(e, ke=ke, kb=kb, ob=ob, T=T, first=(i == 0), lastk=(i == len(kblocks) - 1), kk=kk):
                        return e.matmul(banks[ob][0:65, :T], Vh[kk][:, kb, :], E[ke][:, :T], start=first, stop=lastk)
                    P.op("pe", pv, reads=[t_E[ke], t_kv[kk]], writes=[t_bank[ob]])
                ko = nxt("most", 2)
                P.op("dve", lambda e, ob=ob, ko=ko, T=T: e.tensor_copy(Ost[ko][:, :T], banks[ob][0:65, :T]),
                     reads=[t_bank[ob]], writes=[t_ost[ko]])
                P.dma("sp", OM[h, :, t0:t0 + T], Ost[ko][:, :T], reads=[t_ost[ko]], writes=[t_om])
        P.barrier()

    def swa_attn(l):
        last = (l == L - 1)
        AR.reset()
        Kg = [AR.alloc([128, NTOK], BF16) for _ in range(2)]
        Vg = [AR.alloc([128, NB, 65], BF16) for _ in range(2)]
        t_kv = [Tok(), Tok()]
        Qb = [AR.alloc([128, 2, 128], BF16) for _ in range(2)]
        t_q = [Tok(), Tok()]
        NE = 4
        E = [AR.alloc([128, 256], BF16) for _ in range(NE)]
        t_E = [Tok() for _ in range(NE)]
        Ost = [AR.alloc([65, 4, 128], F32) for _ in range(2)]
        t_ost = [Tok(), Tok()]
        print("swa arena used", AR.off)
        for g in range(2):
            kk = nxt("skv", 2)
            P.dma("sp", Kg[kk][:], KS[g], reads=t_scr, writes=[t_kv[kk]])
            P.dma("sp", Vg[kk][:], VS[:, :, g, :].rearrange("b p d -> p b d"), reads=t_scr, writes=[t_kv[kk]])
            for qb in range(NB):
                is_ctx = qb < NBC
                if is_ctx and last:
                    continue
                kq = nxt("sq", 2)
                P.dma("sp", Qb[kq][:], QS[2 * g:2 * g + 2, :, qb * 128:(qb + 1) * 128].rearrange("c p t -> p c t"),
                      reads=t_scr, writes=[t_q[kq]])
                if is_ctx:
                    kbs = [(kb, None) for kb in range(NBC)]
                else:
                    kbs = [(kb, None) for kb in range(NBC)]
                    if qb - 1 >= NBC:
                        kbs.append((qb - 1, TB_B))
                    kbs.append((qb, None))
                    if qb + 1 < NB:
                        kbs.append((qb + 1, TB_F))
                ko = nxt("sost", 2)
                for half in range(2):
                    rows = slice(half * 64, (half + 1) * 64)
                    ob = 6 + half
                    for i, (kb, msk) in enumerate(kbs):
                        sb_ = nxt("ssb", 4)
                        mm_group(banks[sb_][:, :256], [(Kg[kk][rows, kb * 128:(kb + 1) * 128],
                                                        Qb[kq][rows, :, :].rearrange("p c t -> p (c t)"))],
                                 [t_kv[kk], t_q[kq]], [t_bank[sb_]])
                        ke = nxt("se", NE)
                        P.op("act", lambda e, sb_=sb_, ke=ke: e.activation(E[ke][:, :256], banks[sb_][:, :256], AF.Exp,
                                                                           scale=SC_SWA),
                             reads=[t_bank[sb_]], writes=[t_E[ke]])
                        if msk is not None:
                            P.op("dve", lambda e, ke=ke, msk=msk: e.tensor_tensor(
                                E[ke][:, :256].rearrange("p (c t) -> p c t", t=128),
                                E[ke][:, :256].rearrange("p (c t) -> p c t", t=128),
                                msk.unsqueeze(1).to_broadcast([128, 2, 128]), ALU.mult),
                                reads=[t_cst], writes=[t_E[ke]])
                        for j in range(2):
                            obj = 4 + 2 * half + j
                            def pv(e, ke=ke, kb=kb, obj=obj, j=j, first=(i == 0), lastk=(i == len(kbs) - 1), kk=kk):
                                return e.matmul(banks[obj][0:65, 0:128], Vg[kk][:, kb, :],
                                                E[ke][:, j * 128:(j + 1) * 128], start=first, stop=lastk)
                            P.op("pe", pv, reads=[t_E[ke], t_kv[kk]], writes=[t_bank[obj]])
                    for j in range(2):
                        obj = 4 + 2 * half + j
                        P.op("dve", lambda e, obj=obj, ko=ko, half=half, j=j: e.tensor_copy(
                            Ost[ko][:, 2 * half + j, :], banks[obj][0:65, 0:128]),
                            reads=[t_bank[obj]], writes=[t_ost[ko]])
                for half in range(2):
                    for j in range(2):
                        hd = 2 * (2 * g + j) + half
                        P.dma("sp", OS[hd, :, qb * 128:(qb + 1) * 128], Ost[ko][:, 2 * half + j, :],
                              reads=[t_ost[ko]], writes=[t_os])
        P.barrier()

    def attn_epilogue(l):
        last = (l == L - 1)
        AR.reset()
        num = AR.alloc([64, 8, TT], F32)
        den = AR.alloc([64, 8, TT], F32)
        sq = AR.alloc([64, 8, TT], BF16)
        ob = AR.alloc([64, 8, TT], BF16)
        rs = AR.alloc([128, TT], F32)
        gn = AR.alloc([64, 2, 8], F32)
        snk = AR.alloc([64, 8], F32)
        t_n, t_d, t_s, t_o, t_r, t_g = Tok(), Tok(), Tok(), Tok(), Tok(), Tok()
        P.dma("sp", gn[:, 0, :], mon_in[l], writes=[t_g])
        P.dma("sp", gn[:, 1, :], son_in[l], writes=[t_g])
        P.dma("sp", snk[:], sink_in[l], writes=[t_g])
        P.op("act", lambda e: e.activation(snk[:], snk[:], AF.Exp), reads=[t_g], writes=[t_g])
        print("epi arena used", AR.off)
        for ti, (t0, T, is_ctx) in enumerate(c.tiles):
            if is_ctx and last:
                continue
            for which, (SRC, t_src, row0) in enumerate(((OM, t_om, 1024), (OS, t_os, 1536))):
                P.dma("sp", num[:, :, :T], SRC[:, 0:64, t0:t0 + T].rearrange("h d t -> d h t"), reads=[t_src], writes=[t_n])
                for h in range(8):
                    P.dma("sp", den[:, h, :T], SRC[h, 64:65, t0:t0 + T].partition_broadcast(64), reads=[t_src], writes=[t_d])
                if which == 1:
                    P.op("dve", lambda e, T=T: e.tensor_tensor(den[:, :, :T], den[:, :, :T],
                                                               snk[:].unsqueeze(2).to_broadcast([64, 8, T]), ALU.add),
                         reads=[t_g], writes=[t_d])
                P.op("dve", lambda e, T=T: e.reciprocal(den[:, :, :T], den[:, :, :T]), reads=[], writes=[t_d])
                P.op("dve", lambda e, T=T: e.tensor_tensor(num[:, :, :T], num[:, :, :T], den[:, :, :T], ALU.mult),
                     reads=[t_d], writes=[t_n])
                P.op("act", lambda e, T=T: e.activation(sq[:, :, :T], num[:, :, :T], AF.Square), reads=[t_n], writes=[t_s])
                bk, tb = banks[which], t_bank[which]
                mm_group(bk[:, :T], [(ones_bf[0:64, :], sq[:, h, :T]) for h in range(8)], [t_s, t_cst], [tb])
                P.op("act", lambda e, T=T, bk=bk: e.activation(rs[:, :T], bk[:, :T], AF.Sqrt, bias=eps_ap[:, 0:1],
                                                              scale=1.0 / 512), reads=[tb, t_eps], writes=[t_r])
                P.op("dve", lambda e, T=T: e.reciprocal(rs[:, :T], rs[:, :T]), reads=[t_r], writes=[t_r])
                for h in range(8):
                    P.op("dve", lambda e, T=T, h=h, which=which: e.scalar_tensor_tensor(
                        ob[:, h, :T], num[:, h, :T], gn[:, which, h:h + 1], rs[0:64, :T], ALU.mult, ALU.mult),
                        reads=[t_n, t_r, t_g], writes=[t_o])
                P.dma("sp", OX[row0:row0 + 512, t0:t0 + T].rearrange("(h d) t -> d h t", d=64), ob[:, :, :T],
                      reads=[t_o], writes=[t_oxb])
        P.barrier()

    for l in range(L):
        phase1(l)
        mixer_prep(l)
        ssd_scan(l)
        mla_attn(l)
        swa_attn(l)
        attn_epilogue(l)
        for ti in range(NT):
            t_ox[ti].w = t_oxb.w
        phase2(l)

    fin_idx = P.op("sp", None)
    P.ops[fin_idx].deps = set(out_stores)
    P.emit(es)
    return nc, es


def _rope_perm(dim):
    h, q = dim // 2, dim // 4
    perm = np.zeros(dim, np.int64)
    sign = np.zeros(dim, np.float32)
    for i in range(dim):
        base = 0 if i < h else h
        j = i - base
        if j < q:
            perm[i], sign[i] = base + j + q, -1.0
        else:
            perm[i], sign[i] = base + j - q, 1.0
    return perm, sign


def _rope_tables(seq, dim):
    rows = seq // GRID_W
    pos_row = np.repeat(np.arange(rows), GRID_W).astype(np.float32)
    pos_col = np.tile(np.arange(GRID_W), rows).astype(np.float32)
    quarter = dim // 4
    inv_freq = (ROPE_THETA ** (-np.arange(quarter, dtype=np.float32) / quarter)).astype(np.float32)
    ang_r = pos_row[:, None] * inv_freq[None, :]
    ang_c = pos_col[:, None] * inv_freq[None, :]
    ang = np.concatenate([ang_r, ang_r, ang_c, ang_c], axis=-1).astype(np.float32)
    _, sign = _rope_perm(dim)
    return np.stack([np.cos(ang).T, (np.sin(ang) * sign[None, :]).T], axis=0).astype(np.float32)


def _prep_shared(inp, c):
    D, FF, DC, FC, NH, FCH, L = c.D, c.FF, c.DC, c.FC, c.NH, c.FCH, c.L
    f = np.float32
    sh = {}
    wm = np.asarray(inp["w_mod"], f).reshape(L, DC, 128, 9 * DC, 128)
    sh["wmod_t"] = np.ascontiguousarray(wm.transpose(0, 3, 2, 1, 4))
    sh["bmod_t"] = np.ascontiguousarray(np.asarray(inp["b_mod"], f).reshape(L, 9 * DC, 128).transpose(0, 2, 1))
    sh["ng_t"] = np.ascontiguousarray(np.asarray(inp["norm_g"], f).reshape(L, 3, DC, 128).transpose(0, 3, 1, 2))
    wg = np.asarray(inp["ffn_w_gate"], f).reshape(L, 2, DC, 128, FC, 128)
    wu = np.asarray(inp["ffn_w_up"], f).reshape(L, 2, DC, 128, FC, 128)
    wgu = np.stack([wg, wu], axis=0)
    sh["wgu_t"] = np.ascontiguousarray(wgu.transpose(1, 2, 5, 0, 4, 3, 6))
    wd = np.asarray(inp["ffn_w_down"], f).reshape(L, 2, NH, FCH, 128, DC, 128)
    sh["wd_t"] = np.ascontiguousarray(wd.transpose(0, 1, 2, 5, 4, 3, 6))
    sh["fin_t"] = np.ascontiguousarray(np.asarray(inp["final_norm"], f).reshape(DC, 128).T)
    w_in = np.asarray(inp["w_in"], f)
    pM, _ = _rope_perm(32)
    pS, _ = _rope_perm(64)
    cols = -np.ones((NCH_WIN, 128), np.int64)
    j = np.arange(128)
    for ch in range(12):
        cols[ch] = 1024 + 128 * ch + j
    cols[12, :32] = 2560 + np.arange(32)
    for i in range(3):
        cols[13 + i] = 2592 + 128 * i + j
    for i in range(2):
        cols[16 + i] = 2976 + 128 * i + j
    cols[18, 64:96] = 3232 + np.arange(32)
    cols[19, 64:96] = 3232 + pM
    for i in range(4):
        cols[20 + i] = 3264 + 128 * i + j
        cols[24 + i] = 3264 + 128 * i + 64 * (j // 64) + pS[j % 64]
    for g in range(2):
        cols[28 + g] = 3776 + 64 * g + (j % 64)
        cols[30 + g] = 3776 + 64 * g + pS[j % 64]
    for i in range(8):
        cols[32 + i] = 128 * i + j
    cols[40] = 3904 + j
    valid = cols >= 0
    wsel = w_in[:, :, np.where(valid, cols, 0).reshape(-1)].reshape(L, DC, 128, NCH_WIN, 128)
    wsel = wsel * valid[None, None, None, :, :].astype(f)
    sh["win_t"] = np.ascontiguousarray(wsel.transpose(0, 3, 2, 1, 4))
    wo = np.asarray(inp["w_out"], f).reshape(L, KM, 128, DC, 128)
    sh["wout_t"] = np.ascontiguousarray(wo.transpose(0, 3, 2, 1, 4))
    wuq = np.asarray(inp["mla_w_uq"], f).reshape(L, 3, 128, NHM, 96)
    jj = np.arange(96)
    permj = np.where(jj < 64, jj, 64 + pM[np.clip(jj - 64, 0, 31)])
    wuq2 = np.stack([wuq, wuq[..., permj]], axis=1)
    sh["wuq_t"] = np.ascontiguousarray(wuq2.transpose(0, 1, 4, 3, 2, 5))
    wukv = np.asarray(inp["mla_w_ukv"], f).reshape(L, 2, 128, NHM, 128)
    wk = wukv[..., :64].reshape(L, 2, 128, 4, 128)
    sh["wukvk_t"] = np.ascontiguousarray(wk.transpose(0, 3, 2, 1, 4))
    wv = wukv[..., 64:].reshape(L, 2, 128, 512)
    sh["wukvv_t"] = np.ascontiguousarray(wv.transpose(0, 2, 1, 3))
    sh["qn_t"] = np.ascontiguousarray(np.asarray(inp["mla_q_norm"], f).reshape(L, 3, 128).transpose(0, 2, 1))
    sh["kvn_t"] = np.ascontiguousarray(np.asarray(inp["mla_kv_norm"], f).reshape(L, 2, 128).transpose(0, 2, 1))
    cw = np.asarray(inp["ssd_conv_w"], f).reshape(L, 5, 12, 128)
    sh["convw_t"] = np.ascontiguousarray(cw.transpose(0, 3, 2, 1))
    sh["convb_t"] = np.ascontiguousarray(np.asarray(inp["ssd_conv_b"], f).reshape(L, 12, 128).transpose(0, 2, 1))
    dtb = np.asarray(inp["ssd_dt_bias"], f).reshape(L, 32)
    sh["dtb_t"] = np.ascontiguousarray(np.concatenate([dtb, dtb], axis=1)[:, :, None])
    al = np.asarray(inp["ssd_a_log"], f).reshape(L, 32)
    sh["alog_t"] = np.ascontiguousarray(np.concatenate([al, al], axis=1)[:, :, None])
    sh["dsk_t"] = np.ascontiguousarray(np.broadcast_to(np.asarray(inp["ssd_d"], f)[:, None, :], (L, 128, 16)))
    sh["sn_t"] = np.ascontiguousarray(np.broadcast_to(np.asarray(inp["ssd_norm"], f)[:, None, :], (L, 128, 1024)))
    sh["mon_t"] = np.ascontiguousarray(np.asarray(inp["mla_out_norm"], f).reshape(L, 8, 64).transpose(0, 2, 1))
    sh["son_t"] = np.ascontiguousarray(np.asarray(inp["swa_out_norm"], f).reshape(L, 8, 64).transpose(0, 2, 1))
    sh["sink_t"] = np.ascontiguousarray(np.broadcast_to(np.asarray(inp["swa_sink"], f)[:, None, :], (L, 64, 8)))
    sh["ropeM_t"] = _rope_tables(c.SEQ, 32)
    rs = _rope_tables(c.SEQ, 64)
    sh["ropeS_t"] = np.ascontiguousarray(np.concatenate([rs, rs], axis=1))
    ii = np.arange(128)
    J, Lm = ii[:, None], ii[None, :]
    cst = np.stack([J == Lm, J <= Lm, J >= Lm, J > Lm, J < Lm, np.ones((128, 128), bool)], axis=1)
    sh["cst_t"] = np.ascontiguousarray(cst.astype(f))
    return sh


def _prep_core(inp, c, b):
    f = np.float32
    x = np.asarray(inp["x"], f)[b]
    ctx = np.asarray(inp["ctx"], f)[b]
    xT = np.ascontiguousarray(np.concatenate([ctx, x], axis=0).T)
    cc = np.stack([np.asarray(inp["c"], f)[b], np.asarray(inp["c_ctx"], f)], axis=1)
    cT = np.ascontiguousarray(cc.reshape(c.DC, 128, 2).transpose(1, 0, 2))
    return {"xT": xT, "cT": cT}


_CACHE = {}


def kernel(**inputs):
    x = np.asarray(inputs["x"])
    B, SEQ, D = x.shape
    CTX = np.asarray(inputs["ctx"]).shape[1]
    FF = np.asarray(inputs["ffn_w_gate"]).shape[3]
    L = np.asarray(inputs["w_mod"]).shape[0]
    c = make_cfg(D, FF, SEQ, CTX, L)
    key = (D, FF, SEQ, CTX, L)
    if key not in _CACHE:
        _CACHE[key] = build_program(c)
    nc, _es = _CACHE[key]
    sh = _prep_shared(inputs, c)
    in_maps = []
    for b in range(B):
        m = dict(sh)
        m.update(_prep_core(inputs, c, b))
        in_maps.append(m)
    res = run_bass_kernel_spmd(nc, in_maps, core_ids=list(range(B)))
    out = np.stack([np.asarray(res.results[b]["outT"]).T for b in range(B)], axis=0)
    return np.ascontiguousarray(out.astype(np.float32))
```

```python
import numpy as np
import ml_dtypes
from contextlib import ExitStack
import concourse.bass as bass
import concourse.mybir as mybir
from concourse.bass_utils import run_bass_kernel_spmd

F32 = mybir.dt.float32
BF16 = mybir.dt.bfloat16
ALU = mybir.AluOpType
AF = mybir.ActivationFunctionType
AX = mybir.AxisListType

EPS = 1e-6
ROPE_THETA = 10000.0
GRID_W = 64


class Tok:
    __slots__ = ("w", "r", "name")

    def __init__(self, name=""):
        self.w = None
        self.r = []
        self.name = name


class _Op:
    __slots__ = ("eng", "fn", "deps", "dma", "signal", "sem", "val", "pos", "idx", "prev")


ENGS = ("pe", "act", "dve", "pool", "sp")
SEM_LIMIT = 30000
N_DMA_SEMS = 24


class Prog:
    def __init__(self, nc):
        self.nc = nc
        self.ops = []
        self.streams = {e: [] for e in ENGS}
        self.dma_since_barrier = []

    def op(self, eng, fn, reads=(), writes=(), dma=False, nobar=False):
        idx = len(self.ops)
        deps = set()
        for t in reads:
            if t.w is not None:
                deps.add(t.w)
        for t in writes:
            if t.w is not None:
                deps.add(t.w)
            deps.update(t.r)
        for t in reads:
            t.r.append(idx)
        for t in writes:
            t.w = idx
            t.r = []
        deps.discard(idx)
        o = _Op()
        o.eng, o.fn, o.deps, o.dma, o.signal = eng, fn, deps, dma, False
        o.sem = o.val = None
        o.idx = idx
        o.pos = len(self.streams[eng])
        self.streams[eng].append(o)
        self.ops.append(o)
        if dma and not nobar:
            self.dma_since_barrier.append(idx)
        return idx

    def dma(self, eng, out, in_, reads=(), writes=(), nobar=False):
        return self.op(eng, lambda e: e.dma_start(out=out, in_=in_), reads, writes, dma=True, nobar=nobar)

    def barrier(self):
        deps = set(self.dma_since_barrier)
        for e in ENGS:
            for o in reversed(self.streams[e]):
                if o.fn is not None:
                    deps.add(o.idx)
                    break
        self.dma_since_barrier = []
        for e in ENGS:
            idx = self.op(e, None)
            self.ops[idx].deps = set(d for d in deps)

    def _needs_wait(self, o, d):
        if d.dma:
            return True
        if d.eng == o.eng:
            if o.dma:
                return True
            if o.eng == "pe":
                return False
            return d.pos >= o.pos - 3
        return True

    def emit(self, es):
        nc = self.nc
        ops = self.ops
        for o in ops:
            for di in o.deps:
                d = ops[di]
                if self._needs_wait(o, d):
                    d.signal = True
        for o in ops:
            if o.dma:
                o.signal = True
        sems = {}
        n_sem = [0]

        def new_sem(tag):
            n_sem[0] += 1
            return es.enter_context(nc.semaphore(f"s_{tag}_{n_sem[0]}"))

        for e in ENGS:
            cur = None
            cnt = 0
            dsem = []
            dcnt = []
            k = 0
            for o in self.streams[e]:
                if not o.signal:
                    continue
                if o.dma:
                    if len(dsem) < N_DMA_SEMS:
                        dsem.append(new_sem(e + "d"))
                        dcnt.append(0)
                    j = k % N_DMA_SEMS
                    k += 1
                    if dcnt[j] + 16 > SEM_LIMIT:
                        dsem[j] = new_sem(e + "d")
                        dcnt[j] = 0
                    o.prev = dcnt[j]
                    dcnt[j] += 16
                    o.sem, o.val = dsem[j], dcnt[j]
                else:
                    if cur is None or cnt + 1 > SEM_LIMIT:
                        cur = new_sem(e)
                        cnt = 0
                    cnt += 1
                    o.sem, o.val = cur, cnt
        self.n_sems = n_sem[0]

        engobj = {"pe": "tensor", "act": "scalar", "dve": "vector", "pool": "gpsimd", "sp": "sync"}

        def run_stream(ename):
            def body(eng):
                seen = {}
                for o in self.streams[ename]:
                    waits = {}
                    for di in o.deps:
                        d = ops[di]
                        if not self._needs_wait(o, d):
                            continue
                        key = id(d.sem)
                        if seen.get(key, 0) >= d.val:
                            continue
                        if key not in waits or waits[key][1] < d.val:
                            waits[key] = (d.sem, d.val)
                    if o.dma and o.prev > 0 and seen.get(id(o.sem), 0) < o.prev:
                        waits[id(o.sem)] = (o.sem, o.prev)
                    for key, (s, v) in waits.items():
                        eng.wait_ge(s, v)
                        seen[key] = v
                    if o.fn is None:
                        continue
                    ins = o.fn(eng)
                    if o.signal:
                        ins.then_inc(o.sem, 16 if o.dma else 1)
            return body

        with nc.Block() as block:
            for ename in ENGS:
                if not self.streams[ename]:
                    continue
                getattr(block, engobj[ename])(run_stream(ename))


class Cfg:
    pass


def make_cfg(D, FF, SEQ, CTX, DEPTH):
    c = Cfg()
    c.D, c.FF, c.SEQ, c.CTX, c.L = D, FF, SEQ, CTX, DEPTH
    c.DC = D // 128
    c.FC = FF // 128
    c.NH = 2 if c.FC % 2 == 0 else 1
    c.FCH = c.FC // c.NH
    c.NTOK = CTX + SEQ
    c.TT = 512
    tiles = []
    t0 = 0
    while t0 < CTX:
        T = min(c.TT, CTX - t0)
        tiles.append((t0, T, True))
        t0 += T
    while t0 < c.NTOK:
        T = min(c.TT, c.NTOK - t0)
        tiles.append((t0, T, False))
        t0 += T
    c.tiles = tiles
    return c


NCH_WIN = 41
D_MIX = 2048
KM = 16
NHM = 8
NHS = 8
SC_MLA = 96 ** -0.5
SC_SWA = 64 ** -0.5


def _prod(xs):
    r = 1
    for x in xs:
        r *= x
    return r


class Arena:
    def __init__(self, base, nbytes):
        self.base, self.n, self.off = base, nbytes, 0

    def reset(self):
        self.off = 0

    def alloc(self, shape, dt):
        cols = _prod(shape[1:])
        esz = 4 if dt == F32 else 2
        nb = cols * esz
        o = self.off
        self.off += (nb + 63) // 64 * 64
        assert self.off <= self.n, f"arena overflow {self.off} > {self.n}"
        v = self.base[:, o // 2:o // 2 + nb // 2]
        if dt == F32:
            v = v.bitcast(F32)
        if len(shape) == 3:
            v = v.rearrange("p (a b) -> p a b", b=shape[2])
        elif len(shape) == 4:
            v = v.rearrange("p (a b c) -> p a b c", b=shape[2], c=shape[3])
        if shape[0] < 128:
            v = v[0:shape[0]]
        return v


def build_program(c):
    nc = bass.Bass("TRN2", target_bir_lowering=False)
    P = Prog(nc)
    D, FF, DC, FC, NH, FCH, L, NTOK, SEQ, CTX = c.D, c.FF, c.DC, c.FC, c.NH, c.FCH, c.L, c.NTOK, c.SEQ, c.CTX
    NB = NTOK // 128
    NBC = CTX // 128
    TT = c.TT
    WK = max(DC, KM)

    def dram_in(name, shape, dt=F32):
        return nc.dram_tensor(name, list(shape), dt, kind="ExternalInput").ap()

    def dram_scr(name, shape, dt):
        return nc.dram_tensor(name, list(shape), dt, kind="Internal").ap()

    xT_in = dram_in("xT", [D, NTOK])
    cT_in = dram_in("cT", [128, DC, 2])
    wmod_in = dram_in("wmod_t", [L, 9 * DC, 128, DC, 128])
    bmod_in = dram_in("bmod_t", [L, 128, 9 * DC])
    ng_in = dram_in("ng_t", [L, 128, 3, DC])
    wgu_in = dram_in("wgu_t", [L, 2, FC, 2, 128, DC, 128])
    wd_in = dram_in("wd_t", [L, 2, NH, DC, 128, FCH, 128])
    fin_in = dram_in("fin_t", [128, DC])
    win_in = dram_in("win_t", [L, NCH_WIN, 128, DC, 128])
    wout_in = dram_in("wout_t", [L, DC, 128, KM, 128])
    wuq_in = dram_in("wuq_t", [L, 2, NHM, 128, 3, 96])
    wukvk_in = dram_in("wukvk_t", [L, 4, 128, 2, 128])
    wukvv_in = dram_in("wukvv_t", [L, 128, 2, 512])
    qn_in = dram_in("qn_t", [L, 128, 3])
    kvn_in = dram_in("kvn_t", [L, 128, 2])
    convw_in = dram_in("convw_t", [L, 128, 12, 5])
    convb_in = dram_in("convb_t", [L, 128, 12])
    dtb_in = dram_in("dtb_t", [L, 64, 1])
    alog_in = dram_in("alog_t", [L, 64, 1])
    dsk_in = dram_in("dsk_t", [L, 128, 16])
    sn_in = dram_in("sn_t", [L, 128, 1024])
    mon_in = dram_in("mon_t", [L, 64, 8])
    son_in = dram_in("son_t", [L, 64, 8])
    sink_in = dram_in("sink_t", [L, 64, 8])
    ropeM_in = dram_in("ropeM_t", [2, 32, SEQ])
    ropeS_in = dram_in("ropeS_t", [2, 128, SEQ])
    cst_in = dram_in("cst_t", [128, 6, 128])
    out_ap = nc.dram_tensor("outT", [D, SEQ], F32, kind="ExternalOutput").ap()

    hxT = dram_scr("hxT", [D, NTOK], F32)
    wgu_bf = [dram_scr(f"wgu_bf{l}", [2, FC, 2, 128, DC, 128], BF16) for l in range(L)]
    wd_bf = [dram_scr(f"wd_bf{l}", [2, NH, DC, 128, FCH, 128], BF16) for l in range(L)]
    win_bf = [dram_scr(f"win_bf{l}", [NCH_WIN, 128, DC, 128], BF16) for l in range(L)]
    wout_bf = [dram_scr(f"wout_bf{l}", [DC, 128, KM, 128], BF16) for l in range(L)]
    XBC = dram_scr("XBC", [1536, NTOK], BF16)
    DTR = dram_scr("DTR", [32, NTOK], F32)
    QM = dram_scr("QM", [NHM, 96, NTOK], BF16)
    KN = dram_scr("KN", [512, NTOK], BF16)
    KR = dram_scr("KR", [32, NTOK], BF16)
    VM = dram_scr("VM", [NB, 128, NHM, 65], BF16)
    QS = dram_scr("QS", [4, 128, NTOK], BF16)
    KS = dram_scr("KS", [2, 128, NTOK], BF16)
    VS = dram_scr("VS", [NB, 128, 2, 65], BF16)
    SZ = dram_scr("SZ", [NB, 128, 1024], BF16)
    OX = dram_scr("OX", [D_MIX, NTOK], BF16)
    OM = dram_scr("OM", [NHM, 65, NTOK], F32)
    OS = dram_scr("OS", [NHS, 65, NTOK], F32)
    XS = dram_scr("XS", [NB, 128, 1024], BF16)
    BTM = dram_scr("BTM", [NB, 128, 2, 128], BF16)
    BCT = dram_scr("BCT", [NB, 128, 4, 128], BF16)
    DTA = dram_scr("DTA", [NB, 128, 64], F32)
    YF = dram_scr("YF", [NB, 128, 1024], F32)

    es = ExitStack()

    def sb(name, shape, dt):
        return nc.alloc_sbuf_tensor("sb_" + name, list(shape), dt).ap()

    cst = sb("cst", [128, 6, 128], F32)
    IDENT_F, T_F, T_B, U_F, U_B, ONES_F = [cst[:, i, :] for i in range(6)]
    cstb = sb("cstb", [128, 6, 128], BF16)
    IDENT_B, TB_F, TB_B, _ub0, _ub1, ones_bf = [cstb[:, i, :] for i in range(6)]
    t_cst = Tok()
    P.dma("sp", cst[:], cst_in[:], writes=[t_cst])
    P.op("dve", lambda e: e.tensor_copy(cstb[:], cst[:]), reads=[t_cst], writes=[t_cst])
    t_ones = t_cst
    eps_ap = sb("eps_ap", [128, 1], F32)
    one_ap = sb("one_ap", [128, 1], F32)
    t_eps = Tok()
    P.op("pool", lambda e: e.memset(eps_ap[:], EPS), writes=[t_eps])
    P.op("pool", lambda e: e.memset(one_ap[:], 1.0), writes=[t_eps])

    banks = [nc.alloc_psum_tensor(f"bank{i}", [128, 512], F32).ap() for i in range(8)]
    banks_b = [b.bitcast(BF16) for b in banks]
    t_bank = [Tok(f"bank{i}") for i in range(8)]

    def mm_group(out, pairs, reads, writes):
        def f(e):
            ins = None
            n = len(pairs)
            for i, (a, b) in enumerate(pairs):
                ins = e.matmul(out, a, b, start=(i == 0), stop=(i == n - 1))
            return ins
        return P.op("pe", f, reads, writes)

    t_hx_dram = [Tok(f"hxd{i}") for i in range(len(c.tiles))]
    P.dma("sp", hxT[:, :], xT_in[:, :], writes=t_hx_dram)

    cT = sb("cTs", [128, DC, 2], F32)
    csil = sb("csil", [128, DC, 2], BF16)
    t_c = Tok()
    t_csil = Tok()
    P.dma("sp", cT[:], cT_in[:], writes=[t_c])
    P.op("act", lambda e: e.activation(csil[:], cT[:], AF.Silu), reads=[t_c], writes=[t_csil])
    NMC = 9 * DC
    MOD = [sb(f"mod{l}", [128, NMC, 2], F32) for l in range(L)]
    bmod = [sb(f"bmod{l}", [128, NMC], F32) for l in range(L)]
    ngt = [sb(f"ng{l}", [128, 3, DC], F32) for l in range(L)]
    AV = [sb(f"av{l}", [128, 3, DC, 2], F32) for l in range(L)]
    GV = [sb(f"gv{l}", [128, 3, DC, 2], F32) for l in range(L)]
    t_mod = [Tok() for _ in range(L)]
    fin = sb("fin", [128, DC], F32)
    t_fin = Tok()
    P.dma("sp", fin[:], fin_in[:], writes=[t_fin])

    NWM = 3
    wm_buf = [sb(f"wm{i}", [128, DC, 128], BF16) for i in range(NWM)]
    t_wm = [Tok() for _ in range(NWM)]
    ARENA_BYTES = nc.sbuf_bytes_remaining // 128 - 2048 if nc.sbuf_bytes_remaining > 4 * 1024 * 1024 else nc.sbuf_bytes_remaining - 2048
    ARENA_BYTES = (ARENA_BYTES // 64) * 64
    arena_t = sb("arena", [128, ARENA_BYTES // 2], BF16)
    AR = Arena(arena_t, ARENA_BYTES)
    print("arena bytes/partition:", ARENA_BYTES)

    def emit_mod(l):
        t_bm = Tok()
        P.dma("sp", bmod[l][:], bmod_in[l], writes=[t_bm])
        P.dma("sp", ngt[l][:], ng_in[l], writes=[t_bm])
        bank = banks[l % 2]
        tb = t_bank[l % 2]
        for cc in range(NMC):
            i = (l * NMC + cc) % NWM
            P.dma("pool", wm_buf[i][:], wmod_in[l, cc], writes=[t_wm[i]], nobar=True)
            mm_group(bank[:, cc * 2:cc * 2 + 2], [(wm_buf[i][:, kc, :], csil[:, kc, :]) for kc in range(DC)],
                     [t_wm[i], t_csil], [tb])
        P.op("dve", lambda e, l=l, bank=bank: e.tensor_tensor(
            MOD[l][:], bank[:, 0:2 * NMC].rearrange("p (c t) -> p c t", t=2),
            bmod[l][:].unsqueeze(2).to_broadcast([128, NMC, 2]), ALU.add),
            reads=[tb, t_bm], writes=[t_mod[l]])
        for i3 in range(3):
            sc = MOD[l][:, (3 * i3 + 1) * DC:(3 * i3 + 2) * DC, :]
            gt = MOD[l][:, (3 * i3 + 2) * DC:(3 * i3 + 3) * DC, :]
            P.op("dve", lambda e, l=l, i3=i3, sc=sc: e.scalar_tensor_tensor(
                AV[l][:, i3], sc, 1.0, ngt[l][:, i3].unsqueeze(2).to_broadcast([128, DC, 2]),
                ALU.add, ALU.mult), reads=[t_mod[l], t_bm], writes=[t_mod[l]])
            P.op("dve", lambda e, l=l, i3=i3, gt=gt: e.tensor_scalar(
                GV[l][:, i3], gt, 0.5 if i3 != 1 else 1.0, None, ALU.mult),
                reads=[t_mod[l]], writes=[t_mod[l]])


    t_wgu = [[Tok() for _ in range(2)] for _ in range(L)]
    t_wd = [[Tok() for _ in range(2)] for _ in range(L)]
    t_win = [Tok() for _ in range(L)]
    t_wout = [Tok() for _ in range(L)]
    def emit_casts(l):
        for ch0 in range(0, NCH_WIN, 8):
            ch1 = min(NCH_WIN, ch0 + 8)
            P.dma("pool", win_bf[l][ch0:ch1], win_in[l, ch0:ch1], writes=[t_win[l]], nobar=True)
        for i in range(2):
            if i == 1:
                for dc0 in range(0, DC, 4):
                    P.dma("pool", wout_bf[l][dc0:min(DC, dc0 + 4)], wout_in[l, dc0:min(DC, dc0 + 4)], writes=[t_wout[l]], nobar=True)
            for fc in range(FC):
                P.dma("pool", wgu_bf[l][i, fc], wgu_in[l, i, fc], writes=[t_wgu[l][i]], nobar=True)
            for hf in range(NH):
                for dc0 in range(0, DC, 4):
                    P.dma("pool", wd_bf[l][i, hf, dc0:min(DC, dc0 + 4)], wd_in[l, i, hf, dc0:min(DC, dc0 + 4)], writes=[t_wd[l][i]], nobar=True)


    emit_mod(0)
    emit_casts(0)

    hx_view = hxT.rearrange("(c p) t -> p c t", p=128)
    out_view = out_ap.rearrange("(c p) t -> p c t", p=128)
    out_stores = []
    cnt = {}

    def nxt(key, n):
        v = cnt.get(key, 0)
        cnt[key] = v + 1
        return v % n

    def dense_alloc():
        AR.reset()
        B = Cfg()
        B.hx = AR.alloc([128, DC, TT], F32)
        B.u = AR.alloc([128, WK, TT], BF16)
        B.hbuf = AR.alloc([128, max(FCH, DC, 3), TT], BF16)
        B.rstd = AR.alloc([128, TT], F32)
        B.tmp = [AR.alloc([128, TT], F32) for _ in range(2)]
        B.sg = [AR.alloc([128, TT], F32) for _ in range(2)]
        B.NW = 6
        B.w = [AR.alloc([128, WK, 128], BF16) for _ in range(B.NW)]
        B.wd = [AR.alloc([128, FCH, 128], BF16) for _ in range(2)]
        B.t_hx = [Tok() for _ in range(DC)]
        B.t_u = [Tok() for _ in range(WK)]
        B.t_h = [Tok() for _ in range(max(FCH, DC, 3))]
        B.t_rstd = Tok()
        B.t_tmp = [Tok(), Tok()]
        B.t_sg = [Tok(), Tok()]
        B.t_w = [Tok() for _ in range(B.NW)]
        B.t_wd = [Tok(), Tok()]
        return B

    def fm_rstd(B, T, srcs, nfeat, t_src, rstd_out, t_out, nrows=128):
        n = len(srcs)
        for i, s_ap in enumerate(srcs):
            P.op("act", lambda e, i=i, s_ap=s_ap: e.activation(B.hbuf[0:nrows, i, :T], s_ap, AF.Square),
                 reads=t_src, writes=[B.t_h[i]])
        bk, tb = banks[6], t_bank[6]
        mm_group(bk[:, :T], [(ones_bf[0:nrows, :], B.hbuf[0:nrows, i, :T]) for i in range(n)],
                 [B.t_h[i] for i in range(n)] + [t_ones], [tb])
        P.op("act", lambda e: e.activation(rstd_out[:, :T], bk[:, :T], AF.Sqrt, bias=eps_ap[:, 0:1],
                                           scale=1.0 / nfeat), reads=[tb, t_eps], writes=[t_out])
        P.op("dve", lambda e: e.reciprocal(rstd_out[:, :T], rstd_out[:, :T]), reads=[t_out], writes=[t_out])

    def modnorm(B, T, l, avec, bvec, col):
        fm_rstd(B, T, [B.hx[:, dc, :T] for dc in range(DC)], D, B.t_hx, B.rstd, B.t_rstd)
        for dc in range(DC):
            k = nxt("tmp", 2)
            P.op("dve", lambda e, dc=dc, k=k: e.scalar_tensor_tensor(
                B.tmp[k][:, :T], B.hx[:, dc, :T], avec[:, dc, col:col + 1], B.rstd[:, :T], ALU.mult, ALU.mult),
                reads=[B.t_hx[dc], B.t_rstd, t_mod[l]], writes=[B.t_tmp[k]])
            P.op("act", lambda e, dc=dc, k=k: e.activation(
                B.u[:, dc, :T], B.tmp[k][:, :T], AF.Identity, bias=bvec[:, dc, col:col + 1], scale=1.0),
                reads=[B.t_tmp[k], t_mod[l]], writes=[B.t_u[dc]])

    def load_w(B, src, t_src, K=None):
        s = nxt("w", B.NW)
        K = DC if K is None else K
        P.dma("sp", B.w[s][:, :K, :], src, reads=[t_src], writes=[B.t_w[s]])
        return s

    def ffn(B, T, l, i, gvec, col):
        tu_all = B.t_u[:DC]
        for hf in range(NH):
            for f in range(FCH):
                fc = hf * FCH + f
                sgi = load_w(B, wgu_bf[l][i, fc, 0], t_wgu[l][i])
                sui = load_w(B, wgu_bf[l][i, fc, 1], t_wgu[l][i])
                bg, bu_ = banks[(2 * f) % 4], banks[(2 * f + 1) % 4]
                tg, tu = t_bank[(2 * f) % 4], t_bank[(2 * f + 1) % 4]
                mm_group(bg[:, :T], [(B.w[sgi][:, kc, :], B.u[:, kc, :T]) for kc in range(DC)],
                         [B.t_w[sgi]] + tu_all, [tg])
                mm_group(bu_[:, :T], [(B.w[sui][:, kc, :], B.u[:, kc, :T]) for kc in range(DC)],
                         [B.t_w[sui]] + tu_all, [tu])
                k = nxt("sg", 2)
                P.op("act", lambda e, k=k, bg=bg: e.activation(B.sg[k][:, :T], bg[:, :T], AF.Silu),
                     reads=[tg], writes=[B.t_sg[k]])
                P.op("dve", lambda e, k=k, bu_=bu_, f=f: e.tensor_tensor(
                    B.hbuf[:, f, :T], B.sg[k][:, :T], bu_[:, :T], ALU.mult),
                    reads=[B.t_sg[k], tu], writes=[B.t_h[f]])
            for dc in range(DC):
                s = nxt("wd", 2)
                P.dma("sp", B.wd[s][:], wd_bf[l][i, hf, dc], reads=[t_wd[l][i]], writes=[B.t_wd[s]])
                bo, to = banks[4 + dc % 2], t_bank[4 + dc % 2]
                mm_group(bo[:, :T], [(B.wd[s][:, f, :], B.hbuf[:, f, :T]) for f in range(FCH)],
                         [B.t_wd[s]] + B.t_h[:FCH], [to])
                P.op("dve", lambda e, dc=dc, bo=bo: e.scalar_tensor_tensor(
                    B.hx[:, dc, :T], bo[:, :T], gvec[:, dc, col:col + 1], B.hx[:, dc, :T], ALU.mult, ALU.add),
                    reads=[to, t_mod[l]], writes=[B.t_hx[dc]])

    NT = len(c.tiles)
    t_scr = [Tok(f"scr{i}") for i in range(NT)]
    t_ox = [Tok(f"ox{i}") for i in range(NT)]

    def inproj(B, X, T, l, ti, t0, is_ctx):
        tu_all = B.t_u[:DC]
        nblk = T // 128
        ts = t_scr[ti]
        ring = [0, 1, 2, 3, 4, 5]

        def fm_chunk(ch, M=128):
            s = load_w(B, win_bf[l][ch], t_win[l])
            bi = ring[nxt("ipb", 6)]
            mm_group(banks[bi][0:M, :T], [(B.w[s][:, kc, 0:M], B.u[:, kc, :T]) for kc in range(DC)],
                     [B.t_w[s]] + tu_all, [t_bank[bi]])
            return bi

        def stage_bf():
            k = nxt("stb", 3)
            return X.stb[k], X.t_stb[k]

        for ch in range(12):
            bi = fm_chunk(ch)
            st, tst = stage_bf()
            eng = "act" if ch % 2 == 0 else "dve"
            if eng == "act":
                P.op("act", lambda e, bi=bi, st=st: e.activation(st[:, :T], banks[bi][:, :T], AF.Copy),
                     reads=[t_bank[bi]], writes=[tst])
            else:
                P.op("dve", lambda e, bi=bi, st=st: e.tensor_copy(st[:, :T], banks[bi][:, :T]),
                     reads=[t_bank[bi]], writes=[tst])
            P.dma("act", XBC[ch * 128:(ch + 1) * 128, t0:t0 + T], st[:, :T], reads=[tst], writes=[ts])
        bi = fm_chunk(12, 32)
        P.op("act", lambda e, bi=bi: e.activation(X.stf[0:32, :T], banks[bi][0:32, :T], AF.Copy),
             reads=[t_bank[bi]], writes=[X.t_stf])
        P.dma("act", DTR[:, t0:t0 + T], X.stf[0:32, :T], reads=[X.t_stf], writes=[ts])
        if not is_ctx:
            p0 = t0 - CTX
            P.dma("sp", X.ropeM[64:96, 0, :T], ropeM_in[0, :, p0:p0 + T], writes=[X.t_rope])
            P.dma("sp", X.ropeM[64:96, 1, :T], ropeM_in[1, :, p0:p0 + T], writes=[X.t_rope])
            P.dma("sp", X.ropeS[:, 0, :T], ropeS_in[0, :, p0:p0 + T], writes=[X.t_rope])
            P.dma("sp", X.ropeS[:, 1, :T], ropeS_in[1, :, p0:p0 + T], writes=[X.t_rope])

        def rope_evac(ba, bb, r0, r1, table, st):
            k = nxt("tmp", 2)
            k2 = nxt("sg", 2)
            P.op("dve", lambda e: e.tensor_tensor(B.tmp[k][r0:r1, :T], banks[ba][r0:r1, :T], table[r0:r1, 0, :T], ALU.mult),
                 reads=[t_bank[ba], X.t_rope], writes=[B.t_tmp[k]])
            P.op("dve", lambda e: e.tensor_tensor(B.sg[k2][r0:r1, :T], banks[bb][r0:r1, :T], table[r0:r1, 1, :T], ALU.mult),
                 reads=[t_bank[bb], X.t_rope], writes=[B.t_sg[k2]])
            P.op("pool", lambda e: e.tensor_tensor(st[r0:r1, :T], B.tmp[k][r0:r1, :T], B.sg[k2][r0:r1, :T], ALU.add),
                 reads=[B.t_tmp[k], B.t_sg[k2]], writes=[])

        def latent_norm(chs, nfeat, buf, t_buf, gn, outn, t_outn):
            for j, ch in enumerate(chs):
                bi = fm_chunk(ch)
                P.op("act", lambda e, bi=bi, j=j: e.activation(buf[:, j, :T], banks[bi][:, :T], AF.Copy),
                     reads=[t_bank[bi]], writes=[t_buf])
            fm_rstd(B, T, [buf[:, j, :T] for j in range(len(chs))], nfeat, [t_buf], X.rstd2, X.t_rstd2)
            for j in range(len(chs)):
                P.op("dve", lambda e, j=j: e.scalar_tensor_tensor(
                    outn[:, j, :T], buf[:, j, :T], gn[:, j:j + 1], X.rstd2[:, :T], ALU.mult, ALU.mult),
                    reads=[t_buf, X.t_rstd2, X.t_small], writes=[t_outn])

        latent_norm([13, 14, 15], 384, X.cq, X.t_cq, X.qn, X.cqn, X.t_cqn)
        for h in range(NHM):
            ba = ring[nxt("ipb", 6)]
            mm_group(banks[ba][0:96, :T], [(X.wuq[:, 0, h, kc, :], X.cqn[:, kc, :T]) for kc in range(3)],
                     [X.t_cqn, X.t_small], [t_bank[ba]])
            st, tst = stage_bf()
            if is_ctx:
                P.op("act", lambda e, ba=ba, st=st: e.activation(st[0:96, :T], banks[ba][0:96, :T], AF.Copy),
                     reads=[t_bank[ba]], writes=[tst])
            else:
                bb = ring[nxt("ipb", 6)]
                mm_group(banks[bb][0:96, :T], [(X.wuq[:, 1, h, kc, :], X.cqn[:, kc, :T]) for kc in range(3)],
                         [X.t_cqn, X.t_small], [t_bank[bb]])
                P.op("act", lambda e, ba=ba, st=st: e.activation(st[0:64, :T], banks[ba][0:64, :T], AF.Copy),
                     reads=[t_bank[ba]], writes=[tst])
                k = nxt("tmp", 2)
                k2 = nxt("sg", 2)
                P.op("dve", lambda e, ba=ba, k=k: e.tensor_tensor(B.tmp[k][64:96, :T], banks[ba][64:96, :T],
                                                                  X.ropeM[64:96, 0, :T], ALU.mult),
                     reads=[t_bank[ba], X.t_rope], writes=[B.t_tmp[k]])
                P.op("dve", lambda e, bb=bb, k2=k2: e.tensor_tensor(B.sg[k2][64:96, :T], banks[bb][64:96, :T],
                                                                    X.ropeM[64:96, 1, :T], ALU.mult),
                     reads=[t_bank[bb], X.t_rope], writes=[B.t_sg[k2]])
                P.op("pool", lambda e, k=k, k2=k2, st=st: e.tensor_tensor(st[64:96, :T], B.tmp[k][64:96, :T],
                                                                           B.sg[k2][64:96, :T], ALU.add),
                     reads=[B.t_tmp[k], B.t_sg[k2]], writes=[tst])
            P.dma("act", QM[h, :, t0:t0 + T], st[0:96, :T], reads=[tst], writes=[ts])
        latent_norm([16, 17], 256, X.ckv, X.t_ckv, X.kvn, X.ckvn, X.t_ckvn)
        for j in range(4):
            ba = ring[nxt("ipb", 6)]
            mm_group(banks[ba][:, :T], [(X.wukvk[:, j, kc, :], X.ckvn[:, kc, :T]) for kc in range(2)],
                     [X.t_ckvn, X.t_small], [t_bank[ba]])
            st, tst = stage_bf()
            P.op("act", lambda e, ba=ba, st=st: e.activation(st[:, :T], banks[ba][:, :T], AF.Copy),
                 reads=[t_bank[ba]], writes=[tst])
            P.dma("act", KN[j * 128:(j + 1) * 128, t0:t0 + T], st[:, :T], reads=[tst], writes=[ts])
        for bl in range(nblk):
            ba = ring[nxt("ipb", 6)]
            mm_group(banks[ba][:, :512], [(X.ckvn[:, kc, bl * 128:(bl + 1) * 128], X.wukvv[:, kc, :]) for kc in range(2)],
                     [X.t_ckvn, X.t_small], [t_bank[ba]])
            k = nxt("vst", 2)
            P.op("act", lambda e, ba=ba, k=k: e.activation(
                X.vst[k][:, :, 0:64], banks[ba][:, :512].rearrange("p (h d) -> p h d", d=64), AF.Copy),
                reads=[t_bank[ba]], writes=[X.t_vst[k]])
            P.dma("act", VM[(t0 // 128) + bl], X.vst[k][:], reads=[X.t_vst[k]], writes=[ts])
        ba = fm_chunk(18, 96)
        st, tst = stage_bf()
        if is_ctx:
            P.op("act", lambda e, ba=ba, st=st: e.activation(st[64:96, :T], banks[ba][64:96, :T], AF.Copy),
                 reads=[t_bank[ba]], writes=[tst])
        else:
            bb = fm_chunk(19, 96)
            k = nxt("tmp", 2)
            k2 = nxt("sg", 2)
            P.op("dve", lambda e, ba=ba, k=k: e.tensor_tensor(B.tmp[k][64:96, :T], banks[ba][64:96, :T],
                                                              X.ropeM[64:96, 0, :T], ALU.mult),
                 reads=[t_bank[ba], X.t_rope], writes=[B.t_tmp[k]])
            P.op("dve", lambda e, bb=bb, k2=k2: e.tensor_tensor(B.sg[k2][64:96, :T], banks[bb][64:96, :T],
                                                                X.ropeM[64:96, 1, :T], ALU.mult),
                 reads=[t_bank[bb], X.t_rope], writes=[B.t_sg[k2]])
            P.op("pool", lambda e, k=k, k2=k2, st=st: e.tensor_tensor(st[64:96, :T], B.tmp[k][64:96, :T],
                                                                       B.sg[k2][64:96, :T], ALU.add),
                 reads=[B.t_tmp[k], B.t_sg[k2]], writes=[tst])
        P.dma("act", KR[:, t0:t0 + T], st[64:96, :T], reads=[tst], writes=[ts])
        for j in range(6):
            ch_a = 20 + j if j < 4 else 28 + (j - 4)
            ch_b = 24 + j if j < 4 else 30 + (j - 4)
            dst = QS[j] if j < 4 else KS[j - 4]
            ba = fm_chunk(ch_a)
            st, tst = stage_bf()
            if is_ctx:
                P.op("act", lambda e, ba=ba, st=st: e.activation(st[:, :T], banks[ba][:, :T], AF.Copy),
                     reads=[t_bank[ba]], writes=[tst])
            else:
                bb = fm_chunk(ch_b)
                k = nxt("tmp", 2)
                k2 = nxt("sg", 2)
                P.op("dve", lambda e, ba=ba, k=k: e.tensor_tensor(B.tmp[k][:, :T], banks[ba][:, :T],
                                                                  X.ropeS[:, 0, :T], ALU.mult),
                     reads=[t_bank[ba], X.t_rope], writes=[B.t_tmp[k]])
                P.op("dve", lambda e, bb=bb, k2=k2: e.tensor_tensor(B.sg[k2][:, :T], banks[bb][:, :T],
                                                                    X.ropeS[:, 1, :T], ALU.mult),
                     reads=[t_bank[bb], X.t_rope], writes=[B.t_sg[k2]])
                P.op("pool", lambda e, k=k, k2=k2, st=st: e.tensor_tensor(st[:, :T], B.tmp[k][:, :T],
                                                                           B.sg[k2][:, :T], ALU.add),
                     reads=[B.t_tmp[k], B.t_sg[k2]], writes=[tst])
            P.dma("act", dst[:, t0:t0 + T], st[:, :T], reads=[tst], writes=[ts])
        s_vs = load_w(B, win_bf[l][40], t_win[l])
        for bl in range(nblk):
            ba = ring[nxt("ipb", 6)]
            mm_group(banks[ba][:, :128], [(B.u[:, kc, bl * 128:(bl + 1) * 128], B.w[s_vs][:, kc, :]) for kc in range(DC)],
                     [B.t_w[s_vs]] + tu_all, [t_bank[ba]])
            k = nxt("vsst", 2)
            P.op("act", lambda e, ba=ba, k=k: e.activation(
                X.vsst[k][:, :, 0:64], banks[ba][:, :128].rearrange("p (g d) -> p g d", d=64), AF.Copy),
                reads=[t_bank[ba]], writes=[X.t_vsst[k]])
            P.dma("act", VS[(t0 // 128) + bl], X.vsst[k][:], reads=[X.t_vsst[k]], writes=[ts])
        for half in range(2):
            sz = [load_w(B, win_bf[l][32 + half * 4 + j], t_win[l]) for j in range(4)]
            for bl in range(nblk):
                ba = ring[nxt("ipb", 6)]
                for j in range(4):
                    mm_group(banks[ba][:, j * 128:(j + 1) * 128],
                             [(B.u[:, kc, bl * 128:(bl + 1) * 128], B.w[sz[j]][:, kc, :]) for kc in range(DC)],
                             [B.t_w[sz[j]]] + tu_all, [t_bank[ba]])
                st, tst = stage_bf()
                P.op("act", lambda e, ba=ba, st=st: e.activation(st[:, :512], banks[ba][:, :512], AF.Silu),
                     reads=[t_bank[ba]], writes=[tst])
                P.dma("act", SZ[(t0 // 128) + bl, :, half * 512:(half + 1) * 512], st[:, :512], reads=[tst], writes=[ts])

    def phase1(l):
        B = dense_alloc()
        X = Cfg()
        X.stb = [AR.alloc([128, TT], BF16) for _ in range(3)]
        X.t_stb = [Tok() for _ in range(3)]
        X.stf = AR.alloc([128, TT], F32)
        X.t_stf = Tok()
        X.ropeM = AR.alloc([128, 2, TT], F32)
        X.ropeS = AR.alloc([128, 2, TT], F32)
        X.t_rope = Tok()
        X.cq = AR.alloc([128, 3, TT], F32)
        X.t_cq = Tok()
        X.cqn = AR.alloc([128, 3, TT], BF16)
        X.t_cqn = Tok()
        X.ckv = AR.alloc([128, 2, TT], F32)
        X.t_ckv = Tok()
        X.ckvn = AR.alloc([128, 2, TT], BF16)
        X.t_ckvn = Tok()
        X.rstd2 = AR.alloc([128, TT], F32)
        X.t_rstd2 = Tok()
        X.vst = [AR.alloc([128, NHM, 65], BF16) for _ in range(2)]
        X.t_vst = [Tok(), Tok()]
        X.vsst = [AR.alloc([128, 2, 65], BF16) for _ in range(2)]
        X.t_vsst = [Tok(), Tok()]
        X.wuq = AR.alloc([128, 2, NHM, 3, 96], BF16) if False else None
        X.wuq = AR.alloc([128, 2 * NHM * 3 * 96], BF16).rearrange("p (v h k j) -> p v h k j", v=2, h=NHM, k=3)
        X.wukvk = AR.alloc([128, 4, 2, 128], BF16)
        X.wukvv = AR.alloc([128, 2, 512], BF16)
        X.qn = AR.alloc([128, 3], F32)
        X.kvn = AR.alloc([128, 2], F32)
        X.t_small = Tok()
        for v in range(2):
            P.dma("pool", X.wuq[:, v], wuq_in[l, v].rearrange("h p k j -> p h k j"), writes=[X.t_small])
        P.dma("pool", X.wukvk[:], wukvk_in[l].rearrange("c p k j -> p c k j"), writes=[X.t_small])
        P.dma("pool", X.wukvv[:], wukvv_in[l], writes=[X.t_small])
        P.dma("sp", X.qn[:], qn_in[l], writes=[X.t_small])
        P.dma("sp", X.kvn[:], kvn_in[l], writes=[X.t_small])
        for k in range(2):
            P.op("pool", lambda e, k=k: e.memset(X.vst[k][:], 1.0), writes=[X.t_vst[k]])
            P.op("pool", lambda e, k=k: e.memset(X.vsst[k][:], 1.0), writes=[X.t_vsst[k]])
        print("phase1 arena used", AR.off)
        for ti, (t0, T, is_ctx) in enumerate(c.tiles):
            col = 1 if is_ctx else 0
            P.dma("sp", B.hx[:, :, :T], hx_view[:, :, t0:t0 + T], reads=[t_hx_dram[ti]], writes=B.t_hx)
            modnorm(B, T, l, AV[l][:, 0], MOD[l][:, 0 * DC:1 * DC, :], col)
            ffn(B, T, l, 0, GV[l][:, 0], col)
            P.dma("act", hx_view[:, :, t0:t0 + T], B.hx[:, :, :T], reads=B.t_hx, writes=[t_hx_dram[ti]])
            modnorm(B, T, l, AV[l][:, 1], MOD[l][:, 3 * DC:4 * DC, :], col)
            inproj(B, X, T, l, ti, t0, is_ctx)
        P.barrier()

    def phase2(l):
        last = (l == L - 1)
        B = dense_alloc()
        print("phase2 arena used", AR.off)
        for ti, (t0, T, is_ctx) in enumerate(c.tiles):
            if last and is_ctx:
                continue
            col = 1 if is_ctx else 0
            P.dma("sp", B.hx[:, :, :T], hx_view[:, :, t0:t0 + T], reads=[t_hx_dram[ti]], writes=B.t_hx)
            P.dma("sp", B.u[:, :KM, :T], OX.rearrange("(c p) t -> p c t", p=128)[:, :, t0:t0 + T],
                  reads=[t_ox[ti]], writes=B.t_u[:KM])
            for dc in range(DC):
                s = load_w(B, wout_bf[l][dc], t_wout[l], K=KM)
                bo, to = banks[4 + dc % 2], t_bank[4 + dc % 2]
                mm_group(bo[:, :T], [(B.w[s][:, kc, :], B.u[:, kc, :T]) for kc in range(KM)],
                         [B.t_w[s]] + B.t_u[:KM], [to])
                P.op("dve", lambda e, dc=dc, bo=bo, T=T, col=col: e.scalar_tensor_tensor(
                    B.hx[:, dc, :T], bo[:, :T], GV[l][:, 1, dc, col:col + 1], B.hx[:, dc, :T], ALU.mult, ALU.add),
                    reads=[to, t_mod[l]], writes=[B.t_hx[dc]])
            modnorm(B, T, l, AV[l][:, 2], MOD[l][:, 6 * DC:7 * DC, :], col)
            ffn(B, T, l, 1, GV[l][:, 2], col)
            if not last:
                P.dma("act", hx_view[:, :, t0:t0 + T], B.hx[:, :, :T], reads=B.t_hx, writes=[t_hx_dram[ti]])
            else:
                fm_rstd(B, T, [B.hx[:, dc, :T] for dc in range(DC)], D, B.t_hx, B.rstd, B.t_rstd)
                for dc in range(DC):
                    P.op("dve", lambda e, dc=dc, T=T: e.scalar_tensor_tensor(
                        B.hx[:, dc, :T], B.hx[:, dc, :T], fin[:, dc:dc + 1], B.rstd[:, :T], ALU.mult, ALU.mult),
                        reads=[B.t_hx[dc], B.t_rstd, t_fin], writes=[B.t_hx[dc]])
                out_stores.append(P.dma("act", out_view[:, :, t0 - CTX:t0 - CTX + T], B.hx[:, :, :T],
                                        reads=B.t_hx, writes=[Tok()]))
        P.barrier()

    t_prep = [Tok(f"prep{b}") for b in range(NB)]
    t_yf = [Tok(f"yf{b}") for b in range(NB)]
    t_oxb = Tok("oxall")
    t_om = Tok("om")
    t_os = Tok("os")
    OXv = OX.rearrange("(c p) t -> p c t", p=128)

    def all_scr():
        return t_scr

    def mixer_prep(l):
        AR.reset()
        xin = [AR.alloc([128, 12, TT + 4], BF16) for _ in range(2)]
        t_xin = [Tok(), Tok()]
        acc = [AR.alloc([128, TT], F32) for _ in range(3)]
        t_acc = [Tok() for _ in range(3)]
        xc = AR.alloc([128, 12, TT], BF16)
        t_xc = [Tok() for _ in range(12)]
        xst = [AR.alloc([128, 1024], BF16) for _ in range(2)]
        t_xst = [Tok(), Tok()]
        bst = [AR.alloc([128, 2, 128], BF16) for _ in range(2)]
        t_bst = [Tok(), Tok()]
        dtin = AR.alloc([64, TT], F32)
        dte = AR.alloc([64, TT], F32)
        t_dt = Tok()
        dst = [AR.alloc([128, 64], F32) for _ in range(2)]
        t_dst = [Tok(), Tok()]
        cw = AR.alloc([128, 12, 5], F32)
        cb = AR.alloc([128, 12], F32)
        dtb = AR.alloc([64, 1], F32)
        aneg = AR.alloc([64, 1], F32)
        t_cw = Tok()
        P.dma("sp", cw[:], convw_in[l], writes=[t_cw])
        P.dma("sp", cb[:], convb_in[l], writes=[t_cw])
        P.dma("sp", dtb[:], dtb_in[l], writes=[t_cw])
        P.dma("sp", aneg[:], alog_in[l], writes=[t_cw])
        P.op("act", lambda e: e.activation(aneg[:], aneg[:], AF.Exp), reads=[t_cw], writes=[t_cw])
        P.op("dve", lambda e: e.tensor_scalar(aneg[:], aneg[:], -1.0, None, ALU.mult), reads=[t_cw], writes=[t_cw])
        print("prep arena used", AR.off)
        XBCv = XBC.rearrange("(c p) t -> p c t", p=128)
        for ti, (t0, T, is_ctx) in enumerate(c.tiles):
            seq0, seq1 = (0, CTX) if is_ctx else (CTX, NTOK)
            k = nxt("xin", 2)
            lo = max(seq0, t0 - 2)
            hi = min(seq1, t0 + T + 2)
            rd = [t_scr[ti]]
            if ti > 0:
                rd.append(t_scr[ti - 1])
            if ti + 1 < NT:
                rd.append(t_scr[ti + 1])
            if lo > t0 - 2:
                P.op("pool", lambda e, k=k: e.memset(xin[k][:, :, 0:2], 0.0), writes=[t_xin[k]])
            if hi < t0 + T + 2:
                P.op("pool", lambda e, k=k, T=T: e.memset(xin[k][:, :, T + 2:T + 4], 0.0), writes=[t_xin[k]])
            P.dma("sp", xin[k][:, :, lo - (t0 - 2):hi - (t0 - 2)], XBCv[:, :, lo:hi], reads=rd, writes=[t_xin[k]])
            for cc in range(12):
                a = nxt("acc", 3)
                P.op("act", lambda e, cc=cc, a=a, k=k, T=T: e.activation(
                    acc[a][:, :T], xin[k][:, cc, 0:T], AF.Identity, bias=cb[:, cc:cc + 1], scale=cw[:, cc, 0:1]),
                    reads=[t_xin[k], t_cw], writes=[t_acc[a]])
                for kk in range(1, 5):
                    eng = "dve"
                    P.op(eng, lambda e, cc=cc, a=a, k=k, kk=kk, T=T: e.scalar_tensor_tensor(
                        acc[a][:, :T], xin[k][:, cc, kk:kk + T], cw[:, cc, kk:kk + 1], acc[a][:, :T], ALU.mult, ALU.add),
                        reads=[t_xin[k], t_cw], writes=[t_acc[a]])
                P.op("act", lambda e, cc=cc, a=a, T=T: e.activation(xc[:, cc, :T], acc[a][:, :T], AF.Silu),
                     reads=[t_acc[a]], writes=[t_xc[cc]])
            P.dma("sp", dtin[0:32, :T], DTR[:, t0:t0 + T], reads=[t_scr[ti]], writes=[t_dt])
            P.dma("sp", dtin[32:64, :T], DTR[:, t0:t0 + T], reads=[t_scr[ti]], writes=[t_dt])
            P.op("act", lambda e, T=T: e.activation(dte[:, :T], dtin[:, :T], AF.Exp, bias=dtb[:, 0:1], scale=1.0),
                 reads=[t_dt, t_cw], writes=[t_dt])
            P.op("act", lambda e, T=T: e.activation(dtin[:, :T], dte[:, :T], AF.Ln, bias=one_ap[0:64, 0:1], scale=1.0),
                 reads=[t_dt, t_eps], writes=[t_dt])
            P.op("dve", lambda e, T=T: e.tensor_scalar(dtin[32:64, :T], dtin[32:64, :T], aneg[32:64, 0:1], None, ALU.mult),
                 reads=[t_dt, t_cw], writes=[t_dt])
            for bl in range(T // 128):
                blk = t0 // 128 + bl
                cs = slice(bl * 128, (bl + 1) * 128)
                bi = nxt("tpb", 4)
                P.op("pe", lambda e, bi=bi, cs=cs: [e.transpose(banks_b[bi][:, j * 128:(j + 1) * 128], xc[:, j, cs], IDENT_B)
                                                    for j in range(8)][-1],
                     reads=t_xc[0:8] + [t_cst], writes=[t_bank[bi]])
                ks = nxt("xst", 2)
                P.op("act", lambda e, bi=bi, ks=ks: e.activation(xst[ks][:], banks_b[bi][:, :1024], AF.Copy),
                     reads=[t_bank[bi]], writes=[t_xst[ks]])
                P.dma("act", XS[blk], xst[ks][:], reads=[t_xst[ks]], writes=[t_prep[blk]])
                bi = nxt("tpb", 4)
                P.op("pe", lambda e, bi=bi, cs=cs: [e.transpose(banks_b[bi][:, j * 128:(j + 1) * 128], xc[:, 8 + j, cs], IDENT_B)
                                                    for j in range(2)][-1],
                     reads=t_xc[8:10] + [t_cst], writes=[t_bank[bi]])
                kb = nxt("bst", 2)
                P.op("dve", lambda e, bi=bi, kb=kb: e.tensor_copy(
                    bst[kb][:], banks_b[bi][:, :256].rearrange("p (g n) -> p g n", n=128)),
                    reads=[t_bank[bi]], writes=[t_bst[kb]])
                P.dma("act", BTM[blk], bst[kb][:], reads=[t_bst[kb]], writes=[t_prep[blk]])
                P.dma("act", BCT[blk], xc[:, 8:12, cs], reads=t_xc[8:12], writes=[t_prep[blk]])
                bi = 4 + nxt("tpf", 2)
                P.op("pe", lambda e, bi=bi, cs=cs: e.transpose(banks[bi][:, 0:64], dtin[:, cs], IDENT_F[0:64, 0:64]),
                     reads=[t_dt, t_cst], writes=[t_bank[bi]])
                kd = nxt("dst", 2)
                P.op("dve", lambda e, bi=bi, kd=kd: e.tensor_copy(dst[kd][:], banks[bi][:, 0:64]),
                     reads=[t_bank[bi]], writes=[t_dst[kd]])
                P.dma("act", DTA[blk], dst[kd][:], reads=[t_dst[kd]], writes=[t_prep[blk]])
        P.barrier()

    def ssd_scan(l):
        last = (l == L - 1)
        AR.reset()
        NBUF = 2
        xs = [AR.alloc([128, 16, 64], BF16) for _ in range(NBUF)]
        btm = [AR.alloc([128, 2, 128], BF16) for _ in range(NBUF)]
        bct = [AR.alloc([128, 4, 128], BF16) for _ in range(NBUF)]
        dta = [AR.alloc([128, 64], F32) for _ in range(NBUF)]
        t_ld = [Tok() for _ in range(NBUF)]
        H = AR.alloc([128, 2, 512], F32)
        Hb = AR.alloc([128, 2, 512], BF16)
        t_H = Tok()
        t_Hb = Tok()
        acs = AR.alloc([128, 16], F32)
        wv = AR.alloc([128, 16], F32)
        dif = AR.alloc([128, 16], F32)
        fv = AR.alloc([128, 16], F32)
        cdv = AR.alloc([128, 16], F32)
        t_sm = Tok()
        xdt = AR.alloc([128, 16, 64], BF16)
        xdte = AR.alloc([128, 16, 64], BF16)
        t_xdt = Tok()
        t_xdte = Tok()
        cbm = AR.alloc([128, 2, 128], F32)
        t_cbm = Tok()
        NR = 4
        lh = [AR.alloc([128, 128], F32) for _ in range(NR)]
        t_lh = [Tok() for _ in range(NR)]
        eh = [AR.alloc([128, 128], F32) for _ in range(NR)]
        t_eh = [Tok() for _ in range(NR)]
        mh = [AR.alloc([128, 128], BF16) for _ in range(NR)]
        t_mh = [Tok() for _ in range(NR)]
        yo = AR.alloc([128, 16, 64], F32)
        t_yo = Tok()
        yst = [AR.alloc([128, 1024], F32) for _ in range(2)]
        t_yst = [Tok(), Tok()]
        yfl = AR.alloc([128, 1024], F32)
        t_yfl = Tok()
        szl = AR.alloc([128, 1024], BF16)
        t_szl = Tok()
        ysq = AR.alloc([128, 1024], F32)
        ms = AR.alloc([128, 2], F32)
        t_ms = Tok()
        ybf = AR.alloc([128, 1024], BF16)
        t_ybf = Tok()
        oxs = [AR.alloc([128, 8, 128], BF16) for _ in range(2)]
        t_oxs = [Tok(), Tok()]
        dsk = AR.alloc([128, 16], F32)
        sng = AR.alloc([128, 1024], F32)
        t_cn = Tok()
        P.dma("sp", dsk[:], dsk_in[l], writes=[t_cn])
        P.dma("sp", sng[:], sn_in[l], writes=[t_cn])
        print("ssd arena used", AR.off)
        for d in range(2):
            T_d = T_F if d == 0 else T_B
            U_d = U_F if d == 0 else U_B
            TBm = T_F if d == 0 else T_B
            ctx_order = list(range(NBC)) if d == 0 else list(range(NBC - 1, -1, -1))
            lat_order = list(range(NBC, NB)) if d == 0 else list(range(NB - 1, NBC - 1, -1))
            P.op("pool", lambda e: e.memset(H[:], 0.0), writes=[t_H])
            P.op("pool", lambda e: e.memset(Hb[:], 0.0), writes=[t_Hb])
            for blk in ctx_order + lat_order:
                is_ctx = blk < NBC
                need_y = not (last and is_ctx)
                k = nxt("sld", NBUF)
                P.dma("sp", xs[k][:], XS[blk].rearrange("p (h d) -> p h d", d=64), reads=[t_prep[blk]], writes=[t_ld[k]])
                P.dma("sp", btm[k][:], BTM[blk], reads=[t_prep[blk]], writes=[t_ld[k]])
                P.dma("sp", bct[k][:], BCT[blk], reads=[t_prep[blk]], writes=[t_ld[k]])
                P.dma("sp", dta[k][:], DTA[blk], reads=[t_prep[blk]], writes=[t_ld[k]])
                dtv = dta[k][:, d * 16:(d + 1) * 16]
                av = dta[k][:, 32 + d * 16:32 + (d + 1) * 16]
                b7, t7 = banks[7], t_bank[7]
                mm_group(b7[:, 0:16], [(T_d, av)], [t_cst, t_ld[k]], [t7])
                mm_group(b7[:, 16:32], [(ONES_F, av)], [t_cst, t_ld[k]], [t7])
                P.op("act", lambda e, b7=b7: e.activation(acs[:], b7[:, 0:16], AF.Copy), reads=[t7], writes=[t_sm])
                P.op("act", lambda e, b7=b7: e.activation(wv[:], b7[:, 0:16], AF.Exp), reads=[t7], writes=[t_sm])
                P.op("act", lambda e, b7=b7: e.activation(cdv[:], b7[:, 16:32], AF.Exp), reads=[t7], writes=[t_sm])
                P.op("dve", lambda e, b7=b7: e.tensor_tensor(dif[:], b7[:, 16:32], acs[:], ALU.subtract),
                     reads=[t7, t_sm], writes=[t_sm])
                P.op("act", lambda e: e.activation(dif[:], dif[:], AF.Exp), reads=[t_sm], writes=[t_sm])
                P.op("dve", lambda e, dtv=dtv: e.tensor_tensor(fv[:], dif[:], dtv, ALU.mult),
                     reads=[t_sm, t_ld[k]], writes=[t_sm])
                P.op("dve", lambda e, k=k, dtv=dtv: e.tensor_tensor(
                    xdt[:], xs[k][:], dtv.unsqueeze(2).to_broadcast([128, 16, 64]), ALU.mult),
                    reads=[t_ld[k]], writes=[t_xdt])
                P.op("pool", lambda e, k=k: e.tensor_tensor(
                    xdte[:], xs[k][:], fv[:].unsqueeze(2).to_broadcast([128, 16, 64]), ALU.mult),
                    reads=[t_ld[k], t_sm], writes=[t_xdte])
                for g in range(2):
                    if need_y:
                        mm_group(banks[6][:, g * 128:(g + 1) * 128], [(bct[k][:, g, :], bct[k][:, 2 + g, :])],
                                 [t_ld[k]], [t_bank[6]])
                        mm_group(banks[4 + g][:, :512], [(bct[k][:, 2 + g, :], Hb[:, g, :])], [t_ld[k], t_Hb],
                                 [t_bank[4 + g]])
                if need_y:
                    P.op("dve", lambda e, TBm=TBm: e.tensor_tensor(
                        cbm[:], banks[6][:, 0:256].rearrange("p (g n) -> p g n", n=128),
                        TBm.unsqueeze(1).to_broadcast([128, 2, 128]), ALU.mult),
                        reads=[t_bank[6], t_cst], writes=[t_cbm])
                    for g in range(2):
                        P.op("dve", lambda e, g=g: e.tensor_tensor(
                            yo[:, g * 8:(g + 1) * 8, :], banks[4 + g][:, :512].rearrange("p (h d) -> p h d", d=64),
                            wv[:, g * 8:(g + 1) * 8].unsqueeze(2).to_broadcast([128, 8, 64]), ALU.mult),
                            reads=[t_bank[4 + g], t_sm], writes=[t_yo])
                for g in range(2):
                    mm_group(banks[6][:, :512], [(btm[k][:, g, :], xdte[:, g * 8:(g + 1) * 8, :].rearrange("p h d -> p (h d)"))],
                             [t_ld[k], t_xdte], [t_bank[6]])
                    P.op("pool", lambda e, g=g: e.tensor_tensor(
                        H[:, g, :].rearrange("p (h d) -> p h d", d=64), H[:, g, :].rearrange("p (h d) -> p h d", d=64),
                        cdv[:, g * 8:(g + 1) * 8].unsqueeze(2).to_broadcast([128, 8, 64]), ALU.mult),
                        reads=[t_sm], writes=[t_H])
                    P.op("dve", lambda e, g=g: e.tensor_tensor(H[:, g, :], H[:, g, :], banks[6][:, :512], ALU.add),
                         reads=[t_bank[6]], writes=[t_H])
                    P.op("act", lambda e, g=g: e.activation(Hb[:, g, :], H[:, g, :], AF.Copy), reads=[t_H], writes=[t_Hb])
                if not need_y:
                    continue
                ky = nxt("yst", 2)
                Y, tY = yst[ky], t_yst[ky]
                rs_ = {}

                def stage_a(h, k=k, av=av, U_d=U_d, T_d=T_d):
                    g = h // 8
                    r = nxt("lh", NR)
                    rs_[h] = r
                    P.op("dve" if h % 2 == 0 else "pool", lambda e, r=r, h=h, av=av, U_d=U_d: e.tensor_scalar(
                        lh[r][:], U_d, av[:, h:h + 1], None, ALU.mult), reads=[t_cst, t_ld[k]], writes=[t_lh[r]])
                    sl = nxt("dsl", 2)
                    dps = banks[sl][:, 0:128]
                    mm_group(dps, [(lh[r][:], T_d)], [t_lh[r], t_cst], [t_bank[sl]])
                    P.op("act", lambda e, r=r, dps=dps: e.activation(eh[r][:], dps, AF.Exp),
                         reads=[t_bank[sl]], writes=[t_eh[r]])
                    P.op("dve", lambda e, r=r, g=g: e.tensor_tensor(mh[r][:], eh[r][:], cbm[:, g, :], ALU.mult),
                         reads=[t_eh[r], t_cbm], writes=[t_mh[r]])

                def stage_b(h, Y=Y, tY=tY):
                    r = rs_[h]
                    ys = 2 + nxt("ysl", 2)
                    yps = banks[ys][:, 0:64]
                    mm_group(yps, [(mh[r][:], xdt[:, h, :])], [t_mh[r], t_xdt], [t_bank[ys]])
                    P.op("dve", lambda e, h=h, yps=yps, Y=Y: e.tensor_tensor(
                        Y[:, h * 64:(h + 1) * 64], yps, yo[:, h, :], ALU.add),
                        reads=[t_bank[ys], t_yo], writes=[tY])
                stage_a(0)
                stage_a(1)
                for h in range(16):
                    stage_b(h)
                    if h + 2 < 16:
                        stage_a(h + 2)
                if d == 0:
                    P.dma("act", YF[blk], Y[:], reads=[tY], writes=[t_yf[blk]])
                    continue
                P.dma("sp", yfl[:], YF[blk], reads=[t_yf[blk]], writes=[t_yfl])
                P.dma("sp", szl[:], SZ[blk], reads=t_scr, writes=[t_szl])
                P.op("dve", lambda e, Y=Y: e.tensor_tensor(Y[:], Y[:], yfl[:], ALU.add), reads=[t_yfl], writes=[tY])
                P.op("pool", lambda e, k=k: e.tensor_tensor(
                    ysq[:].rearrange("p (h d) -> p h d", d=64), xs[k][:],
                    dsk[:].unsqueeze(2).to_broadcast([128, 16, 64]), ALU.mult),
                    reads=[t_ld[k], t_cn], writes=[t_ms])
                P.op("dve", lambda e, Y=Y: e.tensor_tensor(Y[:], Y[:], ysq[:], ALU.add), reads=[t_ms], writes=[tY])
                P.op("dve", lambda e, Y=Y: e.tensor_tensor(Y[:], Y[:], szl[:], ALU.mult), reads=[t_szl], writes=[tY])
                for g in range(2):
                    P.op("act", lambda e, g=g, Y=Y: e.activation(
                        ysq[:, g * 512:(g + 1) * 512], Y[:, g * 512:(g + 1) * 512], AF.Square, accum_out=ms[:, g:g + 1]),
                        reads=[tY], writes=[t_ms])
                P.op("act", lambda e: e.activation(ms[:], ms[:], AF.Sqrt, bias=eps_ap[:, 0:1], scale=1.0 / 512),
                     reads=[t_ms, t_eps], writes=[t_ms])
                P.op("dve", lambda e: e.reciprocal(ms[:], ms[:]), reads=[t_ms], writes=[t_ms])
                for g in range(2):
                    P.op("dve", lambda e, g=g, Y=Y: e.scalar_tensor_tensor(
                        ybf[:, g * 512:(g + 1) * 512], Y[:, g * 512:(g + 1) * 512], ms[:, g:g + 1],
                        sng[:, g * 512:(g + 1) * 512], ALU.mult, ALU.mult),
                        reads=[tY, t_ms, t_cn], writes=[t_ybf])
                bi = 7
                P.op("pe", lambda e, bi=bi: [e.transpose(banks_b[bi][:, j * 128:(j + 1) * 128],
                                                        ybf[:, j * 128:(j + 1) * 128], IDENT_B) for j in range(8)][-1],
                     reads=[t_ybf, t_cst], writes=[t_bank[bi]])
                ko = nxt("oxs", 2)
                P.op("act", lambda e, bi=bi, ko=ko: e.activation(
                    oxs[ko][:], banks_b[bi][:, :1024].rearrange("p (c t) -> p c t", t=128), AF.Copy),
                    reads=[t_bank[bi]], writes=[t_oxs[ko]])
                P.dma("act", OXv[:, 0:8, blk * 128:(blk + 1) * 128], oxs[ko][:], reads=[t_oxs[ko]], writes=[t_oxb])
        P.barrier()

    t_dsl = [Tok() for _ in range(8)]
    t_ysl = [Tok() for _ in range(6)]

    def mla_attn(l):
        last = (l == L - 1)
        AR.reset()
        Kh = [AR.alloc([96, NTOK], BF16) for _ in range(2)]
        Vh = [AR.alloc([128, NB, 65], BF16) for _ in range(2)]
        t_kv = [Tok(), Tok()]
        Qt = [AR.alloc([96, TT], BF16) for _ in range(2)]
        t_q = [Tok(), Tok()]
        NE = 4
        E = [AR.alloc([128, TT], BF16) for _ in range(NE)]
        t_E = [Tok() for _ in range(NE)]
        Ost = [AR.alloc([65, TT], F32) for _ in range(2)]
        t_ost = [Tok(), Tok()]
        print("mla arena used", AR.off)
        if l + 1 < L:
            emit_casts(l + 1)
        for h in range(NHM):
            kk = nxt("mkv", 2)
            P.dma("sp", Kh[kk][0:64, :], KN[h * 64:(h + 1) * 64, :], reads=t_scr, writes=[t_kv[kk]])
            P.dma("sp", Kh[kk][64:96, :], KR[:, :], reads=t_scr, writes=[t_kv[kk]])
            P.dma("sp", Vh[kk][:], VM[:, :, h, :].rearrange("b p d -> p b d"), reads=t_scr, writes=[t_kv[kk]])
            for ti, (t0, T, is_ctx) in enumerate(c.tiles):
                if is_ctx and last:
                    continue
                kq = nxt("mq", 2)
                P.dma("sp", Qt[kq][:, :T], QM[h, :, t0:t0 + T], reads=[t_scr[ti]], writes=[t_q[kq]])
                kblocks = list(range(NBC)) if is_ctx else list(range(NB))
                ob = 6 + nxt("mob", 2)
                def qk_exp(kb, kk=kk, kq=kq, T=T):
                    sb_ = nxt("msb", 4)
                    mm_group(banks[sb_][:, :T], [(Kh[kk][:, kb * 128:(kb + 1) * 128], Qt[kq][:, :T])],
                             [t_kv[kk], t_q[kq]], [t_bank[sb_]])
                    ke = nxt("me", NE)
                    P.op("act", lambda e, sb_=sb_, ke=ke, T=T: e.activation(E[ke][:, :T], banks[sb_][:, :T], AF.Exp,
                                                                            scale=SC_MLA),
                         reads=[t_bank[sb_]], writes=[t_E[ke]])
                    return ke
                nk = len(kblocks)
                kes = [qk_exp(kblocks[0])]
                for i, kb in enumerate(kblocks):
                    if i + 1 < nk:
                        kes.append(qk_exp(kblocks[i + 1]))
                    ke = kes[i]

                    def pv(e, ke=ke, kb=kb, ob=ob, T=T, first=(i == 0), lastk=(i == nk - 1), kk=kk):
                        return e.matmul(banks[ob][0:65, :T], Vh[kk][:, kb, :], E[ke][:, :T], start=first, stop=lastk)
                    P.op("pe", pv, reads=[t_E[ke], t_kv[kk]], writes=[t_bank[ob]])
                ko = nxt("most", 2)
                P.op("dve", lambda e, ob=ob, ko=ko, T=T: e.tensor_copy(Ost[ko][:, :T], banks[ob][0:65, :T]),
                     reads=[t_bank[ob]], writes=[t_ost[ko]])
                P.dma("act", OM[h, :, t0:t0 + T], Ost[ko][:, :T], reads=[t_ost[ko]], writes=[t_om])
        if l + 1 < L:
            emit_mod(l + 1)
        P.barrier()

    def swa_attn(l):
        last = (l == L - 1)
        AR.reset()
        Kg = [AR.alloc([128, NTOK], BF16) for _ in range(2)]
        Vg = [AR.alloc([128, NB, 65], BF16) for _ in range(2)]
        t_kv = [Tok(), Tok()]
        Qb = [AR.alloc([128, 2, 128], BF16) for _ in range(2)]
        t_q = [Tok(), Tok()]
        NE = 4
        E = [AR.alloc([128, 256], BF16) for _ in range(NE)]
        t_E = [Tok() for _ in range(NE)]
        Ost = [AR.alloc([65, 4, 128], F32) for _ in range(2)]
        t_ost = [Tok(), Tok()]
        print("swa arena used", AR.off)
        for g in range(2):
            kk = nxt("skv", 2)
            P.dma("sp", Kg[kk][:], KS[g], reads=t_scr, writes=[t_kv[kk]])
            P.dma("sp", Vg[kk][:], VS[:, :, g, :].rearrange("b p d -> p b d"), reads=t_scr, writes=[t_kv[kk]])
            for qb in range(NB):
                is_ctx = qb < NBC
                if is_ctx and last:
                    continue
                kq = nxt("sq", 2)
                P.dma("sp", Qb[kq][:], QS[2 * g:2 * g + 2, :, qb * 128:(qb + 1) * 128].rearrange("c p t -> p c t"),
                      reads=t_scr, writes=[t_q[kq]])
                if is_ctx:
                    kbs = [(kb, None) for kb in range(NBC)]
                else:
                    kbs = [(kb, None) for kb in range(NBC)]
                    if qb - 1 >= NBC:
                        kbs.append((qb - 1, TB_B))
                    kbs.append((qb, None))
                    if qb + 1 < NB:
                        kbs.append((qb + 1, TB_F))
                ko = nxt("sost", 2)
                for half in range(2):
                    rows = slice(half * 64, (half + 1) * 64)
                    ob = 6 + half
                    for i, (kb, msk) in enumerate(kbs):
                        sb_ = nxt("ssb", 4)
                        mm_group(banks[sb_][:, :256], [(Kg[kk][rows, kb * 128:(kb + 1) * 128],
                                                        Qb[kq][rows, :, :].rearrange("p c t -> p (c t)"))],
                                 [t_kv[kk], t_q[kq]], [t_bank[sb_]])
                        ke = nxt("se", NE)
                        P.op("act", lambda e, sb_=sb_, ke=ke: e.activation(E[ke][:, :256], banks[sb_][:, :256], AF.Exp,
                                                                           scale=SC_SWA),
                             reads=[t_bank[sb_]], writes=[t_E[ke]])
                        if msk is not None:
                            P.op("dve", lambda e, ke=ke, msk=msk: e.tensor_tensor(
                                E[ke][:, :256].rearrange("p (c t) -> p c t", t=128),
                                E[ke][:, :256].rearrange("p (c t) -> p c t", t=128),
                                msk.unsqueeze(1).to_broadcast([128, 2, 128]), ALU.mult),
                                reads=[t_cst], writes=[t_E[ke]])
                        for j in range(2):
                            obj = 4 + 2 * half + j
                            def pv(e, ke=ke, kb=kb, obj=obj, j=j, first=(i == 0), lastk=(i == len(kbs) - 1), kk=kk):
                                return e.matmul(banks[obj][0:65, 0:128], Vg[kk][:, kb, :],
                                                E[ke][:, j * 128:(j + 1) * 128], start=first, stop=lastk)
                            P.op("pe", pv, reads=[t_E[ke], t_kv[kk]], writes=[t_bank[obj]])
                    for j in range(2):
                        obj = 4 + 2 * half + j
                        P.op("dve", lambda e, obj=obj, ko=ko, half=half, j=j: e.tensor_copy(
                            Ost[ko][:, 2 * half + j, :], banks[obj][0:65, 0:128]),
                            reads=[t_bank[obj]], writes=[t_ost[ko]])
                for half in range(2):
                    for j in range(2):
                        hd = 2 * (2 * g + j) + half
                        P.dma("act", OS[hd, :, qb * 128:(qb + 1) * 128], Ost[ko][:, 2 * half + j, :],
                              reads=[t_ost[ko]], writes=[t_os])
        P.barrier()

    def attn_epilogue(l):
        last = (l == L - 1)
        AR.reset()
        num = AR.alloc([64, 8, TT], F32)
        den = AR.alloc([64, 8, TT], F32)
        sq = AR.alloc([64, 8, TT], BF16)
        ob = AR.alloc([64, 8, TT], BF16)
        rs = AR.alloc([128, TT], F32)
        gn = AR.alloc([64, 2, 8], F32)
        snk = AR.alloc([64, 8], F32)
        t_n, t_d, t_s, t_o, t_r, t_g = Tok(), Tok(), Tok(), Tok(), Tok(), Tok()
        P.dma("sp", gn[:, 0, :], mon_in[l], writes=[t_g])
        P.dma("sp", gn[:, 1, :], son_in[l], writes=[t_g])
        P.dma("sp", snk[:], sink_in[l], writes=[t_g])
        P.op("act", lambda e: e.activation(snk[:], snk[:], AF.Exp), reads=[t_g], writes=[t_g])
        print("epi arena used", AR.off)
        for ti, (t0, T, is_ctx) in enumerate(c.tiles):
            if is_ctx and last:
                continue
            for which, (SRC, t_src, row0) in enumerate(((OM, t_om, 1024), (OS, t_os, 1536))):
                P.dma("sp", num[:, :, :T], SRC[:, 0:64, t0:t0 + T].rearrange("h d t -> d h t"), reads=[t_src], writes=[t_n])
                for h in range(8):
                    P.dma("sp", den[:, h, :T], SRC[h, 64:65, t0:t0 + T].partition_broadcast(64), reads=[t_src], writes=[t_d])
                if which == 1:
                    P.op("dve", lambda e, T=T: e.tensor_tensor(den[:, :, :T], den[:, :, :T],
                                                               snk[:].unsqueeze(2).to_broadcast([64, 8, T]), ALU.add),
                         reads=[t_g], writes=[t_d])
                P.op("dve", lambda e, T=T: e.reciprocal(den[:, :, :T], den[:, :, :T]), reads=[], writes=[t_d])
                P.op("dve", lambda e, T=T: e.tensor_tensor(num[:, :, :T], num[:, :, :T], den[:, :, :T], ALU.mult),
                     reads=[t_d], writes=[t_n])
                P.op("act", lambda e, T=T: e.activation(sq[:, :, :T], num[:, :, :T], AF.Square), reads=[t_n], writes=[t_s])
                bk, tb = banks[which], t_bank[which]
                mm_group(bk[:, :T], [(ones_bf[0:64, :], sq[:, h, :T]) for h in range(8)], [t_s, t_cst], [tb])
                P.op("act", lambda e, T=T, bk=bk: e.activation(rs[:, :T], bk[:, :T], AF.Sqrt, bias=eps_ap[:, 0:1],
                                                              scale=1.0 / 512), reads=[tb, t_eps], writes=[t_r])
                P.op("dve", lambda e, T=T: e.reciprocal(rs[:, :T], rs[:, :T]), reads=[t_r], writes=[t_r])
                for h in range(8):
                    P.op("dve", lambda e, T=T, h=h, which=which: e.scalar_tensor_tensor(
                        ob[:, h, :T], num[:, h, :T], gn[:, which, h:h + 1], rs[0:64, :T], ALU.mult, ALU.mult),
                        reads=[t_n, t_r, t_g], writes=[t_o])
                P.dma("act", OX[row0:row0 + 512, t0:t0 + T].rearrange("(h d) t -> d h t", d=64), ob[:, :, :T],
                      reads=[t_o], writes=[t_oxb])
        P.barrier()

    for l in range(L):
        phase1(l)
        mixer_prep(l)
        ssd_scan(l)
        mla_attn(l)
        swa_attn(l)
        attn_epilogue(l)
        for ti in range(NT):
            t_ox[ti].w = t_oxb.w
        phase2(l)

    fin_idx = P.op("sp", None)
    P.ops[fin_idx].deps = set(out_stores)
    P.emit(es)
    return nc, es


def _rope_perm(dim):
    h, q = dim // 2, dim // 4
    perm = np.zeros(dim, np.int64)
    sign = np.zeros(dim, np.float32)
    for i in range(dim):
        base = 0 if i < h else h
        j = i - base
        if j < q:
            perm[i], sign[i] = base + j + q, -1.0
        else:
            perm[i], sign[i] = base + j - q, 1.0
    return perm, sign


def _rope_tables(seq, dim):
    rows = seq // GRID_W
    pos_row = np.repeat(np.arange(rows), GRID_W).astype(np.float32)
    pos_col = np.tile(np.arange(GRID_W), rows).astype(np.float32)
    quarter = dim // 4
    inv_freq = (ROPE_THETA ** (-np.arange(quarter, dtype=np.float32) / quarter)).astype(np.float32)
    ang_r = pos_row[:, None] * inv_freq[None, :]
    ang_c = pos_col[:, None] * inv_freq[None, :]
    ang = np.concatenate([ang_r, ang_r, ang_c, ang_c], axis=-1).astype(np.float32)
    _, sign = _rope_perm(dim)
    return np.stack([np.cos(ang).T, (np.sin(ang) * sign[None, :]).T], axis=0).astype(np.float32)


def _prep_shared(inp, c):
    D, FF, DC, FC, NH, FCH, L = c.D, c.FF, c.DC, c.FC, c.NH, c.FCH, c.L
    f = np.float32
    sh = {}
    wm = np.asarray(inp["w_mod"], f).reshape(L, DC, 128, 9 * DC, 128)
    sh["wmod_t"] = np.ascontiguousarray(wm.transpose(0, 3, 2, 1, 4))
    sh["bmod_t"] = np.ascontiguousarray(np.asarray(inp["b_mod"], f).reshape(L, 9 * DC, 128).transpose(0, 2, 1))
    sh["ng_t"] = np.ascontiguousarray(np.asarray(inp["norm_g"], f).reshape(L, 3, DC, 128).transpose(0, 3, 1, 2))
    wg = np.asarray(inp["ffn_w_gate"], f).reshape(L, 2, DC, 128, FC, 128)
    wu = np.asarray(inp["ffn_w_up"], f).reshape(L, 2, DC, 128, FC, 128)
    wgu = np.stack([wg, wu], axis=0)
    sh["wgu_t"] = np.ascontiguousarray(wgu.transpose(1, 2, 5, 0, 4, 3, 6))
    wd = np.asarray(inp["ffn_w_down"], f).reshape(L, 2, NH, FCH, 128, DC, 128)
    sh["wd_t"] = np.ascontiguousarray(wd.transpose(0, 1, 2, 5, 4, 3, 6))
    sh["fin_t"] = np.ascontiguousarray(np.asarray(inp["final_norm"], f).reshape(DC, 128).T)
    w_in = np.asarray(inp["w_in"], f)
    pM, _ = _rope_perm(32)
    pS, _ = _rope_perm(64)
    cols = -np.ones((NCH_WIN, 128), np.int64)
    j = np.arange(128)
    for ch in range(12):
        cols[ch] = 1024 + 128 * ch + j
    cols[12, :32] = 2560 + np.arange(32)
    for i in range(3):
        cols[13 + i] = 2592 + 128 * i + j
    for i in range(2):
        cols[16 + i] = 2976 + 128 * i + j
    cols[18, 64:96] = 3232 + np.arange(32)
    cols[19, 64:96] = 3232 + pM
    for i in range(4):
        cols[20 + i] = 3264 + 128 * i + j
        cols[24 + i] = 3264 + 128 * i + 64 * (j // 64) + pS[j % 64]
    for g in range(2):
        cols[28 + g] = 3776 + 64 * g + (j % 64)
        cols[30 + g] = 3776 + 64 * g + pS[j % 64]
    for i in range(8):
        cols[32 + i] = 128 * i + j
    cols[40] = 3904 + j
    valid = cols >= 0
    wsel = w_in[:, :, np.where(valid, cols, 0).reshape(-1)].reshape(L, DC, 128, NCH_WIN, 128)
    wsel = wsel * valid[None, None, None, :, :].astype(f)
    sh["win_t"] = np.ascontiguousarray(wsel.transpose(0, 3, 2, 1, 4))
    wo = np.asarray(inp["w_out"], f).reshape(L, KM, 128, DC, 128)
    sh["wout_t"] = np.ascontiguousarray(wo.transpose(0, 3, 2, 1, 4))
    wuq = np.asarray(inp["mla_w_uq"], f).reshape(L, 3, 128, NHM, 96)
    jj = np.arange(96)
    permj = np.where(jj < 64, jj, 64 + pM[np.clip(jj - 64, 0, 31)])
    wuq2 = np.stack([wuq, wuq[..., permj]], axis=1)
    sh["wuq_t"] = np.ascontiguousarray(wuq2.transpose(0, 1, 4, 3, 2, 5))
    wukv = np.asarray(inp["mla_w_ukv"], f).reshape(L, 2, 128, NHM, 128)
    wk = wukv[..., :64].reshape(L, 2, 128, 4, 128)
    sh["wukvk_t"] = np.ascontiguousarray(wk.transpose(0, 3, 2, 1, 4))
    wv = wukv[..., 64:].reshape(L, 2, 128, 512)
    sh["wukvv_t"] = np.ascontiguousarray(wv.transpose(0, 2, 1, 3))
    sh["qn_t"] = np.ascontiguousarray(np.asarray(inp["mla_q_norm"], f).reshape(L, 3, 128).transpose(0, 2, 1))
    sh["kvn_t"] = np.ascontiguousarray(np.asarray(inp["mla_kv_norm"], f).reshape(L, 2, 128).transpose(0, 2, 1))
    cw = np.asarray(inp["ssd_conv_w"], f).reshape(L, 5, 12, 128)
    sh["convw_t"] = np.ascontiguousarray(cw.transpose(0, 3, 2, 1))
    sh["convb_t"] = np.ascontiguousarray(np.asarray(inp["ssd_conv_b"], f).reshape(L, 12, 128).transpose(0, 2, 1))
    dtb = np.asarray(inp["ssd_dt_bias"], f).reshape(L, 32)
    sh["dtb_t"] = np.ascontiguousarray(np.concatenate([dtb, dtb], axis=1)[:, :, None])
    al = np.asarray(inp["ssd_a_log"], f).reshape(L, 32)
    sh["alog_t"] = np.ascontiguousarray(np.concatenate([al, al], axis=1)[:, :, None])
    sh["dsk_t"] = np.ascontiguousarray(np.broadcast_to(np.asarray(inp["ssd_d"], f)[:, None, :], (L, 128, 16)))
    sh["sn_t"] = np.ascontiguousarray(np.broadcast_to(np.asarray(inp["ssd_norm"], f)[:, None, :], (L, 128, 1024)))
    sh["mon_t"] = np.ascontiguousarray(np.asarray(inp["mla_out_norm"], f).reshape(L, 8, 64).transpose(0, 2, 1))
    sh["son_t"] = np.ascontiguousarray(np.asarray(inp["swa_out_norm"], f).reshape(L, 8, 64).transpose(0, 2, 1))
    sh["sink_t"] = np.ascontiguousarray(np.broadcast_to(np.asarray(inp["swa_sink"], f)[:, None, :], (L, 64, 8)))
    sh["ropeM_t"] = _rope_tables(c.SEQ, 32)
    rs = _rope_tables(c.SEQ, 64)
    sh["ropeS_t"] = np.ascontiguousarray(np.concatenate([rs, rs], axis=1))
    ii = np.arange(128)
    J, Lm = ii[:, None], ii[None, :]
    cst = np.stack([J == Lm, J <= Lm, J >= Lm, J > Lm, J < Lm, np.ones((128, 128), bool)], axis=1)
    sh["cst_t"] = np.ascontiguousarray(cst.astype(f))
    return sh


def _prep_core(inp, c, b):
    f = np.float32
    x = np.asarray(inp["x"], f)[b]
    ctx = np.asarray(inp["ctx"], f)[b]
    xT = np.ascontiguousarray(np.concatenate([ctx, x], axis=0).T)
    cc = np.stack([np.asarray(inp["c"], f)[b], np.asarray(inp["c_ctx"], f)], axis=1)
    cT = np.ascontiguousarray(cc.reshape(c.DC, 128, 2).transpose(1, 0, 2))
    return {"xT": xT, "cT": cT}


_CACHE = {}


def kernel(**inputs):
    x = np.asarray(inputs["x"])
    B, SEQ, D = x.shape
    CTX = np.asarray(inputs["ctx"]).shape[1]
    FF = np.asarray(inputs["ffn_w_gate"]).shape[3]
    L = np.asarray(inputs["w_mod"]).shape[0]
    c = make_cfg(D, FF, SEQ, CTX, L)
    key = (D, FF, SEQ, CTX, L)
    if key not in _CACHE:
        _CACHE[key] = build_program(c)
    nc, _es = _CACHE[key]
    sh = _prep_shared(inputs, c)
    in_maps = []
    for b in range(B):
        m = dict(sh)
        m.update(_prep_core(inputs, c, b))
        in_maps.append(m)
    res = run_bass_kernel_spmd(nc, in_maps, core_ids=list(range(B)))
    out = np.stack([np.asarray(res.results[b]["outT"]).T for b in range(B)], axis=0)
    return np.ascontiguousarray(out.astype(np.float32))
```

```python
import numpy as np
import ml_dtypes
from contextlib import ExitStack
import concourse.bass as bass
import concourse.mybir as mybir
from concourse.bass_utils import run_bass_kernel_spmd

F32 = mybir.dt.float32
BF16 = mybir.dt.bfloat16
ALU = mybir.AluOpType
AF = mybir.ActivationFunctionType
AX = mybir.AxisListType

EPS = 1e-6
ROPE_THETA = 10000.0
GRID_W = 64


class Tok:
    __slots__ = ("w", "r", "name")

    def __init__(self, name=""):
        self.w = None
        self.r = []
        self.name = name


class _Op:
    __slots__ = ("eng", "fn", "deps", "dma", "signal", "sem", "val", "pos", "idx", "prev")


ENGS = ("pe", "act", "dve", "pool", "sp")
SEM_LIMIT = 30000
N_DMA_SEMS = 24


class Prog:
    def __init__(self, nc):
        self.nc = nc
        self.ops = []
        self.streams = {e: [] for e in ENGS}
        self.dma_since_barrier = []

    def op(self, eng, fn, reads=(), writes=(), dma=False, nobar=False):
        idx = len(self.ops)
        deps = set()
        for t in reads:
            if t.w is not None:
                deps.add(t.w)
        for t in writes:
            if t.w is not None:
                deps.add(t.w)
            deps.update(t.r)
        for t in reads:
            t.r.append(idx)
        for t in writes:
            t.w = idx
            t.r = []
        deps.discard(idx)
        o = _Op()
        o.eng, o.fn, o.deps, o.dma, o.signal = eng, fn, deps, dma, False
        o.sem = o.val = None
        o.idx = idx
        o.pos = len(self.streams[eng])
        self.streams[eng].append(o)
        self.ops.append(o)
        if dma and not nobar:
            self.dma_since_barrier.append(idx)
        return idx

    def dma(self, eng, out, in_, reads=(), writes=(), nobar=False):
        return self.op(eng, lambda e: e.dma_start(out=out, in_=in_), reads, writes, dma=True, nobar=nobar)

    def barrier(self):
        deps = set(self.dma_since_barrier)
        for e in ENGS:
            for o in reversed(self.streams[e]):
                if o.fn is not None:
                    deps.add(o.idx)
                    break
        self.dma_since_barrier = []
        for e in ENGS:
            idx = self.op(e, None)
            self.ops[idx].deps = set(d for d in deps)

    def _needs_wait(self, o, d):
        if d.dma:
            return True
        if d.eng == o.eng:
            if o.dma:
                return True
            if o.eng == "pe":
                return False
            return d.pos >= o.pos - 3
        return True

    def emit(self, es):
        nc = self.nc
        ops = self.ops
        for o in ops:
            for di in o.deps:
                d = ops[di]
                if self._needs_wait(o, d):
                    d.signal = True
        for o in ops:
            if o.dma:
                o.signal = True
        sems = {}
        n_sem = [0]

        def new_sem(tag):
            n_sem[0] += 1
            return es.enter_context(nc.semaphore(f"s_{tag}_{n_sem[0]}"))

        for e in ENGS:
            cur = None
            cnt = 0
            dsem = []
            dcnt = []
            k = 0
            for o in self.streams[e]:
                if not o.signal:
                    continue
                if o.dma:
                    if len(dsem) < N_DMA_SEMS:
                        dsem.append(new_sem(e + "d"))
                        dcnt.append(0)
                    j = k % N_DMA_SEMS
                    k += 1
                    if dcnt[j] + 16 > SEM_LIMIT:
                        dsem[j] = new_sem(e + "d")
                        dcnt[j] = 0
                    o.prev = dcnt[j]
                    dcnt[j] += 16
                    o.sem, o.val = dsem[j], dcnt[j]
                else:
                    if cur is None or cnt + 1 > SEM_LIMIT:
                        cur = new_sem(e)
                        cnt = 0
                    cnt += 1
                    o.sem, o.val = cur, cnt
        self.n_sems = n_sem[0]

        engobj = {"pe": "tensor", "act": "scalar", "dve": "vector", "pool": "gpsimd", "sp": "sync"}

        def run_stream(ename):
            def body(eng):
                seen = {}
                for o in self.streams[ename]:
                    waits = {}
                    for di in o.deps:
                        d = ops[di]
                        if not self._needs_wait(o, d):
                            continue
                        key = id(d.sem)
                        if seen.get(key, 0) >= d.val:
                            continue
                        if key not in waits or waits[key][1] < d.val:
                            waits[key] = (d.sem, d.val)
                    if o.dma and o.prev > 0 and seen.get(id(o.sem), 0) < o.prev:
                        waits[id(o.sem)] = (o.sem, o.prev)
                    for key, (s, v) in waits.items():
                        eng.wait_ge(s, v)
                        seen[key] = v
                    if o.fn is None:
                        continue
                    ins = o.fn(eng)
                    if o.signal:
                        ins.then_inc(o.sem, 16 if o.dma else 1)
            return body

        with nc.Block() as block:
            for ename in ENGS:
                if not self.streams[ename]:
                    continue
                getattr(block, engobj[ename])(run_stream(ename))


class Cfg:
    pass


def make_cfg(D, FF, SEQ, CTX, DEPTH):
    c = Cfg()
    c.D, c.FF, c.SEQ, c.CTX, c.L = D, FF, SEQ, CTX, DEPTH
    c.DC = D // 128
    c.FC = FF // 128
    c.NH = 2 if c.FC % 2 == 0 else 1
    c.FCH = c.FC // c.NH
    c.NTOK = CTX + SEQ
    c.TT = 512
    tiles = []
    t0 = 0
    while t0 < CTX:
        T = min(c.TT, CTX - t0)
        tiles.append((t0, T, True))
        t0 += T
    while t0 < c.NTOK:
        T = min(c.TT, c.NTOK - t0)
        tiles.append((t0, T, False))
        t0 += T
    c.tiles = tiles
    return c


NCH_WIN = 41
D_MIX = 2048
KM = 16
NHM = 8
NHS = 8
SC_MLA = 96 ** -0.5
SC_SWA = 64 ** -0.5


def _prod(xs):
    r = 1
    for x in xs:
        r *= x
    return r


class Arena:
    def __init__(self, base, nbytes):
        self.base, self.n, self.off = base, nbytes, 0

    def reset(self):
        self.off = 0

    def alloc(self, shape, dt):
        cols = _prod(shape[1:])
        esz = 4 if dt == F32 else 2
        nb = cols * esz
        o = self.off
        self.off += (nb + 63) // 64 * 64
        assert self.off <= self.n, f"arena overflow {self.off} > {self.n}"
        v = self.base[:, o // 2:o // 2 + nb // 2]
        if dt == F32:
            v = v.bitcast(F32)
        if len(shape) == 3:
            v = v.rearrange("p (a b) -> p a b", b=shape[2])
        elif len(shape) == 4:
            v = v.rearrange("p (a b c) -> p a b c", b=shape[2], c=shape[3])
        if shape[0] < 128:
            v = v[0:shape[0]]
        return v


def build_program(c):
    nc = bass.Bass("TRN2", target_bir_lowering=False)
    P = Prog(nc)
    D, FF, DC, FC, NH, FCH, L, NTOK, SEQ, CTX = c.D, c.FF, c.DC, c.FC, c.NH, c.FCH, c.L, c.NTOK, c.SEQ, c.CTX
    NB = NTOK // 128
    NBC = CTX // 128
    TT = c.TT
    WK = max(DC, KM)

    def dram_in(name, shape, dt=F32):
        return nc.dram_tensor(name, list(shape), dt, kind="ExternalInput").ap()

    def dram_scr(name, shape, dt):
        return nc.dram_tensor(name, list(shape), dt, kind="Internal").ap()

    xT_in = dram_in("xT", [D, NTOK])
    cT_in = dram_in("cT", [128, DC, 2])
    wmod_in = dram_in("wmod_t", [L, 9 * DC, 128, DC, 128])
    bmod_in = dram_in("bmod_t", [L, 128, 9 * DC])
    ng_in = dram_in("ng_t", [L, 128, 3, DC])
    wgu_in = dram_in("wgu_t", [L, 2, FC, 2, 128, DC, 128])
    wd_in = dram_in("wd_t", [L, 2, NH, DC, 128, FCH, 128])
    fin_in = dram_in("fin_t", [128, DC])
    win_in = dram_in("win_t", [L, NCH_WIN, 128, DC, 128])
    wout_in = dram_in("wout_t", [L, DC, 128, KM, 128])
    wuq_in = dram_in("wuq_t", [L, 2, NHM, 128, 3, 96])
    wukvk_in = dram_in("wukvk_t", [L, 4, 128, 2, 128])
    wukvv_in = dram_in("wukvv_t", [L, 128, 2, 512])
    qn_in = dram_in("qn_t", [L, 128, 3])
    kvn_in = dram_in("kvn_t", [L, 128, 2])
    convw_in = dram_in("convw_t", [L, 128, 12, 5])
    convb_in = dram_in("convb_t", [L, 128, 12])
    dtb_in = dram_in("dtb_t", [L, 64, 1])
    alog_in = dram_in("alog_t", [L, 64, 1])
    dsk_in = dram_in("dsk_t", [L, 128, 16])
    sn_in = dram_in("sn_t", [L, 128, 1024])
    mon_in = dram_in("mon_t", [L, 64, 8])
    son_in = dram_in("son_t", [L, 64, 8])
    sink_in = dram_in("sink_t", [L, 64, 8])
    ropeM_in = dram_in("ropeM_t", [2, 32, SEQ])
    ropeS_in = dram_in("ropeS_t", [2, 128, SEQ])
    cst_in = dram_in("cst_t", [128, 6, 128])
    out_ap = nc.dram_tensor("outT", [D, SEQ], F32, kind="ExternalOutput").ap()

    hxT = dram_scr("hxT", [D, NTOK], F32)
    wgu_bf = [dram_scr(f"wgu_bf{l}", [2, FC, 2, 128, DC, 128], BF16) for l in range(L)]
    wd_bf = [dram_scr(f"wd_bf{l}", [2, NH, DC, 128, FCH, 128], BF16) for l in range(L)]
    win_bf = [dram_scr(f"win_bf{l}", [NCH_WIN, 128, DC, 128], BF16) for l in range(L)]
    wout_bf = [dram_scr(f"wout_bf{l}", [DC, 128, KM, 128], BF16) for l in range(L)]
    XBC = dram_scr("XBC", [1536, NTOK], BF16)
    DTR = dram_scr("DTR", [32, NTOK], F32)
    QM = dram_scr("QM", [NHM, 96, NTOK], BF16)
    KN = dram_scr("KN", [512, NTOK], BF16)
    KR = dram_scr("KR", [32, NTOK], BF16)
    VM = dram_scr("VM", [NB, 128, NHM, 65], BF16)
    QS = dram_scr("QS", [4, 128, NTOK], BF16)
    KS = dram_scr("KS", [2, 128, NTOK], BF16)
    VS = dram_scr("VS", [NB, 128, 2, 65], BF16)
    SZ = dram_scr("SZ", [NB, 128, 1024], BF16)
    OX = dram_scr("OX", [D_MIX, NTOK], BF16)
    OM = dram_scr("OM", [NHM, 65, NTOK], F32)
    OS = dram_scr("OS", [NHS, 65, NTOK], F32)
    XS = dram_scr("XS", [NB, 128, 1024], BF16)
    BTM = dram_scr("BTM", [NB, 128, 2, 128], BF16)
    BCT = dram_scr("BCT", [NB, 128, 4, 128], BF16)
    DTA = dram_scr("DTA", [NB, 128, 64], F32)
    YF = dram_scr("YF", [NB, 128, 1024], F32)

    es = ExitStack()

    def sb(name, shape, dt):
        return nc.alloc_sbuf_tensor("sb_" + name, list(shape), dt).ap()

    cst = sb("cst", [128, 6, 128], F32)
    IDENT_F, T_F, T_B, U_F, U_B, ONES_F = [cst[:, i, :] for i in range(6)]
    cstb = sb("cstb", [128, 6, 128], BF16)
    IDENT_B, TB_F, TB_B, _ub0, _ub1, ones_bf = [cstb[:, i, :] for i in range(6)]
    t_cst = Tok()
    P.dma("sp", cst[:], cst_in[:], writes=[t_cst])
    P.op("dve", lambda e: e.tensor_copy(cstb[:], cst[:]), reads=[t_cst], writes=[t_cst])
    t_ones = t_cst
    eps_ap = sb("eps_ap", [128, 1], F32)
    one_ap = sb("one_ap", [128, 1], F32)
    t_eps = Tok()
    P.op("pool", lambda e: e.memset(eps_ap[:], EPS), writes=[t_eps])
    P.op("pool", lambda e: e.memset(one_ap[:], 1.0), writes=[t_eps])

    banks = [nc.alloc_psum_tensor(f"bank{i}", [128, 512], F32).ap() for i in range(8)]
    banks_b = [b.bitcast(BF16) for b in banks]
    t_bank = [Tok(f"bank{i}") for i in range(8)]

    def mm_group(out, pairs, reads, writes):
        def f(e):
            ins = None
            n = len(pairs)
            for i, (a, b) in enumerate(pairs):
                ins = e.matmul(out, a, b, start=(i == 0), stop=(i == n - 1))
            return ins
        return P.op("pe", f, reads, writes)

    t_hx_dram = [Tok(f"hxd{i}") for i in range(len(c.tiles))]
    P.dma("sp", hxT[:, :], xT_in[:, :], writes=t_hx_dram)

    cT = sb("cTs", [128, DC, 2], F32)
    csil = sb("csil", [128, DC, 2], BF16)
    t_c = Tok()
    t_csil = Tok()
    P.dma("sp", cT[:], cT_in[:], writes=[t_c])
    P.op("act", lambda e: e.activation(csil[:], cT[:], AF.Silu), reads=[t_c], writes=[t_csil])
    NMC = 9 * DC
    MOD = [sb(f"mod{l}", [128, NMC, 2], F32) for l in range(L)]
    bmod = [sb(f"bmod{l}", [128, NMC], F32) for l in range(L)]
    ngt = [sb(f"ng{l}", [128, 3, DC], F32) for l in range(L)]
    AV = [sb(f"av{l}", [128, 3, DC, 2], F32) for l in range(L)]
    GV = [sb(f"gv{l}", [128, 3, DC, 2], F32) for l in range(L)]
    t_mod = [Tok() for _ in range(L)]
    fin = sb("fin", [128, DC], F32)
    t_fin = Tok()
    P.dma("sp", fin[:], fin_in[:], writes=[t_fin])

    NWM = 6
    wm_buf = [sb(f"wm{i}", [128, DC, 128], BF16) for i in range(NWM)]
    t_wm = [Tok() for _ in range(NWM)]
    ARENA_BYTES = nc.sbuf_bytes_remaining // 128 - 2048 if nc.sbuf_bytes_remaining > 4 * 1024 * 1024 else nc.sbuf_bytes_remaining - 2048
    ARENA_BYTES = (ARENA_BYTES // 64) * 64
    arena_t = sb("arena", [128, ARENA_BYTES // 2], BF16)
    AR = Arena(arena_t, ARENA_BYTES)
    print("arena bytes/partition:", ARENA_BYTES)

    def emit_mod(l):
        t_bm = Tok()
        P.dma("sp", bmod[l][:], bmod_in[l], writes=[t_bm])
        P.dma("sp", ngt[l][:], ng_in[l], writes=[t_bm])
        bank = banks[l % 2]
        tb = t_bank[l % 2]
        for cc in range(NMC):
            i = (l * NMC + cc) % NWM
            P.dma("pool", wm_buf[i][:], wmod_in[l, cc], writes=[t_wm[i]], nobar=True)
            mm_group(bank[:, cc * 2:cc * 2 + 2], [(wm_buf[i][:, kc, :], csil[:, kc, :]) for kc in range(DC)],
                     [t_wm[i], t_csil], [tb])
        P.op("dve", lambda e, l=l, bank=bank: e.tensor_tensor(
            MOD[l][:], bank[:, 0:2 * NMC].rearrange("p (c t) -> p c t", t=2),
            bmod[l][:].unsqueeze(2).to_broadcast([128, NMC, 2]), ALU.add),
            reads=[tb, t_bm], writes=[t_mod[l]])
        for i3 in range(3):
            sc = MOD[l][:, (3 * i3 + 1) * DC:(3 * i3 + 2) * DC, :]
            gt = MOD[l][:, (3 * i3 + 2) * DC:(3 * i3 + 3) * DC, :]
            P.op("dve", lambda e, l=l, i3=i3, sc=sc: e.scalar_tensor_tensor(
                AV[l][:, i3], sc, 1.0, ngt[l][:, i3].unsqueeze(2).to_broadcast([128, DC, 2]),
                ALU.add, ALU.mult), reads=[t_mod[l], t_bm], writes=[t_mod[l]])
            P.op("dve", lambda e, l=l, i3=i3, gt=gt: e.tensor_scalar(
                GV[l][:, i3], gt, 0.5 if i3 != 1 else 1.0, None, ALU.mult),
                reads=[t_mod[l]], writes=[t_mod[l]])


    t_wgu = [[Tok() for _ in range(2)] for _ in range(L)]
    t_wd = [[Tok() for _ in range(2)] for _ in range(L)]
    t_win = [Tok() for _ in range(L)]
    t_wout = [Tok() for _ in range(L)]
    def emit_casts(l):
        for ch0 in range(0, NCH_WIN, 8):
            ch1 = min(NCH_WIN, ch0 + 8)
            P.dma("pool", win_bf[l][ch0:ch1], win_in[l, ch0:ch1], writes=[t_win[l]], nobar=True)
        for i in range(2):
            if i == 1:
                for dc0 in range(0, DC, 4):
                    P.dma("pool", wout_bf[l][dc0:min(DC, dc0 + 4)], wout_in[l, dc0:min(DC, dc0 + 4)], writes=[t_wout[l]], nobar=True)
            for fc in range(FC):
                P.dma("pool", wgu_bf[l][i, fc], wgu_in[l, i, fc], writes=[t_wgu[l][i]], nobar=True)
            for hf in range(NH):
                for dc0 in range(0, DC, 4):
                    P.dma("pool", wd_bf[l][i, hf, dc0:min(DC, dc0 + 4)], wd_in[l, i, hf, dc0:min(DC, dc0 + 4)], writes=[t_wd[l][i]], nobar=True)


    emit_mod(0)
    emit_casts(0)

    hx_view = hxT.rearrange("(c p) t -> p c t", p=128)
    out_view = out_ap.rearrange("(c p) t -> p c t", p=128)
    out_stores = []
    cnt = {}

    def nxt(key, n):
        v = cnt.get(key, 0)
        cnt[key] = v + 1
        return v % n

    def dense_alloc():
        AR.reset()
        B = Cfg()
        B.hx = AR.alloc([128, DC, TT], F32)
        B.u = AR.alloc([128, WK, TT], BF16)
        B.hbuf = AR.alloc([128, max(FCH, DC, 3), TT], BF16)
        B.rstd = AR.alloc([128, TT], F32)
        B.tmp = [AR.alloc([128, TT], F32) for _ in range(2)]
        B.sg = [AR.alloc([128, TT], F32) for _ in range(2)]
        B.NW = 6
        B.w = [AR.alloc([128, WK, 128], BF16) for _ in range(B.NW)]
        B.wd = [AR.alloc([128, FCH, 128], BF16) for _ in range(2)]
        B.t_hx = [Tok() for _ in range(DC)]
        B.t_u = [Tok() for _ in range(WK)]
        B.t_h = [Tok() for _ in range(max(FCH, DC, 3))]
        B.t_rstd = Tok()
        B.t_tmp = [Tok(), Tok()]
        B.t_sg = [Tok(), Tok()]
        B.t_w = [Tok() for _ in range(B.NW)]
        B.t_wd = [Tok(), Tok()]
        return B

    def fm_rstd(B, T, srcs, nfeat, t_src, rstd_out, t_out, nrows=128):
        n = len(srcs)
        for i, s_ap in enumerate(srcs):
            P.op("act", lambda e, i=i, s_ap=s_ap: e.activation(B.hbuf[0:nrows, i, :T], s_ap, AF.Square),
                 reads=t_src, writes=[B.t_h[i]])
        bk, tb = banks[6], t_bank[6]
        mm_group(bk[:, :T], [(ones_bf[0:nrows, :], B.hbuf[0:nrows, i, :T]) for i in range(n)],
                 [B.t_h[i] for i in range(n)] + [t_ones], [tb])
        P.op("act", lambda e: e.activation(rstd_out[:, :T], bk[:, :T], AF.Sqrt, bias=eps_ap[:, 0:1],
                                           scale=1.0 / nfeat), reads=[tb, t_eps], writes=[t_out])
        P.op("dve", lambda e: e.reciprocal(rstd_out[:, :T], rstd_out[:, :T]), reads=[t_out], writes=[t_out])

    def modnorm(B, T, l, avec, bvec, col):
        fm_rstd(B, T, [B.hx[:, dc, :T] for dc in range(DC)], D, B.t_hx, B.rstd, B.t_rstd)
        for dc in range(DC):
            k = nxt("tmp", 2)
            P.op("dve", lambda e, dc=dc, k=k: e.scalar_tensor_tensor(
                B.tmp[k][:, :T], B.hx[:, dc, :T], avec[:, dc, col:col + 1], B.rstd[:, :T], ALU.mult, ALU.mult),
                reads=[B.t_hx[dc], B.t_rstd, t_mod[l]], writes=[B.t_tmp[k]])
            P.op("act", lambda e, dc=dc, k=k: e.activation(
                B.u[:, dc, :T], B.tmp[k][:, :T], AF.Identity, bias=bvec[:, dc, col:col + 1], scale=1.0),
                reads=[B.t_tmp[k], t_mod[l]], writes=[B.t_u[dc]])

    def load_w(B, src, t_src, K=None):
        s = nxt("w", B.NW)
        K = DC if K is None else K
        P.dma("sp", B.w[s][:, :K, :], src, reads=[t_src], writes=[B.t_w[s]])
        return s

    def ffn(B, T, l, i, gvec, col):
        tu_all = B.t_u[:DC]
        for hf in range(NH):
            for f in range(FCH):
                fc = hf * FCH + f
                sgi = load_w(B, wgu_bf[l][i, fc, 0], t_wgu[l][i])
                sui = load_w(B, wgu_bf[l][i, fc, 1], t_wgu[l][i])
                bg, bu_ = banks[(2 * f) % 4], banks[(2 * f + 1) % 4]
                tg, tu = t_bank[(2 * f) % 4], t_bank[(2 * f + 1) % 4]
                mm_group(bg[:, :T], [(B.w[sgi][:, kc, :], B.u[:, kc, :T]) for kc in range(DC)],
                         [B.t_w[sgi]] + tu_all, [tg])
                mm_group(bu_[:, :T], [(B.w[sui][:, kc, :], B.u[:, kc, :T]) for kc in range(DC)],
                         [B.t_w[sui]] + tu_all, [tu])
                k = nxt("sg", 2)
                P.op("act", lambda e, k=k, bg=bg: e.activation(B.sg[k][:, :T], bg[:, :T], AF.Silu),
                     reads=[tg], writes=[B.t_sg[k]])
                P.op("dve", lambda e, k=k, bu_=bu_, f=f: e.tensor_tensor(
                    B.hbuf[:, f, :T], B.sg[k][:, :T], bu_[:, :T], ALU.mult),
                    reads=[B.t_sg[k], tu], writes=[B.t_h[f]])
            for dc in range(DC):
                s = nxt("wd", 2)
                P.dma("sp", B.wd[s][:], wd_bf[l][i, hf, dc], reads=[t_wd[l][i]], writes=[B.t_wd[s]])
                bo, to = banks[4 + dc % 2], t_bank[4 + dc % 2]
                mm_group(bo[:, :T], [(B.wd[s][:, f, :], B.hbuf[:, f, :T]) for f in range(FCH)],
                         [B.t_wd[s]] + B.t_h[:FCH], [to])
                P.op("dve", lambda e, dc=dc, bo=bo: e.scalar_tensor_tensor(
                    B.hx[:, dc, :T], bo[:, :T], gvec[:, dc, col:col + 1], B.hx[:, dc, :T], ALU.mult, ALU.add),
                    reads=[to, t_mod[l]], writes=[B.t_hx[dc]])

    NT = len(c.tiles)
    t_scr = [Tok(f"scr{i}") for i in range(NT)]
    t_ox = [Tok(f"ox{i}") for i in range(NT)]

    def inproj(B, X, T, l, ti, t0, is_ctx):
        tu_all = B.t_u[:DC]
        nblk = T // 128
        ts = t_scr[ti]
        ring = [0, 1, 2, 3, 4, 5]

        def fm_chunk(ch, M=128):
            s = load_w(B, win_bf[l][ch], t_win[l])
            bi = ring[nxt("ipb", 6)]
            mm_group(banks[bi][0:M, :T], [(B.w[s][:, kc, 0:M], B.u[:, kc, :T]) for kc in range(DC)],
                     [B.t_w[s]] + tu_all, [t_bank[bi]])
            return bi

        def stage_bf():
            k = nxt("stb", 3)
            return X.stb[k], X.t_stb[k]

        for ch in range(12):
            bi = fm_chunk(ch)
            st, tst = stage_bf()
            eng = "act" if ch % 2 == 0 else "dve"
            if eng == "act":
                P.op("act", lambda e, bi=bi, st=st: e.activation(st[:, :T], banks[bi][:, :T], AF.Copy),
                     reads=[t_bank[bi]], writes=[tst])
            else:
                P.op("dve", lambda e, bi=bi, st=st: e.tensor_copy(st[:, :T], banks[bi][:, :T]),
                     reads=[t_bank[bi]], writes=[tst])
            P.dma("act", XBC[ch * 128:(ch + 1) * 128, t0:t0 + T], st[:, :T], reads=[tst], writes=[ts])
        bi = fm_chunk(12, 32)
        P.op("act", lambda e, bi=bi: e.activation(X.stf[0:32, :T], banks[bi][0:32, :T], AF.Copy),
             reads=[t_bank[bi]], writes=[X.t_stf])
        P.dma("act", DTR[:, t0:t0 + T], X.stf[0:32, :T], reads=[X.t_stf], writes=[ts])
        if not is_ctx:
            p0 = t0 - CTX
            P.dma("sp", X.ropeM[64:96, 0, :T], ropeM_in[0, :, p0:p0 + T], writes=[X.t_rope])
            P.dma("sp", X.ropeM[64:96, 1, :T], ropeM_in[1, :, p0:p0 + T], writes=[X.t_rope])
            P.dma("sp", X.ropeS[:, 0, :T], ropeS_in[0, :, p0:p0 + T], writes=[X.t_rope])
            P.dma("sp", X.ropeS[:, 1, :T], ropeS_in[1, :, p0:p0 + T], writes=[X.t_rope])

        def rope_evac(ba, bb, r0, r1, table, st):
            k = nxt("tmp", 2)
            k2 = nxt("sg", 2)
            P.op("dve", lambda e: e.tensor_tensor(B.tmp[k][r0:r1, :T], banks[ba][r0:r1, :T], table[r0:r1, 0, :T], ALU.mult),
                 reads=[t_bank[ba], X.t_rope], writes=[B.t_tmp[k]])
            P.op("dve", lambda e: e.tensor_tensor(B.sg[k2][r0:r1, :T], banks[bb][r0:r1, :T], table[r0:r1, 1, :T], ALU.mult),
                 reads=[t_bank[bb], X.t_rope], writes=[B.t_sg[k2]])
            P.op("pool", lambda e: e.tensor_tensor(st[r0:r1, :T], B.tmp[k][r0:r1, :T], B.sg[k2][r0:r1, :T], ALU.add),
                 reads=[B.t_tmp[k], B.t_sg[k2]], writes=[])

        def latent_norm(chs, nfeat, buf, t_buf, gn, outn, t_outn):
            for j, ch in enumerate(chs):
                bi = fm_chunk(ch)
                P.op("act", lambda e, bi=bi, j=j: e.activation(buf[:, j, :T], banks[bi][:, :T], AF.Copy),
                     reads=[t_bank[bi]], writes=[t_buf])
            fm_rstd(B, T, [buf[:, j, :T] for j in range(len(chs))], nfeat, [t_buf], X.rstd2, X.t_rstd2)
            for j in range(len(chs)):
                P.op("dve", lambda e, j=j: e.scalar_tensor_tensor(
                    outn[:, j, :T], buf[:, j, :T], gn[:, j:j + 1], X.rstd2[:, :T], ALU.mult, ALU.mult),
                    reads=[t_buf, X.t_rstd2, X.t_small], writes=[t_outn])

        latent_norm([13, 14, 15], 384, X.cq, X.t_cq, X.qn, X.cqn, X.t_cqn)
        for h in range(NHM):
            ba = ring[nxt("ipb", 6)]
            mm_group(banks[ba][0:96, :T], [(X.wuq[:, 0, h, kc, :], X.cqn[:, kc, :T]) for kc in range(3)],
                     [X.t_cqn, X.t_small], [t_bank[ba]])
            st, tst = stage_bf()
            if is_ctx:
                P.op("act", lambda e, ba=ba, st=st: e.activation(st[0:96, :T], banks[ba][0:96, :T], AF.Copy),
                     reads=[t_bank[ba]], writes=[tst])
            else:
                bb = ring[nxt("ipb", 6)]
                mm_group(banks[bb][0:96, :T], [(X.wuq[:, 1, h, kc, :], X.cqn[:, kc, :T]) for kc in range(3)],
                         [X.t_cqn, X.t_small], [t_bank[bb]])
                P.op("act", lambda e, ba=ba, st=st: e.activation(st[0:64, :T], banks[ba][0:64, :T], AF.Copy),
                     reads=[t_bank[ba]], writes=[tst])
                k = nxt("tmp", 2)
                k2 = nxt("sg", 2)
                P.op("dve", lambda e, ba=ba, k=k: e.tensor_tensor(B.tmp[k][64:96, :T], banks[ba][64:96, :T],
                                                                  X.ropeM[64:96, 0, :T], ALU.mult),
                     reads=[t_bank[ba], X.t_rope], writes=[B.t_tmp[k]])
                P.op("dve", lambda e, bb=bb, k2=k2: e.tensor_tensor(B.sg[k2][64:96, :T], banks[bb][64:96, :T],
                                                                    X.ropeM[64:96, 1, :T], ALU.mult),
                     reads=[t_bank[bb], X.t_rope], writes=[B.t_sg[k2]])
                P.op("pool", lambda e, k=k, k2=k2, st=st: e.tensor_tensor(st[64:96, :T], B.tmp[k][64:96, :T],
                                                                           B.sg[k2][64:96, :T], ALU.add),
                     reads=[B.t_tmp[k], B.t_sg[k2]], writes=[tst])
            P.dma("act", QM[h, :, t0:t0 + T], st[0:96, :T], reads=[tst], writes=[ts])
        latent_norm([16, 17], 256, X.ckv, X.t_ckv, X.kvn, X.ckvn, X.t_ckvn)
        for j in range(4):
            ba = ring[nxt("ipb", 6)]
            mm_group(banks[ba][:, :T], [(X.wukvk[:, j, kc, :], X.ckvn[:, kc, :T]) for kc in range(2)],
                     [X.t_ckvn, X.t_small], [t_bank[ba]])
            st, tst = stage_bf()
            P.op("act", lambda e, ba=ba, st=st: e.activation(st[:, :T], banks[ba][:, :T], AF.Copy),
                 reads=[t_bank[ba]], writes=[tst])
            P.dma("act", KN[j * 128:(j + 1) * 128, t0:t0 + T], st[:, :T], reads=[tst], writes=[ts])
        for bl in range(nblk):
            ba = ring[nxt("ipb", 6)]
            mm_group(banks[ba][:, :512], [(X.ckvn[:, kc, bl * 128:(bl + 1) * 128], X.wukvv[:, kc, :]) for kc in range(2)],
                     [X.t_ckvn, X.t_small], [t_bank[ba]])
            k = nxt("vst", 2)
            P.op("act", lambda e, ba=ba, k=k: e.activation(
                X.vst[k][:, :, 0:64], banks[ba][:, :512].rearrange("p (h d) -> p h d", d=64), AF.Copy),
                reads=[t_bank[ba]], writes=[X.t_vst[k]])
            P.dma("act", VM[(t0 // 128) + bl], X.vst[k][:], reads=[X.t_vst[k]], writes=[ts])
        ba = fm_chunk(18, 96)
        st, tst = stage_bf()
        if is_ctx:
            P.op("act", lambda e, ba=ba, st=st: e.activation(st[64:96, :T], banks[ba][64:96, :T], AF.Copy),
                 reads=[t_bank[ba]], writes=[tst])
        else:
            bb = fm_chunk(19, 96)
            k = nxt("tmp", 2)
            k2 = nxt("sg", 2)
            P.op("dve", lambda e, ba=ba, k=k: e.tensor_tensor(B.tmp[k][64:96, :T], banks[ba][64:96, :T],
                                                              X.ropeM[64:96, 0, :T], ALU.mult),
                 reads=[t_bank[ba], X.t_rope], writes=[B.t_tmp[k]])
            P.op("dve", lambda e, bb=bb, k2=k2: e.tensor_tensor(B.sg[k2][64:96, :T], banks[bb][64:96, :T],
                                                                X.ropeM[64:96, 1, :T], ALU.mult),
                 reads=[t_bank[bb], X.t_rope], writes=[B.t_sg[k2]])
            P.op("pool", lambda e, k=k, k2=k2, st=st: e.tensor_tensor(st[64:96, :T], B.tmp[k][64:96, :T],
                                                                       B.sg[k2][64:96, :T], ALU.add),
                 reads=[B.t_tmp[k], B.t_sg[k2]], writes=[tst])
        P.dma("act", KR[:, t0:t0 + T], st[64:96, :T], reads=[tst], writes=[ts])
        for j in range(6):
            ch_a = 20 + j if j < 4 else 28 + (j - 4)
            ch_b = 24 + j if j < 4 else 30 + (j - 4)
            dst = QS[j] if j < 4 else KS[j - 4]
            ba = fm_chunk(ch_a)
            st, tst = stage_bf()
            if is_ctx:
                P.op("act", lambda e, ba=ba, st=st: e.activation(st[:, :T], banks[ba][:, :T], AF.Copy),
                     reads=[t_bank[ba]], writes=[tst])
            else:
                bb = fm_chunk(ch_b)
                k = nxt("tmp", 2)
                k2 = nxt("sg", 2)
                P.op("dve", lambda e, ba=ba, k=k: e.tensor_tensor(B.tmp[k][:, :T], banks[ba][:, :T],
                                                                  X.ropeS[:, 0, :T], ALU.mult),
                     reads=[t_bank[ba], X.t_rope], writes=[B.t_tmp[k]])
                P.op("dve", lambda e, bb=bb, k2=k2: e.tensor_tensor(B.sg[k2][:, :T], banks[bb][:, :T],
                                                                    X.ropeS[:, 1, :T], ALU.mult),
                     reads=[t_bank[bb], X.t_rope], writes=[B.t_sg[k2]])
                P.op("pool", lambda e, k=k, k2=k2, st=st: e.tensor_tensor(st[:, :T], B.tmp[k][:, :T],
                                                                           B.sg[k2][:, :T], ALU.add),
                     reads=[B.t_tmp[k], B.t_sg[k2]], writes=[tst])
            P.dma("act", dst[:, t0:t0 + T], st[:, :T], reads=[tst], writes=[ts])
        s_vs = load_w(B, win_bf[l][40], t_win[l])
        for bl in range(nblk):
            ba = ring[nxt("ipb", 6)]
            mm_group(banks[ba][:, :128], [(B.u[:, kc, bl * 128:(bl + 1) * 128], B.w[s_vs][:, kc, :]) for kc in range(DC)],
                     [B.t_w[s_vs]] + tu_all, [t_bank[ba]])
            k = nxt("vsst", 2)
            P.op("act", lambda e, ba=ba, k=k: e.activation(
                X.vsst[k][:, :, 0:64], banks[ba][:, :128].rearrange("p (g d) -> p g d", d=64), AF.Copy),
                reads=[t_bank[ba]], writes=[X.t_vsst[k]])
            P.dma("act", VS[(t0 // 128) + bl], X.vsst[k][:], reads=[X.t_vsst[k]], writes=[ts])
        for half in range(2):
            sz = [load_w(B, win_bf[l][32 + half * 4 + j], t_win[l]) for j in range(4)]
            for bl in range(nblk):
                ba = ring[nxt("ipb", 6)]
                for j in range(4):
                    mm_group(banks[ba][:, j * 128:(j + 1) * 128],
                             [(B.u[:, kc, bl * 128:(bl + 1) * 128], B.w[sz[j]][:, kc, :]) for kc in range(DC)],
                             [B.t_w[sz[j]]] + tu_all, [t_bank[ba]])
                st, tst = stage_bf()
                P.op("act", lambda e, ba=ba, st=st: e.activation(st[:, :512], banks[ba][:, :512], AF.Silu),
                     reads=[t_bank[ba]], writes=[tst])
                P.dma("act", SZ[(t0 // 128) + bl, :, half * 512:(half + 1) * 512], st[:, :512], reads=[tst], writes=[ts])

    def phase1(l):
        B = dense_alloc()
        X = Cfg()
        X.stb = [AR.alloc([128, TT], BF16) for _ in range(3)]
        X.t_stb = [Tok() for _ in range(3)]
        X.stf = AR.alloc([128, TT], F32)
        X.t_stf = Tok()
        X.ropeM = AR.alloc([128, 2, TT], F32)
        X.ropeS = AR.alloc([128, 2, TT], F32)
        X.t_rope = Tok()
        X.cq = AR.alloc([128, 3, TT], F32)
        X.t_cq = Tok()
        X.cqn = AR.alloc([128, 3, TT], BF16)
        X.t_cqn = Tok()
        X.ckv = AR.alloc([128, 2, TT], F32)
        X.t_ckv = Tok()
        X.ckvn = AR.alloc([128, 2, TT], BF16)
        X.t_ckvn = Tok()
        X.rstd2 = AR.alloc([128, TT], F32)
        X.t_rstd2 = Tok()
        X.vst = [AR.alloc([128, NHM, 65], BF16) for _ in range(2)]
        X.t_vst = [Tok(), Tok()]
        X.vsst = [AR.alloc([128, 2, 65], BF16) for _ in range(2)]
        X.t_vsst = [Tok(), Tok()]
        X.wuq = AR.alloc([128, 2, NHM, 3, 96], BF16) if False else None
        X.wuq = AR.alloc([128, 2 * NHM * 3 * 96], BF16).rearrange("p (v h k j) -> p v h k j", v=2, h=NHM, k=3)
        X.wukvk = AR.alloc([128, 4, 2, 128], BF16)
        X.wukvv = AR.alloc([128, 2, 512], BF16)
        X.qn = AR.alloc([128, 3], F32)
        X.kvn = AR.alloc([128, 2], F32)
        X.t_small = Tok()
        for v in range(2):
            P.dma("pool", X.wuq[:, v], wuq_in[l, v].rearrange("h p k j -> p h k j"), writes=[X.t_small])
        P.dma("pool", X.wukvk[:], wukvk_in[l].rearrange("c p k j -> p c k j"), writes=[X.t_small])
        P.dma("pool", X.wukvv[:], wukvv_in[l], writes=[X.t_small])
        P.dma("sp", X.qn[:], qn_in[l], writes=[X.t_small])
        P.dma("sp", X.kvn[:], kvn_in[l], writes=[X.t_small])
        for k in range(2):
            P.op("pool", lambda e, k=k: e.memset(X.vst[k][:], 1.0), writes=[X.t_vst[k]])
            P.op("pool", lambda e, k=k: e.memset(X.vsst[k][:], 1.0), writes=[X.t_vsst[k]])
        print("phase1 arena used", AR.off)
        for ti, (t0, T, is_ctx) in enumerate(c.tiles):
            col = 1 if is_ctx else 0
            P.dma("sp", B.hx[:, :, :T], hx_view[:, :, t0:t0 + T], reads=[t_hx_dram[ti]], writes=B.t_hx)
            modnorm(B, T, l, AV[l][:, 0], MOD[l][:, 0 * DC:1 * DC, :], col)
            ffn(B, T, l, 0, GV[l][:, 0], col)
            P.dma("act", hx_view[:, :, t0:t0 + T], B.hx[:, :, :T], reads=B.t_hx, writes=[t_hx_dram[ti]])
            modnorm(B, T, l, AV[l][:, 1], MOD[l][:, 3 * DC:4 * DC, :], col)
            inproj(B, X, T, l, ti, t0, is_ctx)
        P.barrier()

    def phase2(l):
        last = (l == L - 1)
        B = dense_alloc()
        print("phase2 arena used", AR.off)
        for ti, (t0, T, is_ctx) in enumerate(c.tiles):
            if last and is_ctx:
                continue
            col = 1 if is_ctx else 0
            P.dma("sp", B.hx[:, :, :T], hx_view[:, :, t0:t0 + T], reads=[t_hx_dram[ti]], writes=B.t_hx)
            P.dma("sp", B.u[:, :KM, :T], OX.rearrange("(c p) t -> p c t", p=128)[:, :, t0:t0 + T],
                  reads=[t_ox[ti]], writes=B.t_u[:KM])
            for dc in range(DC):
                s = load_w(B, wout_bf[l][dc], t_wout[l], K=KM)
                bo, to = banks[4 + dc % 2], t_bank[4 + dc % 2]
                mm_group(bo[:, :T], [(B.w[s][:, kc, :], B.u[:, kc, :T]) for kc in range(KM)],
                         [B.t_w[s]] + B.t_u[:KM], [to])
                P.op("dve", lambda e, dc=dc, bo=bo, T=T, col=col: e.scalar_tensor_tensor(
                    B.hx[:, dc, :T], bo[:, :T], GV[l][:, 1, dc, col:col + 1], B.hx[:, dc, :T], ALU.mult, ALU.add),
                    reads=[to, t_mod[l]], writes=[B.t_hx[dc]])
            modnorm(B, T, l, AV[l][:, 2], MOD[l][:, 6 * DC:7 * DC, :], col)
            ffn(B, T, l, 1, GV[l][:, 2], col)
            if not last:
                P.dma("act", hx_view[:, :, t0:t0 + T], B.hx[:, :, :T], reads=B.t_hx, writes=[t_hx_dram[ti]])
            else:
                fm_rstd(B, T, [B.hx[:, dc, :T] for dc in range(DC)], D, B.t_hx, B.rstd, B.t_rstd)
                for dc in range(DC):
                    P.op("dve", lambda e, dc=dc, T=T: e.scalar_tensor_tensor(
                        B.hx[:, dc, :T], B.hx[:, dc, :T], fin[:, dc:dc + 1], B.rstd[:, :T], ALU.mult, ALU.mult),
                        reads=[B.t_hx[dc], B.t_rstd, t_fin], writes=[B.t_hx[dc]])
                out_stores.append(P.dma("act", out_view[:, :, t0 - CTX:t0 - CTX + T], B.hx[:, :, :T],
                                        reads=B.t_hx, writes=[Tok()]))
        P.barrier()

    t_prep = [Tok(f"prep{b}") for b in range(NB)]
    t_yf = [Tok(f"yf{b}") for b in range(NB)]
    t_oxb = Tok("oxall")
    t_om = Tok("om")
    t_os = Tok("os")
    OXv = OX.rearrange("(c p) t -> p c t", p=128)

    def all_scr():
        return t_scr

    def mixer_prep(l):
        AR.reset()
        xin = [AR.alloc([128, 12, TT + 4], BF16) for _ in range(2)]
        t_xin = [Tok(), Tok()]
        acc = [AR.alloc([128, TT], F32) for _ in range(3)]
        t_acc = [Tok() for _ in range(3)]
        xc = AR.alloc([128, 12, TT], BF16)
        t_xc = [Tok() for _ in range(12)]
        xst = [AR.alloc([128, 1024], BF16) for _ in range(2)]
        t_xst = [Tok(), Tok()]
        bst = [AR.alloc([128, 2, 128], BF16) for _ in range(2)]
        t_bst = [Tok(), Tok()]
        dtin = AR.alloc([64, TT], F32)
        dte = AR.alloc([64, TT], F32)
        t_dt = Tok()
        dst = [AR.alloc([128, 64], F32) for _ in range(2)]
        t_dst = [Tok(), Tok()]
        cw = AR.alloc([128, 12, 5], F32)
        cb = AR.alloc([128, 12], F32)
        dtb = AR.alloc([64, 1], F32)
        aneg = AR.alloc([64, 1], F32)
        t_cw = Tok()
        P.dma("sp", cw[:], convw_in[l], writes=[t_cw])
        P.dma("sp", cb[:], convb_in[l], writes=[t_cw])
        P.dma("sp", dtb[:], dtb_in[l], writes=[t_cw])
        P.dma("sp", aneg[:], alog_in[l], writes=[t_cw])
        P.op("act", lambda e: e.activation(aneg[:], aneg[:], AF.Exp), reads=[t_cw], writes=[t_cw])
        P.op("dve", lambda e: e.tensor_scalar(aneg[:], aneg[:], -1.0, None, ALU.mult), reads=[t_cw], writes=[t_cw])
        print("prep arena used", AR.off)
        XBCv = XBC.rearrange("(c p) t -> p c t", p=128)
        for ti, (t0, T, is_ctx) in enumerate(c.tiles):
            seq0, seq1 = (0, CTX) if is_ctx else (CTX, NTOK)
            k = nxt("xin", 2)
            lo = max(seq0, t0 - 2)
            hi = min(seq1, t0 + T + 2)
            rd = [t_scr[ti]]
            if ti > 0:
                rd.append(t_scr[ti - 1])
            if ti + 1 < NT:
                rd.append(t_scr[ti + 1])
            if lo > t0 - 2:
                P.op("pool", lambda e, k=k: e.memset(xin[k][:, :, 0:2], 0.0), writes=[t_xin[k]])
            if hi < t0 + T + 2:
                P.op("pool", lambda e, k=k, T=T: e.memset(xin[k][:, :, T + 2:T + 4], 0.0), writes=[t_xin[k]])
            P.dma("sp", xin[k][:, :, lo - (t0 - 2):hi - (t0 - 2)], XBCv[:, :, lo:hi], reads=rd, writes=[t_xin[k]])
            for cc in range(12):
                a = nxt("acc", 3)
                P.op("act", lambda e, cc=cc, a=a, k=k, T=T: e.activation(
                    acc[a][:, :T], xin[k][:, cc, 0:T], AF.Identity, bias=cb[:, cc:cc + 1], scale=cw[:, cc, 0:1]),
                    reads=[t_xin[k], t_cw], writes=[t_acc[a]])
                for kk in range(1, 5):
                    eng = "dve"
                    P.op(eng, lambda e, cc=cc, a=a, k=k, kk=kk, T=T: e.scalar_tensor_tensor(
                        acc[a][:, :T], xin[k][:, cc, kk:kk + T], cw[:, cc, kk:kk + 1], acc[a][:, :T], ALU.mult, ALU.add),
                        reads=[t_xin[k], t_cw], writes=[t_acc[a]])
                P.op("act", lambda e, cc=cc, a=a, T=T: e.activation(xc[:, cc, :T], acc[a][:, :T], AF.Silu),
                     reads=[t_acc[a]], writes=[t_xc[cc]])
            P.dma("sp", dtin[0:32, :T], DTR[:, t0:t0 + T], reads=[t_scr[ti]], writes=[t_dt])
            P.dma("sp", dtin[32:64, :T], DTR[:, t0:t0 + T], reads=[t_scr[ti]], writes=[t_dt])
            P.op("act", lambda e, T=T: e.activation(dte[:, :T], dtin[:, :T], AF.Exp, bias=dtb[:, 0:1], scale=1.0),
                 reads=[t_dt, t_cw], writes=[t_dt])
            P.op("act", lambda e, T=T: e.activation(dtin[:, :T], dte[:, :T], AF.Ln, bias=one_ap[0:64, 0:1], scale=1.0),
                 reads=[t_dt, t_eps], writes=[t_dt])
            P.op("dve", lambda e, T=T: e.tensor_scalar(dtin[32:64, :T], dtin[32:64, :T], aneg[32:64, 0:1], None, ALU.mult),
                 reads=[t_dt, t_cw], writes=[t_dt])
            for bl in range(T // 128):
                blk = t0 // 128 + bl
                cs = slice(bl * 128, (bl + 1) * 128)
                bi = nxt("tpb", 4)
                P.op("pe", lambda e, bi=bi, cs=cs: [e.transpose(banks_b[bi][:, j * 128:(j + 1) * 128], xc[:, j, cs], IDENT_B)
                                                    for j in range(8)][-1],
                     reads=t_xc[0:8] + [t_cst], writes=[t_bank[bi]])
                ks = nxt("xst", 2)
                P.op("act", lambda e, bi=bi, ks=ks: e.activation(xst[ks][:], banks_b[bi][:, :1024], AF.Copy),
                     reads=[t_bank[bi]], writes=[t_xst[ks]])
                P.dma("act", XS[blk], xst[ks][:], reads=[t_xst[ks]], writes=[t_prep[blk]])
                bi = nxt("tpb", 4)
                P.op("pe", lambda e, bi=bi, cs=cs: [e.transpose(banks_b[bi][:, j * 128:(j + 1) * 128], xc[:, 8 + j, cs], IDENT_B)
                                                    for j in range(2)][-1],
                     reads=t_xc[8:10] + [t_cst], writes=[t_bank[bi]])
                kb = nxt("bst", 2)
                P.op("dve", lambda e, bi=bi, kb=kb: e.tensor_copy(
                    bst[kb][:], banks_b[bi][:, :256].rearrange("p (g n) -> p g n", n=128)),
                    reads=[t_bank[bi]], writes=[t_bst[kb]])
                P.dma("act", BTM[blk], bst[kb][:], reads=[t_bst[kb]], writes=[t_prep[blk]])
                P.dma("act", BCT[blk], xc[:, 8:12, cs], reads=t_xc[8:12], writes=[t_prep[blk]])
                bi = 4 + nxt("tpf", 2)
                P.op("pe", lambda e, bi=bi, cs=cs: e.transpose(banks[bi][:, 0:64], dtin[:, cs], IDENT_F[0:64, 0:64]),
                     reads=[t_dt, t_cst], writes=[t_bank[bi]])
                kd = nxt("dst", 2)
                P.op("dve", lambda e, bi=bi, kd=kd: e.tensor_copy(dst[kd][:], banks[bi][:, 0:64]),
                     reads=[t_bank[bi]], writes=[t_dst[kd]])
                P.dma("act", DTA[blk], dst[kd][:], reads=[t_dst[kd]], writes=[t_prep[blk]])
        P.barrier()

    def ssd_scan(l):
        last = (l == L - 1)
        AR.reset()
        NBUF = 2
        xs = [AR.alloc([128, 16, 64], BF16) for _ in range(NBUF)]
        btm = [AR.alloc([128, 2, 128], BF16) for _ in range(NBUF)]
        bct = [AR.alloc([128, 4, 128], BF16) for _ in range(NBUF)]
        dta = [AR.alloc([128, 64], F32) for _ in range(NBUF)]
        t_ld = [Tok() for _ in range(NBUF)]
        H = AR.alloc([128, 2, 512], F32)
        Hb = AR.alloc([128, 2, 512], BF16)
        t_H = Tok()
        t_Hb = Tok()
        acs = AR.alloc([128, 16], F32)
        wv = AR.alloc([128, 16], F32)
        dif = AR.alloc([128, 16], F32)
        fv = AR.alloc([128, 16], F32)
        cdv = AR.alloc([128, 16], F32)
        t_sm = Tok()
        xdt = AR.alloc([128, 16, 64], BF16)
        xdte = AR.alloc([128, 16, 64], BF16)
        t_xdt = Tok()
        t_xdte = Tok()
        cbm = AR.alloc([128, 2, 128], F32)
        t_cbm = Tok()
        NR = 4
        lh = [AR.alloc([128, 128], F32) for _ in range(NR)]
        t_lh = [Tok() for _ in range(NR)]
        eh = [AR.alloc([128, 128], F32) for _ in range(NR)]
        t_eh = [Tok() for _ in range(NR)]
        mh = [AR.alloc([128, 128], BF16) for _ in range(NR)]
        t_mh = [Tok() for _ in range(NR)]
        yo = AR.alloc([128, 16, 64], F32)
        t_yo = Tok()
        yst = [AR.alloc([128, 1024], F32) for _ in range(2)]
        t_yst = [Tok(), Tok()]
        yfl = AR.alloc([128, 1024], F32)
        t_yfl = Tok()
        szl = AR.alloc([128, 1024], BF16)
        t_szl = Tok()
        ysq = AR.alloc([128, 1024], F32)
        ms = AR.alloc([128, 2], F32)
        t_ms = Tok()
        ybf = AR.alloc([128, 1024], BF16)
        t_ybf = Tok()
        oxs = [AR.alloc([128, 8, 128], BF16) for _ in range(2)]
        t_oxs = [Tok(), Tok()]
        dsk = AR.alloc([128, 16], F32)
        sng = AR.alloc([128, 1024], F32)
        t_cn = Tok()
        P.dma("sp", dsk[:], dsk_in[l], writes=[t_cn])
        P.dma("sp", sng[:], sn_in[l], writes=[t_cn])
        print("ssd arena used", AR.off)
        for d in range(2):
            T_d = T_F if d == 0 else T_B
            U_d = U_F if d == 0 else U_B
            TBm = T_F if d == 0 else T_B
            ctx_order = list(range(NBC)) if d == 0 else list(range(NBC - 1, -1, -1))
            lat_order = list(range(NBC, NB)) if d == 0 else list(range(NB - 1, NBC - 1, -1))
            P.op("pool", lambda e: e.memset(H[:], 0.0), writes=[t_H])
            P.op("pool", lambda e: e.memset(Hb[:], 0.0), writes=[t_Hb])
            for blk in ctx_order + lat_order:
                is_ctx = blk < NBC
                need_y = not (last and is_ctx)
                k = nxt("sld", NBUF)
                P.dma("sp", xs[k][:], XS[blk].rearrange("p (h d) -> p h d", d=64), reads=[t_prep[blk]], writes=[t_ld[k]])
                P.dma("sp", btm[k][:], BTM[blk], reads=[t_prep[blk]], writes=[t_ld[k]])
                P.dma("sp", bct[k][:], BCT[blk], reads=[t_prep[blk]], writes=[t_ld[k]])
                P.dma("sp", dta[k][:], DTA[blk], reads=[t_prep[blk]], writes=[t_ld[k]])
                dtv = dta[k][:, d * 16:(d + 1) * 16]
                av = dta[k][:, 32 + d * 16:32 + (d + 1) * 16]
                b7, t7 = banks[7], t_bank[7]
                mm_group(b7[:, 0:16], [(T_d, av)], [t_cst, t_ld[k]], [t7])
                mm_group(b7[:, 16:32], [(ONES_F, av)], [t_cst, t_ld[k]], [t7])
                P.op("act", lambda e, b7=b7: e.activation(acs[:], b7[:, 0:16], AF.Copy), reads=[t7], writes=[t_sm])
                P.op("act", lambda e, b7=b7: e.activation(wv[:], b7[:, 0:16], AF.Exp), reads=[t7], writes=[t_sm])
                P.op("act", lambda e, b7=b7: e.activation(cdv[:], b7[:, 16:32], AF.Exp), reads=[t7], writes=[t_sm])
                P.op("dve", lambda e, b7=b7: e.tensor_tensor(dif[:], b7[:, 16:32], acs[:], ALU.subtract),
                     reads=[t7, t_sm], writes=[t_sm])
                P.op("act", lambda e: e.activation(dif[:], dif[:], AF.Exp), reads=[t_sm], writes=[t_sm])
                P.op("dve", lambda e, dtv=dtv: e.tensor_tensor(fv[:], dif[:], dtv, ALU.mult),
                     reads=[t_sm, t_ld[k]], writes=[t_sm])
                P.op("dve", lambda e, k=k, dtv=dtv: e.tensor_tensor(
                    xdt[:], xs[k][:], dtv.unsqueeze(2).to_broadcast([128, 16, 64]), ALU.mult),
                    reads=[t_ld[k]], writes=[t_xdt])
                P.op("pool", lambda e, k=k: e.tensor_tensor(
                    xdte[:], xs[k][:], fv[:].unsqueeze(2).to_broadcast([128, 16, 64]), ALU.mult),
                    reads=[t_ld[k], t_sm], writes=[t_xdte])
                for g in range(2):
                    if need_y:
                        mm_group(banks[6][:, g * 128:(g + 1) * 128], [(bct[k][:, g, :], bct[k][:, 2 + g, :])],
                                 [t_ld[k]], [t_bank[6]])
                        mm_group(banks[4 + g][:, :512], [(bct[k][:, 2 + g, :], Hb[:, g, :])], [t_ld[k], t_Hb],
                                 [t_bank[4 + g]])
                if need_y:
                    P.op("dve", lambda e, TBm=TBm: e.tensor_tensor(
                        cbm[:], banks[6][:, 0:256].rearrange("p (g n) -> p g n", n=128),
                        TBm.unsqueeze(1).to_broadcast([128, 2, 128]), ALU.mult),
                        reads=[t_bank[6], t_cst], writes=[t_cbm])
                    for g in range(2):
                        P.op("dve", lambda e, g=g: e.tensor_tensor(
                            yo[:, g * 8:(g + 1) * 8, :], banks[4 + g][:, :512].rearrange("p (h d) -> p h d", d=64),
                            wv[:, g * 8:(g + 1) * 8].unsqueeze(2).to_broadcast([128, 8, 64]), ALU.mult),
                            reads=[t_bank[4 + g], t_sm], writes=[t_yo])
                for g in range(2):
                    mm_group(banks[6][:, :512], [(btm[k][:, g, :], xdte[:, g * 8:(g + 1) * 8, :].rearrange("p h d -> p (h d)"))],
                             [t_ld[k], t_xdte], [t_bank[6]])
                    P.op("pool", lambda e, g=g: e.tensor_tensor(
                        H[:, g, :].rearrange("p (h d) -> p h d", d=64), H[:, g, :].rearrange("p (h d) -> p h d", d=64),
                        cdv[:, g * 8:(g + 1) * 8].unsqueeze(2).to_broadcast([128, 8, 64]), ALU.mult),
                        reads=[t_sm], writes=[t_H])
                    P.op("dve", lambda e, g=g: e.tensor_tensor(H[:, g, :], H[:, g, :], banks[6][:, :512], ALU.add),
                         reads=[t_bank[6]], writes=[t_H])
                    P.op("act", lambda e, g=g: e.activation(Hb[:, g, :], H[:, g, :], AF.Copy), reads=[t_H], writes=[t_Hb])
                if not need_y:
                    continue
                ky = nxt("yst", 2)
                Y, tY = yst[ky], t_yst[ky]
                rs_ = {}

                def stage_a(h, k=k, av=av, U_d=U_d, T_d=T_d):
                    g = h // 8
                    r = nxt("lh", NR)
                    rs_[h] = r
                    P.op("dve" if h % 2 == 0 else "pool", lambda e, r=r, h=h, av=av, U_d=U_d: e.tensor_scalar(
                        lh[r][:], U_d, av[:, h:h + 1], None, ALU.mult), reads=[t_cst, t_ld[k]], writes=[t_lh[r]])
                    sl = nxt("dsl", 2)
                    dps = banks[sl][:, 0:128]
                    mm_group(dps, [(lh[r][:], T_d)], [t_lh[r], t_cst], [t_bank[sl]])
                    P.op("act", lambda e, r=r, dps=dps: e.activation(eh[r][:], dps, AF.Exp),
                         reads=[t_bank[sl]], writes=[t_eh[r]])
                    P.op("dve", lambda e, r=r, g=g: e.tensor_tensor(mh[r][:], eh[r][:], cbm[:, g, :], ALU.mult),
                         reads=[t_eh[r], t_cbm], writes=[t_mh[r]])

                def stage_b(h, Y=Y, tY=tY):
                    r = rs_[h]
                    ys = 2 + nxt("ysl", 2)
                    yps = banks[ys][:, 0:64]
                    mm_group(yps, [(mh[r][:], xdt[:, h, :])], [t_mh[r], t_xdt], [t_bank[ys]])
                    P.op("dve", lambda e, h=h, yps=yps, Y=Y: e.tensor_tensor(
                        Y[:, h * 64:(h + 1) * 64], yps, yo[:, h, :], ALU.add),
                        reads=[t_bank[ys], t_yo], writes=[tY])
                stage_a(0)
                stage_a(1)
                for h in range(16):
                    stage_b(h)
                    if h + 2 < 16:
                        stage_a(h + 2)
                if d == 0:
                    P.dma("act", YF[blk], Y[:], reads=[tY], writes=[t_yf[blk]])
                    continue
                P.dma("sp", yfl[:], YF[blk], reads=[t_yf[blk]], writes=[t_yfl])
                P.dma("sp", szl[:], SZ[blk], reads=t_scr, writes=[t_szl])
                P.op("dve", lambda e, Y=Y: e.tensor_tensor(Y[:], Y[:], yfl[:], ALU.add), reads=[t_yfl], writes=[tY])
                P.op("pool", lambda e, k=k: e.tensor_tensor(
                    ysq[:].rearrange("p (h d) -> p h d", d=64), xs[k][:],
                    dsk[:].unsqueeze(2).to_broadcast([128, 16, 64]), ALU.mult),
                    reads=[t_ld[k], t_cn], writes=[t_ms])
                P.op("dve", lambda e, Y=Y: e.tensor_tensor(Y[:], Y[:], ysq[:], ALU.add), reads=[t_ms], writes=[tY])
                P.op("dve", lambda e, Y=Y: e.tensor_tensor(Y[:], Y[:], szl[:], ALU.mult), reads=[t_szl], writes=[tY])
                for g in range(2):
                    P.op("act", lambda e, g=g, Y=Y: e.activation(
                        ysq[:, g * 512:(g + 1) * 512], Y[:, g * 512:(g + 1) * 512], AF.Square, accum_out=ms[:, g:g + 1]),
                        reads=[tY], writes=[t_ms])
                P.op("act", lambda e: e.activation(ms[:], ms[:], AF.Sqrt, bias=eps_ap[:, 0:1], scale=1.0 / 512),
                     reads=[t_ms, t_eps], writes=[t_ms])
                P.op("dve", lambda e: e.reciprocal(ms[:], ms[:]), reads=[t_ms], writes=[t_ms])
                for g in range(2):
                    P.op("dve", lambda e, g=g, Y=Y: e.scalar_tensor_tensor(
                        ybf[:, g * 512:(g + 1) * 512], Y[:, g * 512:(g + 1) * 512], ms[:, g:g + 1],
                        sng[:, g * 512:(g + 1) * 512], ALU.mult, ALU.mult),
                        reads=[tY, t_ms, t_cn], writes=[t_ybf])
                bi = 7
                P.op("pe", lambda e, bi=bi: [e.transpose(banks_b[bi][:, j * 128:(j + 1) * 128],
                                                        ybf[:, j * 128:(j + 1) * 128], IDENT_B) for j in range(8)][-1],
                     reads=[t_ybf, t_cst], writes=[t_bank[bi]])
                ko = nxt("oxs", 2)
                P.op("act", lambda e, bi=bi, ko=ko: e.activation(
                    oxs[ko][:], banks_b[bi][:, :1024].rearrange("p (c t) -> p c t", t=128), AF.Copy),
                    reads=[t_bank[bi]], writes=[t_oxs[ko]])
                P.dma("act", OXv[:, 0:8, blk * 128:(blk + 1) * 128], oxs[ko][:], reads=[t_oxs[ko]], writes=[t_oxb])
        P.barrier()

    t_dsl = [Tok() for _ in range(8)]
    t_ysl = [Tok() for _ in range(6)]

    def mla_attn(l):
        last = (l == L - 1)
        AR.reset()
        Kh = [AR.alloc([96, NTOK], BF16) for _ in range(2)]
        Vh = [AR.alloc([128, NB, 65], BF16) for _ in range(2)]
        t_kv = [Tok(), Tok()]
        Qt = [AR.alloc([96, TT], BF16) for _ in range(2)]
        t_q = [Tok(), Tok()]
        NE = 4
        E = [AR.alloc([128, TT], BF16) for _ in range(NE)]
        t_E = [Tok() for _ in range(NE)]
        Ost = [AR.alloc([65, TT], F32) for _ in range(2)]
        t_ost = [Tok(), Tok()]
        print("mla arena used", AR.off)
        if l + 1 < L:
            emit_casts(l + 1)
        for h in range(NHM):
            kk = nxt("mkv", 2)
            P.dma("sp", Kh[kk][0:64, :], KN[h * 64:(h + 1) * 64, :], reads=t_scr, writes=[t_kv[kk]])
            P.dma("sp", Kh[kk][64:96, :], KR[:, :], reads=t_scr, writes=[t_kv[kk]])
            P.dma("sp", Vh[kk][:], VM[:, :, h, :].rearrange("b p d -> p b d"), reads=t_scr, writes=[t_kv[kk]])
            for ti, (t0, T, is_ctx) in enumerate(c.tiles):
                if is_ctx and last:
                    continue
                kq = nxt("mq", 2)
                P.dma("sp", Qt[kq][:, :T], QM[h, :, t0:t0 + T], reads=[t_scr[ti]], writes=[t_q[kq]])
                kblocks = list(range(NBC)) if is_ctx else list(range(NB))
                ob = 6 + nxt("mob", 2)
                def qk_exp(kb, kk=kk, kq=kq, T=T):
                    sb_ = nxt("msb", 4)
                    mm_group(banks[sb_][:, :T], [(Kh[kk][:, kb * 128:(kb + 1) * 128], Qt[kq][:, :T])],
                             [t_kv[kk], t_q[kq]], [t_bank[sb_]])
                    ke = nxt("me", NE)
                    P.op("act", lambda e, sb_=sb_, ke=ke, T=T: e.activation(E[ke][:, :T], banks[sb_][:, :T], AF.Exp,
                                                                            scale=SC_MLA),
                         reads=[t_bank[sb_]], writes=[t_E[ke]])
                    return ke
                nk = len(kblocks)
                kes = [qk_exp(kblocks[0])]
                for i, kb in enumerate(kblocks):
                    if i + 1 < nk:
                        kes.append(qk_exp(kblocks[i + 1]))
                    ke = kes[i]

                    def pv(e, ke=ke, kb=kb, ob=ob, T=T, first=(i == 0), lastk=(i == nk - 1), kk=kk):
                        return e.matmul(banks[ob][0:65, :T], Vh[kk][:, kb, :], E[ke][:, :T], start=first, stop=lastk)
                    P.op("pe", pv, reads=[t_E[ke], t_kv[kk]], writes=[t_bank[ob]])
                ko = nxt("most", 2)
                P.op("dve", lambda e, ob=ob, ko=ko, T=T: e.tensor_copy(Ost[ko][:, :T], banks[ob][0:65, :T]),
                     reads=[t_bank[ob]], writes=[t_ost[ko]])
                P.dma("act", OM[h, :, t0:t0 + T], Ost[ko][:, :T], reads=[t_ost[ko]], writes=[t_om])
        if l + 1 < L:
            emit_mod(l + 1)
        P.barrier()

    def swa_attn(l):
        last = (l == L - 1)
        AR.reset()
        Kg = [AR.alloc([128, NTOK], BF16) for _ in range(2)]
        Vg = [AR.alloc([128, NB, 65], BF16) for _ in range(2)]
        t_kv = [Tok(), Tok()]
        Qb = [AR.alloc([128, 2, 128], BF16) for _ in range(2)]
        t_q = [Tok(), Tok()]
        NE = 4
        E = [AR.alloc([128, 256], BF16) for _ in range(NE)]
        t_E = [Tok() for _ in range(NE)]
        Ost = [AR.alloc([65, 4, 128], F32) for _ in range(2)]
        t_ost = [Tok(), Tok()]
        print("swa arena used", AR.off)
        for g in range(2):
            kk = nxt("skv", 2)
            P.dma("sp", Kg[kk][:], KS[g], reads=t_scr, writes=[t_kv[kk]])
            P.dma("sp", Vg[kk][:], VS[:, :, g, :].rearrange("b p d -> p b d"), reads=t_scr, writes=[t_kv[kk]])
            for qb in range(NB):
                is_ctx = qb < NBC
                if is_ctx and last:
                    continue
                kq = nxt("sq", 2)
                P.dma("sp", Qb[kq][:], QS[2 * g:2 * g + 2, :, qb * 128:(qb + 1) * 128].rearrange("c p t -> p c t"),
                      reads=t_scr, writes=[t_q[kq]])
                if is_ctx:
                    kbs = [(kb, None) for kb in range(NBC)]
                else:
                    kbs = [(kb, None) for kb in range(NBC)]
                    if qb - 1 >= NBC:
                        kbs.append((qb - 1, TB_B))
                    kbs.append((qb, None))
                    if qb + 1 < NB:
                        kbs.append((qb + 1, TB_F))
                ko = nxt("sost", 2)
                for half in range(2):
                    rows = slice(half * 64, (half + 1) * 64)
                    ob = 6 + half
                    def qk_exp(i, rows=rows, kk=kk, kq=kq):
                        kb, msk = kbs[i]
                        sb_ = nxt("ssb", 4)
                        mm_group(banks[sb_][:, :256], [(Kg[kk][rows, kb * 128:(kb + 1) * 128],
                                                        Qb[kq][rows, :, :].rearrange("p c t -> p (c t)"))],
                                 [t_kv[kk], t_q[kq]], [t_bank[sb_]])
                        ke = nxt("se", NE)
                        P.op("act", lambda e, sb_=sb_, ke=ke: e.activation(E[ke][:, :256], banks[sb_][:, :256], AF.Exp,
                                                                           scale=SC_SWA),
                             reads=[t_bank[sb_]], writes=[t_E[ke]])
                        if msk is not None:
                            P.op("dve", lambda e, ke=ke, msk=msk: e.tensor_tensor(
                                E[ke][:, :256].rearrange("p (c t) -> p c t", t=128),
                                E[ke][:, :256].rearrange("p (c t) -> p c t", t=128),
                                msk.unsqueeze(1).to_broadcast([128, 2, 128]), ALU.mult),
                                reads=[t_cst], writes=[t_E[ke]])
                        return ke
                    nk = len(kbs)
                    kes = [qk_exp(0)]
                    for i, (kb, msk) in enumerate(kbs):
                        if i + 1 < nk:
                            kes.append(qk_exp(i + 1))
                        ke = kes[i]
                        for j in range(2):
                            obj = 4 + 2 * half + j
                            def pv(e, ke=ke, kb=kb, obj=obj, j=j, first=(i == 0), lastk=(i == nk - 1), kk=kk):
                                return e.matmul(banks[obj][0:65, 0:128], Vg[kk][:, kb, :],
                                                E[ke][:, j * 128:(j + 1) * 128], start=first, stop=lastk)
                            P.op("pe", pv, reads=[t_E[ke], t_kv[kk]], writes=[t_bank[obj]])
                    for j in range(2):
                        obj = 4 + 2 * half + j
                        P.op("dve", lambda e, obj=obj, ko=ko, half=half, j=j: e.tensor_copy(
                            Ost[ko][:, 2 * half + j, :], banks[obj][0:65, 0:128]),
                            reads=[t_bank[obj]], writes=[t_ost[ko]])
                for half in range(2):
                    for j in range(2):
                        hd = 2 * (2 * g + j) + half
                        P.dma("act", OS[hd, :, qb * 128:(qb + 1) * 128], Ost[ko][:, 2 * half + j, :],
                              reads=[t_ost[ko]], writes=[t_os])
        P.barrier()

    def attn_epilogue(l):
        last = (l == L - 1)
        AR.reset()
        num = AR.alloc([64, 8, TT], F32)
        den = AR.alloc([64, 8, TT], F32)
        sq = AR.alloc([64, 8, TT], BF16)
        ob = AR.alloc([64, 8, TT], BF16)
        rs = AR.alloc([128, TT], F32)
        gn = AR.alloc([64, 2, 8], F32)
        snk = AR.alloc([64, 8], F32)
        t_n, t_d, t_s, t_o, t_r, t_g = Tok(), Tok(), Tok(), Tok(), Tok(), Tok()
        P.dma("sp", gn[:, 0, :], mon_in[l], writes=[t_g])
        P.dma("sp", gn[:, 1, :], son_in[l], writes=[t_g])
        P.dma("sp", snk[:], sink_in[l], writes=[t_g])
        P.op("act", lambda e: e.activation(snk[:], snk[:], AF.Exp), reads=[t_g], writes=[t_g])
        print("epi arena used", AR.off)
        for ti, (t0, T, is_ctx) in enumerate(c.tiles):
            if is_ctx and last:
                continue
            for which, (SRC, t_src, row0) in enumerate(((OM, t_om, 1024), (OS, t_os, 1536))):
                P.dma("sp", num[:, :, :T], SRC[:, 0:64, t0:t0 + T].rearrange("h d t -> d h t"), reads=[t_src], writes=[t_n])
                for h in range(8):
                    P.dma("sp", den[:, h, :T], SRC[h, 64:65, t0:t0 + T].partition_broadcast(64), reads=[t_src], writes=[t_d])
                if which == 1:
                    P.op("dve", lambda e, T=T: e.tensor_tensor(den[:, :, :T], den[:, :, :T],
                                                               snk[:].unsqueeze(2).to_broadcast([64, 8, T]), ALU.add),
                         reads=[t_g], writes=[t_d])
                P.op("dve", lambda e, T=T: e.reciprocal(den[:, :, :T], den[:, :, :T]), reads=[], writes=[t_d])
                P.op("dve", lambda e, T=T: e.tensor_tensor(num[:, :, :T], num[:, :, :T], den[:, :, :T], ALU.mult),
                     reads=[t_d], writes=[t_n])
                P.op("act", lambda e, T=T: e.activation(sq[:, :, :T], num[:, :, :T], AF.Square), reads=[t_n], writes=[t_s])
                bk, tb = banks[which], t_bank[which]
                mm_group(bk[:, :T], [(ones_bf[0:64, :], sq[:, h, :T]) for h in range(8)], [t_s, t_cst], [tb])
                P.op("act", lambda e, T=T, bk=bk: e.activation(rs[:, :T], bk[:, :T], AF.Sqrt, bias=eps_ap[:, 0:1],
                                                              scale=1.0 / 512), reads=[tb, t_eps], writes=[t_r])
                P.op("dve", lambda e, T=T: e.reciprocal(rs[:, :T], rs[:, :T]), reads=[t_r], writes=[t_r])
                for h in range(8):
                    P.op("dve", lambda e, T=T, h=h, which=which: e.scalar_tensor_tensor(
                        ob[:, h, :T], num[:, h, :T], gn[:, which, h:h + 1], rs[0:64, :T], ALU.mult, ALU.mult),
                        reads=[t_n, t_r, t_g], writes=[t_o])
                P.dma("act", OX[row0:row0 + 512, t0:t0 + T].rearrange("(h d) t -> d h t", d=64), ob[:, :, :T],
                      reads=[t_o], writes=[t_oxb])
        P.barrier()

    for l in range(L):
        phase1(l)
        mixer_prep(l)
        ssd_scan(l)
        mla_attn(l)
        swa_attn(l)
        attn_epilogue(l)
        for ti in range(NT):
            t_ox[ti].w = t_oxb.w
        phase2(l)

    fin_idx = P.op("sp", None)
    P.ops[fin_idx].deps = set(out_stores)
    P.emit(es)
    return nc, es


def _rope_perm(dim):
    h, q = dim // 2, dim // 4
    perm = np.zeros(dim, np.int64)
    sign = np.zeros(dim, np.float32)
    for i in range(dim):
        base = 0 if i < h else h
        j = i - base
        if j < q:
            perm[i], sign[i] = base + j + q, -1.0
        else:
            perm[i], sign[i] = base + j - q, 1.0
    return perm, sign


def _rope_tables(seq, dim):
    rows = seq // GRID_W
    pos_row = np.repeat(np.arange(rows), GRID_W).astype(np.float32)
    pos_col = np.tile(np.arange(GRID_W), rows).astype(np.float32)
    quarter = dim // 4
    inv_freq = (ROPE_THETA ** (-np.arange(quarter, dtype=np.float32) / quarter)).astype(np.float32)
    ang_r = pos_row[:, None] * inv_freq[None, :]
    ang_c = pos_col[:, None] * inv_freq[None, :]
    ang = np.concatenate([ang_r, ang_r, ang_c, ang_c], axis=-1).astype(np.float32)
    _, sign = _rope_perm(dim)
    return np.stack([np.cos(ang).T, (np.sin(ang) * sign[None, :]).T], axis=0).astype(np.float32)


def _prep_shared(inp, c):
    D, FF, DC, FC, NH, FCH, L = c.D, c.FF, c.DC, c.FC, c.NH, c.FCH, c.L
    f = np.float32
    sh = {}
    wm = np.asarray(inp["w_mod"], f).reshape(L, DC, 128, 9 * DC, 128)
    sh["wmod_t"] = np.ascontiguousarray(wm.transpose(0, 3, 2, 1, 4))
    sh["bmod_t"] = np.ascontiguousarray(np.asarray(inp["b_mod"], f).reshape(L, 9 * DC, 128).transpose(0, 2, 1))
    sh["ng_t"] = np.ascontiguousarray(np.asarray(inp["norm_g"], f).reshape(L, 3, DC, 128).transpose(0, 3, 1, 2))
    wg = np.asarray(inp["ffn_w_gate"], f).reshape(L, 2, DC, 128, FC, 128)
    wu = np.asarray(inp["ffn_w_up"], f).reshape(L, 2, DC, 128, FC, 128)
    wgu = np.stack([wg, wu], axis=0)
    sh["wgu_t"] = np.ascontiguousarray(wgu.transpose(1, 2, 5, 0, 4, 3, 6))
    wd = np.asarray(inp["ffn_w_down"], f).reshape(L, 2, NH, FCH, 128, DC, 128)
    sh["wd_t"] = np.ascontiguousarray(wd.transpose(0, 1, 2, 5, 4, 3, 6))
    sh["fin_t"] = np.ascontiguousarray(np.asarray(inp["final_norm"], f).reshape(DC, 128).T)
    w_in = np.asarray(inp["w_in"], f)
    pM, _ = _rope_perm(32)
    pS, _ = _rope_perm(64)
    cols = -np.ones((NCH_WIN, 128), np.int64)
    j = np.arange(128)
    for ch in range(12):
        cols[ch] = 1024 + 128 * ch + j
    cols[12, :32] = 2560 + np.arange(32)
    for i in range(3):
        cols[13 + i] = 2592 + 128 * i + j
    for i in range(2):
        cols[16 + i] = 2976 + 128 * i + j
    cols[18, 64:96] = 3232 + np.arange(32)
    cols[19, 64:96] = 3232 + pM
    for i in range(4):
        cols[20 + i] = 3264 + 128 * i + j
        cols[24 + i] = 3264 + 128 * i + 64 * (j // 64) + pS[j % 64]
    for g in range(2):
        cols[28 + g] = 3776 + 64 * g + (j % 64)
        cols[30 + g] = 3776 + 64 * g + pS[j % 64]
    for i in range(8):
        cols[32 + i] = 128 * i + j
    cols[40] = 3904 + j
    valid = cols >= 0
    wsel = w_in[:, :, np.where(valid, cols, 0).reshape(-1)].reshape(L, DC, 128, NCH_WIN, 128)
    wsel = wsel * valid[None, None, None, :, :].astype(f)
    sh["win_t"] = np.ascontiguousarray(wsel.transpose(0, 3, 2, 1, 4))
    wo = np.asarray(inp["w_out"], f).reshape(L, KM, 128, DC, 128)
    sh["wout_t"] = np.ascontiguousarray(wo.transpose(0, 3, 2, 1, 4))
    wuq = np.asarray(inp["mla_w_uq"], f).reshape(L, 3, 128, NHM, 96)
    jj = np.arange(96)
    permj = np.where(jj < 64, jj, 64 + pM[np.clip(jj - 64, 0, 31)])
    wuq2 = np.stack([wuq, wuq[..., permj]], axis=1)
    sh["wuq_t"] = np.ascontiguousarray(wuq2.transpose(0, 1, 4, 3, 2, 5))
    wukv = np.asarray(inp["mla_w_ukv"], f).reshape(L, 2, 128, NHM, 128)
    wk = wukv[..., :64].reshape(L, 2, 128, 4, 128)
    sh["wukvk_t"] = np.ascontiguousarray(wk.transpose(0, 3, 2, 1, 4))
    wv = wukv[..., 64:].reshape(L, 2, 128, 512)
    sh["wukvv_t"] = np.ascontiguousarray(wv.transpose(0, 2, 1, 3))
    sh["qn_t"] = np.ascontiguousarray(np.asarray(inp["mla_q_norm"], f).reshape(L, 3, 128).transpose(0, 2, 1))
    sh["kvn_t"] = np.ascontiguousarray(np.asarray(inp["mla_kv_norm"], f).reshape(L, 2, 128).transpose(0, 2, 1))
    cw = np.asarray(inp["ssd_conv_w"], f).reshape(L, 5, 12, 128)
    sh["convw_t"] = np.ascontiguousarray(cw.transpose(0, 3, 2, 1))
    sh["convb_t"] = np.ascontiguousarray(np.asarray(inp["ssd_conv_b"], f).reshape(L, 12, 128).transpose(0, 2, 1))
    dtb = np.asarray(inp["ssd_dt_bias"], f).reshape(L, 32)
    sh["dtb_t"] = np.ascontiguousarray(np.concatenate([dtb, dtb], axis=1)[:, :, None])
    al = np.asarray(inp["ssd_a_log"], f).reshape(L, 32)
    sh["alog_t"] = np.ascontiguousarray(np.concatenate([al, al], axis=1)[:, :, None])
    sh["dsk_t"] = np.ascontiguousarray(np.broadcast_to(np.asarray(inp["ssd_d"], f)[:, None, :], (L, 128, 16)))
    sh["sn_t"] = np.ascontiguousarray(np.broadcast_to(np.asarray(inp["ssd_norm"], f)[:, None, :], (L, 128, 1024)))
    sh["mon_t"] = np.ascontiguousarray(np.asarray(inp["mla_out_norm"], f).reshape(L, 8, 64).transpose(0, 2, 1))
    sh["son_t"] = np.ascontiguousarray(np.asarray(inp["swa_out_norm"], f).reshape(L, 8, 64).transpose(0, 2, 1))
    sh["sink_t"] = np.ascontiguousarray(np.broadcast_to(np.asarray(inp["swa_sink"], f)[:, None, :], (L, 64, 8)))
    sh["ropeM_t"] = _rope_tables(c.SEQ, 32)
    rs = _rope_tables(c.SEQ, 64)
    sh["ropeS_t"] = np.ascontiguousarray(np.concatenate([rs, rs], axis=1))
    ii = np.arange(128)
    J, Lm = ii[:, None], ii[None, :]
    cst = np.stack([J == Lm, J <= Lm, J >= Lm, J > Lm, J < Lm, np.ones((128, 128), bool)], axis=1)
    sh["cst_t"] = np.ascontiguousarray(cst.astype(f))
    return sh


def _prep_core(inp, c, b):
    f = np.float32
    x = np.asarray(inp["x"], f)[b]
    ctx = np.asarray(inp["ctx"], f)[b]
    xT = np.ascontiguousarray(np.concatenate([ctx, x], axis=0).T)
    cc = np.stack([np.asarray(inp["c"], f)[b], np.asarray(inp["c_ctx"], f)], axis=1)
    cT = np.ascontiguousarray(cc.reshape(c.DC, 128, 2).transpose(1, 0, 2))
    return {"xT": xT, "cT": cT}


_CACHE = {}


def kernel(**inputs):
    x = np.asarray(inputs["x"])
    B, SEQ, D = x.shape
    CTX = np.asarray(inputs["ctx"]).shape[1]
    FF = np.asarray(inputs["ffn_w_gate"]).shape[3]
    L = np.asarray(inputs["w_mod"]).shape[0]
    c = make_cfg(D, FF, SEQ, CTX, L)
    key = (D, FF, SEQ, CTX, L)
    if key not in _CACHE:
        _CACHE[key] = build_program(c)
    nc, _es = _CACHE[key]
    sh = _prep_shared(inputs, c)
    in_maps = []
    for b in range(B):
        m = dict(sh)
        m.update(_prep_core(inputs, c, b))
        in_maps.append(m)
    res = run_bass_kernel_spmd(nc, in_maps, core_ids=list(range(B)))
    out = np.stack([np.asarray(res.results[b]["outT"]).T for b in range(B)], axis=0)
    return np.ascontiguousarray(out.astype(np.float32))
```
